# Optimizing a Trainium2 kernel written in Bass

```python
import math
import jax, jax.numpy as jnp
from jax import lax
import numpy as np

D_MODEL = 1024
BATCH = 4
SEQ = 8192
DEPTH = 2

D_MIX = D_MODEL
MLA_HEADS = 4
MLA_NOPE = 128
MLA_ROPE = 64
MLA_QK = MLA_NOPE + MLA_ROPE
MLA_V = 128
MLA_Q_RANK = 384
MLA_KV_RANK = 256
MLA_OUT = MLA_HEADS * MLA_V
ROPE_THETA = 10000.0
Q_BLOCK = 128
HG_HEADS = 4
HG_DK = 64
HG_DV = 64
HG_KEY = HG_HEADS * HG_DK
HG_OUT = HG_HEADS * HG_DV
HG_CHUNK = 64
S5_OUT = D_MIX - MLA_OUT - HG_OUT
S5_GROUP = 16
S5_GROUPS = S5_OUT // S5_GROUP
S5_STATE = 64
DT_MIN = 0.001
DT_MAX = 0.1
D_FF = 4 * D_MODEL
IN_WIDTHS = (MLA_Q_RANK, MLA_KV_RANK, MLA_ROPE, HG_KEY, HG_KEY, HG_OUT, HG_OUT, S5_OUT)
D_IN = sum(IN_WIDTHS)
ALPHA = (2 * DEPTH) ** 0.25
BETA = (8 * DEPTH) ** -0.25
LN_EPS = 1e-5
RMS_EPS = 1e-6

kernel_name = "hybrid_mla_hgrn2_s5_deepnorm"


def _layernorm(x, g, b):
    xf = x.astype(jnp.float32)
    mu = jnp.mean(xf, -1, keepdims=True)
    var = jnp.mean(jnp.square(xf - mu), -1, keepdims=True)
    return ((xf - mu) * lax.rsqrt(var + LN_EPS) * g + b).astype(x.dtype)


def _rmsnorm(x, g):
    xf = x.astype(jnp.float32)
    return (xf * lax.rsqrt(jnp.mean(xf * xf, -1, keepdims=True) + RMS_EPS) * g).astype(x.dtype)


def _rope(x, pos):
    r = x.shape[-1]
    freqs = ROPE_THETA ** (-jnp.arange(0, r, 2, dtype=jnp.float32) / r)
    ang = pos[:, None] * freqs[None, :]
    cos = jnp.cos(ang)[:, None, :]
    sin = jnp.sin(ang)[:, None, :]
    xf = x.astype(jnp.float32)
    x1, x2 = xf[..., : r // 2], xf[..., r // 2:]
    return jnp.concatenate([x1 * cos - x2 * sin, x2 * cos + x1 * sin], -1).astype(x.dtype)


def _mla(c_q, c_kv, k_rope, q_norm_g, w_uq, kv_norm_g, w_ukv):
    B, S, _ = c_q.shape
    pos = jnp.arange(S, dtype=jnp.float32)
    q = (_rmsnorm(c_q, q_norm_g) @ w_uq).reshape(B, S, MLA_HEADS, MLA_QK)
    q = jnp.concatenate([q[..., :MLA_NOPE], _rope(q[..., MLA_NOPE:], pos)], -1)
    kv = (_rmsnorm(c_kv, kv_norm_g) @ w_ukv).reshape(B, S, MLA_HEADS, MLA_NOPE + MLA_V)
    k_pe = jnp.broadcast_to(_rope(k_rope[:, :, None, :], pos), (B, S, MLA_HEADS, MLA_ROPE))
    k = jnp.concatenate([kv[..., :MLA_NOPE], k_pe], -1)
    v = kv[..., MLA_NOPE:]
    scale = MLA_QK ** -0.5
    n_blk = S // Q_BLOCK
    q_blocks = q.reshape(B, n_blk, Q_BLOCK, MLA_HEADS, MLA_QK).transpose(1, 0, 2, 3, 4)
    k_pos = jnp.arange(S)

    def attend(args):
        q_blk, start = args
        s = jnp.einsum('bqhd,bkhd->bhqk', q_blk, k).astype(jnp.float32) * scale
        q_pos = start + jnp.arange(Q_BLOCK)
        s = jnp.where(k_pos[None, :] <= q_pos[:, None], s, -jnp.inf)
        p = jax.nn.softmax(s, axis=-1).astype(v.dtype)
        return jnp.einsum('bhqk,bkhd->bqhd', p, v)

    o = lax.map(attend, (q_blocks, jnp.arange(n_blk, dtype=jnp.int32) * Q_BLOCK))
    return o.transpose(1, 0, 2, 3, 4).reshape(B, S, MLA_OUT)


def _hgrn2(q, f_pre, i, g, lb, norm_g):
    B, S, _ = q.shape
    f32 = jnp.float32
    n_chunk = S // HG_CHUNK
    zf = f_pre.astype(f32)
    log_f = jnp.logaddexp(jnp.log(lb), jnp.log1p(-lb) + jax.nn.log_sigmoid(zf))
    k = (1.0 - lb) * jax.nn.sigmoid(-zf)

    def chunks(t, d):
        return t.astype(f32).reshape(B, n_chunk, HG_CHUNK, HG_HEADS, d).transpose(1, 0, 3, 2, 4)

    qc, kc, vc, lfc = chunks(q, HG_DK), chunks(k, HG_DK), chunks(i, HG_DV), chunks(log_f, HG_DK)
    causal = jnp.tril(jnp.ones((HG_CHUNK, HG_CHUNK), dtype=bool))[:, :, None]

    def step(state, xs):
        qb, kb, vb, lfb = xs
        b = jnp.cumsum(lfb, axis=2)
        diff = b[:, :, :, None, :] - b[:, :, None, :, :]
        decay = jnp.exp(jnp.where(causal, diff, -jnp.inf))
        att = jnp.einsum('bhtd,bhsd,bhtsd->bhts', qb, kb, decay)
        o = (jnp.einsum('bhts,bhsv->bhtv', att, vb)
             + jnp.einsum('bhtd,bhdv->bhtv', qb * jnp.exp(b), state))
        b_last = b[:, :, -1:, :]
        state = (jnp.exp(b_last[:, :, 0, :])[..., None] * state
                 + jnp.einsum('bhsd,bhsv->bhdv', kb * jnp.exp(b_last - b), vb))
        return state, o

    s0 = jnp.zeros((B, HG_HEADS, HG_DK, HG_DV), f32)
    _, o = lax.scan(step, s0, (qc, kc, vc, lfc))
    o = o.transpose(1, 0, 3, 2, 4).reshape(B, S, HG_HEADS, HG_DV)
    o = _rmsnorm(o, norm_g) * jax.nn.silu(g.astype(f32).reshape(B, S, HG_HEADS, HG_DV))
    return o.reshape(B, S, HG_OUT).astype(q.dtype)


def _s5(u, a_re, a_im, b_re, b_im, c_re, c_im, d_skip, log_dt, w_glu, b_glu):
    B, S, _ = u.shape
    f32 = jnp.float32
    uf = u.astype(f32).reshape(B, S, S5_GROUPS, S5_GROUP)
    dt = jnp.exp(log_dt.astype(f32))[:, None]
    ar, ai = a_re.astype(f32), a_im.astype(f32)
    mag = jnp.exp(dt * ar)
    abar_re, abar_im = mag * jnp.cos(dt * ai), mag * jnp.sin(dt * ai)
    den = ar * ar + ai * ai
    num_re, num_im = abar_re - 1.0, abar_im
    coef_re = ((num_re * ar + num_im * ai) / den)[..., None]
    coef_im = ((num_im * ar - num_re * ai) / den)[..., None]
    br, bi = b_re.astype(f32), b_im.astype(f32)
    bbar_re = coef_re * br - coef_im * bi
    bbar_im = coef_re * bi + coef_im * br
    x_re = jnp.einsum('gnp,bsgp->bsgn', bbar_re, uf)
    x_im = jnp.einsum('gnp,bsgp->bsgn', bbar_im, uf)
    a_re_t = jnp.broadcast_to(abar_re, x_re.shape)
    a_im_t = jnp.broadcast_to(abar_im, x_re.shape)

    def combine(e1, e2):
        a1r, a1i, b1r, b1i = e1
        a2r, a2i, b2r, b2i = e2
        return (a2r * a1r - a2i * a1i,
                a2r * a1i + a2i * a1r,
                a2r * b1r - a2i * b1i + b2r,
                a2r * b1i + a2i * b1r + b2i)

    _, _, h_re, h_im = lax.associative_scan(combine, (a_re_t, a_im_t, x_re, x_im), axis=1)
    y = (jnp.einsum('gpn,bsgn->bsgp', c_re.astype(f32), h_re)
         - jnp.einsum('gpn,bsgn->bsgp', c_im.astype(f32), h_im)
         + d_skip.astype(f32).reshape(S5_GROUPS, S5_GROUP) * uf)
    y = jax.nn.gelu(y.reshape(B, S, S5_OUT))
    y = y * jax.nn.sigmoid(y @ w_glu.astype(f32) + b_glu.astype(f32))
    return y.astype(u.dtype)


def setup_inputs(seed: int = 0) -> dict:
    key = jax.random.key(seed)
    ks = list(jax.random.split(key, 32))
    L = DEPTH
    G, N, P = S5_GROUPS, S5_STATE, S5_GROUP

    def nrm(j, shape, scale):
        return jax.random.normal(ks[j], shape, jnp.float32) * scale

    n_idx = jnp.arange(N, dtype=jnp.float32)
    return {
        "x": nrm(0, (BATCH, SEQ, D_MODEL), 1.0),
        "ln_in_g": 1.0 + nrm(1, (D_MODEL,), 0.02),
        "ln_in_b": nrm(2, (D_MODEL,), 0.02),
        "w_in": nrm(3, (L, D_MODEL, D_IN), D_MODEL ** -0.5),
        "mla_q_norm_g": 1.0 + nrm(4, (L, MLA_Q_RANK), 0.02),
        "mla_w_uq": nrm(5, (L, MLA_Q_RANK, MLA_HEADS * MLA_QK), MLA_Q_RANK ** -0.5),
        "mla_kv_norm_g": 1.0 + nrm(6, (L, MLA_KV_RANK), 0.02),
        "mla_w_ukv": nrm(7, (L, MLA_KV_RANK, MLA_HEADS * (MLA_NOPE + MLA_V)), MLA_KV_RANK ** -0.5),
        "hg_lower_bound": 1.0 + nrm(8, (L, HG_KEY), 0.1),
        "hg_norm_g": 1.0 + nrm(9, (L, HG_DV), 0.02),
        "s5_a_re": -0.5 + nrm(10, (L, G, N), 0.01),
        "s5_a_im": math.pi * n_idx + nrm(11, (L, G, N), 0.01),
        "s5_b_re": nrm(12, (L, G, N, P), (2.0 * P) ** -0.5),
        "s5_b_im": nrm(13, (L, G, N, P), (2.0 * P) ** -0.5),
        "s5_c_re": nrm(14, (L, G, P, N), (2.0 * N) ** -0.5 * 4.0),
        "s5_c_im": nrm(15, (L, G, P, N), (2.0 * N) ** -0.5 * 4.0),
        "s5_d": nrm(16, (L, S5_OUT), 1.0),
        "s5_log_dt": jax.random.uniform(ks[17], (L, G), jnp.float32, math.log(DT_MIN), math.log(DT_MAX)),
        "s5_w_glu": nrm(18, (L, S5_OUT, S5_OUT), S5_OUT ** -0.5),
        "s5_b_glu": nrm(19, (L, S5_OUT), 0.02),
        "w_out": nrm(20, (L, D_MIX, D_MODEL), D_MIX ** -0.5 * BETA),
        "ln1_g": 1.0 + nrm(21, (L, D_MODEL), 0.02),
        "ln1_b": nrm(22, (L, D_MODEL), 0.02),
        "w_ff1": nrm(23, (L, D_MODEL, D_FF), D_MODEL ** -0.5),
        "w_ff2": nrm(24, (L, D_FF, D_MODEL), D_FF ** -0.5 * BETA),
        "ln2_g": 1.0 + nrm(25, (L, D_MODEL), 0.02),
        "ln2_b": nrm(26, (L, D_MODEL), 0.02),
    }


def reference(x, ln_in_g, ln_in_b, w_in, mla_q_norm_g, mla_w_uq, mla_kv_norm_g, mla_w_ukv,
              hg_lower_bound, hg_norm_g, s5_a_re, s5_a_im, s5_b_re, s5_b_im, s5_c_re, s5_c_im,
              s5_d, s5_log_dt, s5_w_glu, s5_b_glu, w_out, ln1_g, ln1_b, w_ff1, w_ff2,
              ln2_g, ln2_b):
    h = _layernorm(x, ln_in_g, ln_in_b)
    lb_all = jnp.cumsum(jax.nn.softmax(hg_lower_bound.astype(jnp.float32), axis=0), axis=0)
    lb_all = lb_all - lb_all[0]
    offsets = [sum(IN_WIDTHS[:j]) for j in range(1, len(IN_WIDTHS))]
    for l in range(DEPTH):
        proj = h @ w_in[l]
        c_q, c_kv, k_rope, hq, hf, hi, hg, su = jnp.split(proj, offsets, axis=-1)
        o_mla = _mla(c_q, c_kv, k_rope, mla_q_norm_g[l], mla_w_uq[l], mla_kv_norm_g[l], mla_w_ukv[l])
        o_hg = _hgrn2(hq, hf, hi, hg, lb_all[l], hg_norm_g[l])
        o_s5 = _s5(su, s5_a_re[l], s5_a_im[l], s5_b_re[l], s5_b_im[l], s5_c_re[l], s5_c_im[l],
                   s5_d[l], s5_log_dt[l], s5_w_glu[l], s5_b_glu[l])
        mix = jnp.concatenate([o_mla, o_hg, o_s5], axis=-1) @ w_out[l]
        h = _layernorm(ALPHA * h + mix, ln1_g[l], ln1_b[l])
        ff = jnp.square(jax.nn.relu(h @ w_ff1[l])) @ w_ff2[l]
        h = _layernorm(ALPHA * h + ff, ln2_g[l], ln2_b[l])
    return h
```

```python
import math
from contextlib import ExitStack

import numpy as np
import ml_dtypes
import concourse.bass as bass
import concourse.mybir as mybir
from concourse.bass_utils import run_bass_kernel_spmd

F32 = mybir.dt.float32
BF16 = mybir.dt.bfloat16
I32 = mybir.dt.int32
AF = mybir.ActivationFunctionType
ALU = mybir.AluOpType

D = 1024
DEPTH = 2
SEQ = 8192
BATCH = 4
NH = 4
NOPE = 128
ROPE = 64
QK = 192
DV = 128
QR = 384
KVR = 256
HGH = 4
HDK = 64
HDV = 64
S5G = 16
S5P = 16
S5N = 64
DFF = 4096
DIN = 1984
ALPHA = (2 * DEPTH) ** 0.25
LN_EPS = 1e-5
RMS_EPS = 1e-6
ROPE_THETA = 10000.0
T = 512
OFF_CQ, OFF_CKV, OFF_KR, OFF_HQ, OFF_HF, OFF_HI, OFF_HG, OFF_SU = (
    0, 384, 640, 704, 960, 1216, 1472, 1728)


class Res:
    __slots__ = ("name", "last_w", "readers", "psum")

    def __init__(self, name, psum=False):
        self.name = name
        self.last_w = None
        self.readers = []
        self.psum = psum


class Op:
    __slots__ = ("eng", "fn", "deps", "dma", "flag", "token", "prewait", "idx")


class Sync:
    SEM_LIMIT = 30000

    def __init__(self, nc, es, n_dma_sems=20):
        self.nc = nc
        self.es = es
        self.eng_sem = {}
        self.eng_cnt = {}
        n_sw = 6
        self.dma_sems = [es.enter_context(nc.semaphore(f"dma{i}")) for i in range(n_dma_sems + n_sw)]
        self.dma_cnt = [0] * (n_dma_sems + n_sw)
        self.dma_pool = {"hw": list(range(n_dma_sems)), "sw": list(range(n_dma_sems, n_dma_sems + n_sw))}
        self.dma_rr = {"hw": 0, "sw": 0}
        self.waited = {e: {} for e in ("pe", "act", "dve", "pool", "sp")}
        self.nsem = 0

    def new_eng_sem(self, e):
        self.nsem += 1
        s = self.es.enter_context(self.nc.semaphore(f"s_{e}_{self.nsem}"))
        self.eng_sem[e] = s
        self.eng_cnt[e] = 0
        return s


class Sched:
    def __init__(self, sync):
        self.sync = sync
        self.nc = sync.nc
        self.ops = []

    def add(self, eng, fn, reads=(), writes=(), dma=False):
        op = Op()
        op.eng, op.fn, op.dma = eng, fn, dma
        op.flag = False
        op.token = None
        op.prewait = None
        op.idx = len(self.ops)
        deps = []
        xw = [r for r in reads if r.psum]
        if xw:
            writes = list(writes) + [r for r in xw if r not in writes]
        for r in reads:
            w = r.last_w
            if w is not None:
                if w.dma or w.eng != eng or dma or eng != "pe":
                    deps.append(w)
            r.readers.append(op)
        for wres in writes:
            w = wres.last_w
            if w is not None and (w.dma or dma or w.eng != eng or eng != "pe"):
                deps.append(w)
            for rd in wres.readers:
                if rd is op:
                    continue
                if rd.dma or dma or rd.eng != eng or eng != "pe":
                    deps.append(rd)
            wres.last_w = op
            wres.readers = []
        for d_ in deps:
            d_.flag = True
        op.deps = deps
        self.ops.append(op)
        return op

    def pe(self, fn, reads=(), writes=()):
        return self.add("pe", fn, reads, writes)

    def act(self, fn, reads=(), writes=()):
        return self.add("act", fn, reads, writes)

    def dve(self, fn, reads=(), writes=()):
        return self.add("dve", fn, reads, writes)

    def pool(self, fn, reads=(), writes=()):
        return self.add("pool", fn, reads, writes)

    def dma(self, fn, reads=(), writes=(), q="sp"):
        return self.add(q, fn, reads, writes, dma=True)

    def emit(self, name):
        sy = self.sync
        nc = self.nc
        ops = self.ops
        last = {}
        for op in ops:
            if not op.dma:
                last[op.eng] = op
        for op in last.values():
            op.flag = True
        for op in ops:
            if op.dma:
                kind = "sw" if op.eng == "pool" else "hw"
                pool_ = sy.dma_pool[kind]
                j = pool_[sy.dma_rr[kind] % len(pool_)]
                sy.dma_rr[kind] += 1
                op.prewait = (sy.dma_sems[j], sy.dma_cnt[j])
                sy.dma_cnt[j] += 16
                op.token = (sy.dma_sems[j], sy.dma_cnt[j])
            elif op.flag:
                e = op.eng
                if e not in sy.eng_sem or sy.eng_cnt[e] >= sy.SEM_LIMIT:
                    sy.new_eng_sem(e)
                sy.eng_cnt[e] += 1
                op.token = (sy.eng_sem[e], sy.eng_cnt[e])
        end_tokens = [op.token for op in last.values()]
        end_tokens += [(s, c) for s, c in zip(sy.dma_sems, sy.dma_cnt) if c > 0]

        def stream(e):
            def body(eng):
                waited = sy.waited[e]

                def wait(tok):
                    s, v = tok
                    if v <= 0:
                        return
                    key = id(s)
                    if waited.get(key, 0) >= v:
                        return
                    eng.wait_ge(s, v)
                    waited[key] = v

                for op in ops:
                    if op.eng != e:
                        continue
                    if op.dma:
                        wait(op.prewait)
                    need = {}
                    for d_ in op.deps:
                        s, v = d_.token
                        k = id(s)
                        if k not in need or need[k][1] < v:
                            need[k] = (s, v)
                    for tok in need.values():
                        wait(tok)
                    inst = op.fn(eng)
                    if op.dma:
                        inst.then_inc(op.token[0], 16)
                    elif op.flag:
                        inst.then_inc(op.token[0], 1)
                for tok in end_tokens:
                    wait(tok)
            return body

        with nc.Block(name) as block:
            block.tensor(stream("pe"))
            block.scalar(stream("act"))
            block.vector(stream("dve"))
            block.gpsimd(stream("pool"))
            block.sync(stream("sp"))
        self.ops = []


class Buf:
    def __init__(self, t, name, nchunk=1, psum=False):
        self.t = t
        self.r = [Res(f"{name}.{i}", psum) for i in range(nchunk)]

    def all(self):
        return list(self.r)


class Ctx:
    N = [0]

    def __init__(self, nc, es):
        self.nc = nc
        self.es = es

    def sb(self, name, shape, dtype, nchunk=1):
        Ctx.N[0] += 1
        t = self.es.enter_context(self.nc.sbuf_tensor(f"{name}_{Ctx.N[0]}", list(shape), dtype))
        return Buf(t, name, nchunk)

    def ps(self, name, shape=(128, 512), dtype=F32, nchunk=1):
        Ctx.N[0] += 1
        t = self.es.enter_context(self.nc.psum_tensor(f"{name}_{Ctx.N[0]}", list(shape), dtype))
        return Buf(t, name, nchunk, psum=True)


def mm(s, out_ap, lhsT, rhs, start, stop, reads, writes):
    def fn(eng):
        return eng.matmul(out_ap, lhsT, rhs, start=start, stop=stop)
    return s.pe(fn, reads, writes)


def act(s, out_ap, in_ap, func, reads, writes, bias=None, scale=None):
    def fn(eng):
        kw = {}
        if bias is not None:
            kw["bias"] = bias
        if scale is not None:
            kw["scale"] = scale
        return eng.activation(out_ap, in_ap, func, **kw)
    return s.act(fn, reads, writes)


def tt(s, e, out_ap, in0, in1, op, reads, writes):
    def fn(eng):
        return eng.tensor_tensor(out_ap, in0, in1, op)
    return s.add(e, fn, reads, writes)


def ts(s, e, out_ap, in0, s1, s2, op0, op1, reads, writes):
    def fn(eng):
        if op1 is None:
            return eng.tensor_scalar(out_ap, in0, s1, None, op0)
        return eng.tensor_scalar(out_ap, in0, s1, s2, op0, op1)
    return s.add(e, fn, reads, writes)


def stt(s, out_ap, in0, scalar, in1, op0, op1, reads, writes):
    def fn(eng):
        return eng.scalar_tensor_tensor(out_ap, in0, scalar, in1, op0, op1)
    return s.dve(fn, reads, writes)


def cp(s, e, out_ap, in_ap, reads, writes):
    if e == "act":
        def fn(eng):
            return eng.copy(out_ap, in_ap)
    else:
        def fn(eng):
            return eng.tensor_copy(out_ap, in_ap)
    return s.add(e, fn, reads, writes)


def dma(s, out_ap, in_ap, reads, writes, q="sp"):
    def fn(eng):
        return eng.dma_start(out_ap, in_ap)
    return s.dma(fn, reads, writes, q=q)


def memset(s, e, ap, val, writes):
    def fn(eng):
        return eng.memset(ap, val)
    return s.add(e, fn, (), writes)


def load_w_bf16(s, dst, src_view, nk, ncols, res=None):
    res = dst.all() if res is None else res
    for k in range(nk):
        for c0 in range(0, ncols, 2048):
            c1 = min(ncols, c0 + 2048)
            dma(s, dst.t[:, k, c0:c1], src_view[:, k, c0:c1], (), res, q="pool")


class LNState:
    pass


class LN:
    def __init__(self, s, z, gbuf, bbuf, out32, outb, ps1, ps2, tmp, slot=0):
        self.s, self.z, self.g, self.b = s, z, gbuf, bbuf
        self.out32, self.outb, self.ps1, self.ps2, self.tmp, self.slot = out32, outb, ps1, ps2, tmp, slot

    def stats(self, k):
        s, z, tmp, ps1, ps2 = self.s, self.z, self.tmp, self.ps1, self.ps2
        KC = D // 128
        onesb = tmp["onesb"]
        zb = tmp["zb"][k % 2]
        sq = tmp["sq"][k % 2]
        cp(s, "act", zb.t[:, :], z.t[:, k, :], [z.r[k]], zb.all())
        act(s, sq.t[:, :], z.t[:, k, :], AF.Square, [z.r[k]], sq.all())
        mm(s, ps1.t[:, :], onesb.t[:, :], zb.t[:, :], k == 0, k == KC - 1, zb.all() + onesb.all(), ps1.all())
        mm(s, ps2.t[:, :], onesb.t[:, :], sq.t[:, :], k == 0, k == KC - 1, sq.all() + onesb.all(), ps2.all())

    def rstd(self):
        s, tmp, ps1, ps2 = self.s, self.tmp, self.ps1, self.ps2
        mean, rstd, nm = tmp["mean"][self.slot], tmp["rstd"][self.slot], tmp["nm"][self.slot]
        act(s, mean.t[:, :], ps1.t[:, :], AF.Copy, ps1.all(), mean.all(), scale=1.0 / D)
        tt(s, "dve", nm.t[:, :], mean.t[:, :], mean.t[:, :], ALU.mult, mean.all(), nm.all())
        stt(s, rstd.t[:, :], ps2.t[:, :], 1.0 / D, nm.t[:, :], ALU.mult, ALU.subtract,
            ps2.all() + nm.all(), rstd.all())
        act(s, rstd.t[:, :], rstd.t[:, :], AF.Ln, rstd.all() + tmp["eps"].all(), rstd.all(),
            bias=tmp["eps"].t[:, 0:1])
        act(s, rstd.t[:, :], rstd.t[:, :], AF.Exp, rstd.all(), rstd.all(), scale=-0.5)
        stt(s, nm.t[:, :], mean.t[:, :], -1.0, rstd.t[:, :], ALU.mult, ALU.mult,
            mean.all() + rstd.all(), nm.all())

    def apply(self, k):
        s, z, tmp = self.s, self.z, self.tmp
        rstd, nm = tmp["rstd"][self.slot], tmp["nm"][self.slot]
        g_ap, b_ap = self.g.t, self.b.t
        gb = self.g.all() + self.b.all()
        t_ = tmp["t"][k % 2]
        tt(s, "dve", t_.t[:, :], z.t[:, k, :], rstd.t[:, :], ALU.mult, [z.r[k]] + rstd.all(), t_.all())
        tt(s, "pool", t_.t[:, :], t_.t[:, :], nm.t[:, :], ALU.add, t_.all() + nm.all(), t_.all())
        act(s, self.out32.t[:, k, :], t_.t[:, :], AF.Identity, t_.all() + gb, [self.out32.r[k]],
            bias=b_ap[:, k:k + 1], scale=g_ap[:, k:k + 1])
        ts(s, "dve", self.outb.t[:, k, :], t_.t[:, :], g_ap[:, k:k + 1], b_ap[:, k:k + 1], ALU.mult, ALU.add,
           t_.all() + gb, [self.outb.r[k]])


def emit_layernorm(s, c, z, gbuf, bbuf, out32, outb, onesb, ps1, ps2, tmp, slot=0):
    ln = LN(s, z, gbuf, bbuf, out32, outb, ps1, ps2, tmp, slot)
    for k in range(D // 128):
        ln.stats(k)
    ln.rstd()
    for k in range(D // 128):
        ln.apply(k)


def ln_tmp(c, s, nslot=2):
    tmp = {
        "sq": [c.sb("lnsq", (128, T), BF16) for _ in range(2)],
        "zb": [c.sb("lnzb", (128, T), BF16) for _ in range(2)],
        "t": [c.sb("lnt", (128, T), F32) for _ in range(2)],
        "mean": [c.sb("lnmean", (128, T), F32) for _ in range(nslot)],
        "rstd": [c.sb("lnrstd", (128, T), F32) for _ in range(nslot)],
        "nm": [c.sb("lnnm", (128, T), F32) for _ in range(nslot)],
        "eps": c.sb("lneps", (128, 1), F32),
        "onesb": c.sb("lnones", (128, 128), BF16),
    }
    memset(s, "pool", tmp["onesb"].t[:, :], 1.0, tmp["onesb"].all())
    memset(s, "pool", tmp["eps"].t[:, :], LN_EPS, tmp["eps"].all())
    return tmp


def fm(ap):
    return ap.rearrange("(k p) s -> p k s", p=128)


def stage_ln_in(nc, sync, dr, S):
    with ExitStack() as es:
        c = Ctx(nc, es)
        s = Sched(sync)
        NTILE = S // T
        zb = [c.sb("z", (128, 8, T), F32, 8) for _ in range(3)]
        ob = [c.sb("ob", (128, 8, T), BF16, 8) for _ in range(2)]
        g = c.sb("g", (128, 8), F32)
        b = c.sb("b", (128, 8), F32)
        ps1, ps2 = [c.ps("ps1") for _ in range(2)], [c.ps("ps2") for _ in range(2)]
        tmp = ln_tmp(c, s)
        dma(s, g.t[:, :], dr["ln_in_g"][:, :], (), g.all())
        dma(s, b.t[:, :], dr["ln_in_b"][:, :], (), b.all())
        xv, hv, hbv = fm(dr["xT"]), fm(dr["H"]), fm(dr["Hb"])

        def load(j):
            dma(s, zb[j % 3].t[:, :, :], xv[:, :, j * T:(j + 1) * T], (), zb[j % 3].all())

        def store(j):
            sl_ = slice(j * T, (j + 1) * T)
            dma(s, hv[:, :, sl_], zb[j % 3].t[:, :, :], zb[j % 3].all(), ())
            dma(s, hbv[:, :, sl_], ob[j % 2].t[:, :, :], ob[j % 2].all(), ())
        load(0)
        prev = None
        for j in range(NTILE):
            z, o = zb[j % 3], ob[j % 2]
            if j + 1 < NTILE:
                load(j + 1)
            ln = LN(s, z, g, b, z, o, ps1[j % 2], ps2[j % 2], tmp, j % 2)
            for k in range(8):
                ln.stats(k)
                if prev is not None:
                    prev.apply(k)
            if prev is not None:
                store(j - 1)
            ln.rstd()
            prev = ln
        for k in range(8):
            prev.apply(k)
        store(NTILE - 1)
        s.emit("ln_in")


def wv(ap):
    return ap.rearrange("(k p) n -> p k n", p=128)


def stage_c1(nc, sync, dr, S, l, w1_pref=None):
    with ExitStack() as es:
        c = Ctx(nc, es)
        s = Sched(sync)
        NTILE = S // T
        w = c.sb("wout", (128, 8, D), BF16)
        load_w_bf16(s, w, wv(dr["w_out"][l]), 8, D)
        g = c.sb("g", (128, 8), F32)
        b = c.sb("b", (128, 8), F32)
        dma(s, g.t[:, :], dr["ln1_g"][l], (), g.all())
        dma(s, b.t[:, :], dr["ln1_b"][l], (), b.all())
        tmp = ln_tmp(c, s)
        hb_ = [c.sb("h", (128, 8, T), F32, 8) for _ in range(3)]
        mb = [c.sb("mix", (128, 8, T), BF16, 8) for _ in range(3)]
        ob = [c.sb("ob", (128, 8, T), BF16, 8) for _ in range(2)]
        pm = [c.ps("pm") for _ in range(4)]
        ps1, ps2 = [c.ps("ps1") for _ in range(2)], [c.ps("ps2") for _ in range(2)]
        hv, mv, h1v, h1bv = fm(dr["H"]), fm(dr["MIX"]), fm(dr["H1"]), fm(dr["H1b"])

        def load(j):
            sl_ = slice(j * T, (j + 1) * T)
            dma(s, mb[j % 3].t[:, :, :], mv[:, :, sl_], (), mb[j % 3].all())
            dma(s, hb_[j % 3].t[:, :, :], hv[:, :, sl_], (), hb_[j % 3].all())

        def store(j):
            sl_ = slice(j * T, (j + 1) * T)
            dma(s, h1v[:, :, sl_], hb_[j % 3].t[:, :, :], hb_[j % 3].all(), ())
            dma(s, h1bv[:, :, sl_], ob[j % 2].t[:, :, :], ob[j % 2].all(), ())
        load(0)
        prev = None
        for j in range(NTILE):
            h, mx, o = hb_[j % 3], mb[j % 3], ob[j % 2]
            if j + 1 < NTILE:
                load(j + 1)
            if j == 1 and w1_pref is not None:
                load_w_bf16(s, w1_pref, wv(dr["w_ff1"][l]), 8, DFF)
            ln = LN(s, h, g, b, h, o, ps1[j % 2], ps2[j % 2], tmp, j % 2)
            for m in range(8):
                p = pm[m % 4]
                for k in range(8):
                    mm(s, p.t[:, :], w.t[:, k, m * 128:(m + 1) * 128], mx.t[:, k, :],
                       k == 0, k == 7, [w.r[0], mx.r[k]], p.all())
                stt(s, h.t[:, m, :], h.t[:, m, :], ALPHA, p.t[:, :], ALU.mult, ALU.add,
                    [h.r[m]] + p.all(), [h.r[m]])
                if m >= 2:
                    ln.stats(m - 2)
                if prev is not None:
                    prev.apply(m)
            if prev is not None:
                store(j - 1)
            ln.stats(6)
            ln.stats(7)
            ln.rstd()
            prev = ln
        for m in range(8):
            prev.apply(m)
        store(NTILE - 1)
        s.emit(f"c1_{l}")


def stage_c2(nc, sync, dr, S, l, out32, outb, w1_pref=None):
    with ExitStack() as es:
        c = Ctx(nc, es)
        s = Sched(sync)
        NTILE = S // T
        if w1_pref is not None:
            w1 = w1_pref
        else:
            w1 = c.sb("w1", (128, 8, DFF), BF16)
            load_w_bf16(s, w1, wv(dr["w_ff1"][l]), 8, DFF)
        w2 = c.sb("w2", (128, 32, D), BF16)
        load_w_bf16(s, w2, wv(dr["w_ff2"][l]), 32, D)
        g = c.sb("g", (128, 8), F32)
        b = c.sb("b", (128, 8), F32)
        dma(s, g.t[:, :], dr["ln2_g"][l], (), g.all())
        dma(s, b.t[:, :], dr["ln2_b"][l], (), b.all())
        tmp = ln_tmp(c, s, nslot=1)
        h = c.sb("h", (128, 8, T), F32, 8)
        hbb = [c.sb("hb", (128, 8, T), BF16, 8) for _ in range(2)]
        a = c.sb("a", (128, 32, T), BF16, 32)
        sq = tmp["zb"]
        pf = [c.ps("pf") for _ in range(4)]
        pz = [c.ps("pz") for _ in range(2)]
        ps1, ps2 = c.ps("ps1"), c.ps("ps2")
        h1v, h1bv = fm(dr["H1"]), fm(dr["H1b"])
        o32v = fm(out32)
        obv = fm(outb) if outb is not None else None

        def load_hb(j):
            dma(s, hbb[j % 2].t[:, :, :], h1bv[:, :, j * T:(j + 1) * T], (), hbb[j % 2].all())

        def load_h(j):
            dma(s, h.t[:, :, :], h1v[:, :, j * T:(j + 1) * T], (), h.all())

        def store(j):
            sl_ = slice(j * T, (j + 1) * T)
            dma(s, o32v[:, :, sl_], h.t[:, :, :], h.all(), ())
            if obv is not None:
                dma(s, obv[:, :, sl_], hbb[j % 2].t[:, :, :], hbb[j % 2].all(), ())
        load_hb(0)
        load_h(0)
        if NTILE > 1:
            load_hb(1)
        prev = None
        for j in range(NTILE):
            hbt = hbb[j % 2]
            for m in range(32):
                p = pf[m % 4]
                for k in range(8):
                    mm(s, p.t[:, :], w1.t[:, k, m * 128:(m + 1) * 128], hbt.t[:, k, :],
                       k == 0, k == 7, [w1.r[0], hbt.r[k]], p.all())
                q = sq[m % 2]
                act(s, q.t[:, :], p.t[:, :], AF.Square, p.all(), q.all())
                stt(s, a.t[:, m, :], p.t[:, :], 0.0, q.t[:, :], ALU.is_gt, ALU.mult,
                    p.all() + q.all(), [a.r[m]])
                if prev is not None and m % 4 == 3:
                    prev.apply(m // 4)
            if prev is not None:
                store(j - 1)
                load_h(j)
                if j + 1 < NTILE:
                    load_hb(j + 1)
            ln = LN(s, h, g, b, h, hbt, ps1, ps2, tmp, 0)
            for m in range(8):
                p = pz[m % 2]
                for k in range(32):
                    mm(s, p.t[:, :], w2.t[:, k, m * 128:(m + 1) * 128], a.t[:, k, :],
                       k == 0, k == 31, [w2.r[0], a.r[k]], p.all())
                stt(s, h.t[:, m, :], h.t[:, m, :], ALPHA, p.t[:, :], ALU.mult, ALU.add,
                    [h.r[m]] + p.all(), [h.r[m]])
                if m >= 1:
                    ln.stats(m - 1)
            ln.stats(7)
            ln.rstd()
            prev = ln
        for m in range(8):
            prev.apply(m)
        store(NTILE - 1)
        s.emit(f"c2_{l}")


LAYER_W = {
    "w_in": (D, DIN), "mla_w_uq": (QR, NH * QK), "mla_w_ukv": (KVR, NH * (NOPE + DV)),
    "w_out": (D, D), "w_ff1": (D, DFF), "w_ff2": (DFF, D), "s5_w_glu": (256, 256),
}
LAYER_V128 = {"ln1_g": 8, "ln1_b": 8, "ln2_g": 8, "ln2_b": 8,
              "mla_q_norm_g": 3, "mla_kv_norm_g": 2}


class Rot:
    def __init__(self, bufs):
        self.bufs = bufs
        self.i = 0

    def get(self):
        b = self.bufs[self.i % len(self.bufs)]
        self.i += 1
        return b


def stage_a(nc, sync, dr, S, l, fuse_ln_in=False):
    with ExitStack() as es:
        c = Ctx(nc, es)
        s = Sched(sync)
        NTILE = S // T
        win = c.sb("win", (128, 8, DIN), BF16)
        load_w_bf16(s, win, wv(dr["w_in"][l]), 8, DIN)
        wrotk = c.sb("wrotk", (128, 8, 64), BF16)
        for k in range(8):
            ts(s, "dve", wrotk.t[:, k, 0:32], win.t[:, k, OFF_KR + 32:OFF_KR + 64], -1.0, None,
               ALU.mult, None, win.all(), wrotk.all())
            cp(s, "dve", wrotk.t[:, k, 32:64], win.t[:, k, OFF_KR:OFF_KR + 32], win.all(), wrotk.all())
        gq = c.sb("gq", (128, 3), F32)
        gkv = c.sb("gkv", (128, 2), F32)
        dma(s, gq.t[:, :], dr["mla_q_norm_g"][l], (), gq.all())
        dma(s, gkv.t[:, :], dr["mla_kv_norm_g"][l], (), gkv.all())
        stq = c.sb("stq", (128, 3, NH * QK), F32)
        dma(s, stq.t[:, :, :], wv(dr["mla_w_uq"][l]), (), stq.all())
        wuq = c.sb("wuq", (128, 3, NH * QK), BF16)
        for k in range(3):
            ts(s, "dve", wuq.t[:, k, :], stq.t[:, k, :], gq.t[:, k:k + 1], None, ALU.mult, None,
               stq.all() + gq.all(), wuq.all())
        wqr = c.sb("wqr", (128, 3, 256), BF16)
        wqx = c.sb("wqx", (128, 3, 256), BF16)
        for k in range(3):
            for h in range(NH):
                b0 = h * QK + NOPE
                cp(s, "pool", wqr.t[:, k, h * 64:(h + 1) * 64], wuq.t[:, k, b0:b0 + 64],
                   wuq.all(), wqr.all())
                ts(s, "dve", wqx.t[:, k, h * 64:h * 64 + 32], wuq.t[:, k, b0 + 32:b0 + 64], -1.0, None,
                   ALU.mult, None, wuq.all(), wqx.all())
                cp(s, "dve", wqx.t[:, k, h * 64 + 32:h * 64 + 64], wuq.t[:, k, b0:b0 + 32],
                   wuq.all(), wqx.all())
        stkv = c.sb("stkv", (128, 2, NH * 256), F32)
        dma(s, stkv.t[:, :, :], wv(dr["mla_w_ukv"][l]), (), stkv.all())
        wukv = c.sb("wukv", (128, 2, NH * 256), BF16)
        for k in range(2):
            ts(s, "dve", wukv.t[:, k, :], stkv.t[:, k, :], gkv.t[:, k:k + 1], None, ALU.mult, None,
               stkv.all() + gkv.all(), wukv.all())
        wvv = c.sb("wvv", (128, 2, NH * DV), BF16)
        for k in range(2):
            for h in range(NH):
                cp(s, "pool", wvv.t[:, k, h * DV:(h + 1) * DV],
                   wukv.t[:, k, h * 256 + NOPE:h * 256 + 256], wukv.all(), wvv.all())
        ones32 = c.sb("ones32", (128, 128), F32)
        memset(s, "pool", ones32.t[:, :], 1.0, ones32.all())
        epsr = c.sb("epsr", (128, 1), F32)
        memset(s, "pool", epsr.t[:, :], RMS_EPS, epsr.all())
        lnsc = c.sb("lnsc", (128, 1), F32)
        memset(s, "pool", lnsc.t[:, :], math.log(QK ** -0.5), lnsc.all())
        hbb = [c.sb("hb", (128, 8, T), BF16, 8) for _ in range(2)]
        cq = c.sb("cq", (128, 3, T), BF16, 3)
        ckv = c.sb("ckv", (128, 2, T), BF16, 2)
        sqb = [c.sb("sq", (128, T), F32) for _ in range(3)]
        rq = c.sb("rq", (128, T), F32)
        rkv = c.sb("rkv", (128, T), F32)
        rcol = c.sb("rcol", (128, 8), F32)
        cosb = [c.sb("cos", (128, T), F32) for _ in range(2)]
        sinb = [c.sb("sin", (128, T), F32) for _ in range(2)]
        cs = c.sb("cs", (128, T), F32)
        sn = c.sb("sn", (128, T), F32)
        t1 = Rot([c.sb("t1", (128, T), F32) for _ in range(2)])
        t2 = Rot([c.sb("t2", (128, T), F32) for _ in range(2)])
        o16 = Rot([c.sb("o16", (128, T), BF16) for _ in range(6)])
        o32 = Rot([c.sb("o32", (128, T), F32) for _ in range(6)])
        pp = Rot([c.ps("pp") for _ in range(4 if fuse_ln_in else 6)])
        pssq = c.ps("pssq")
        pcol = c.ps("pcol", (128, 8))
        hbv = fm(dr["Hb"])
        evac_i = [0]

        pending = []

        def tick():
            if pending:
                pending.pop(0)()

        def proj(p, pairs, M=128):
            n = len(pairs)
            for i, (lh, rh, rd) in enumerate(pairs):
                mm(s, p.t[0:M, :], lh, rh, i == 0, i == n - 1, rd, p.all())
            tick()

        if fuse_ln_in:
            zx = [c.sb("zx", (128, 8, T), F32, 8) for _ in range(2)]
            g_in = c.sb("g_in", (128, 8), F32)
            b_in = c.sb("b_in", (128, 8), F32)
            dma(s, g_in.t[:, :], dr["ln_in_g"][:, :], (), g_in.all())
            dma(s, b_in.t[:, :], dr["ln_in_b"][:, :], (), b_in.all())
            lntmp = ln_tmp(c, s, nslot=1)
            lps1, lps2 = c.ps("lps1"), c.ps("lps2")
            xv_, hv_ = fm(dr["xT"]), fm(dr["H"])

            def ln_pieces(j):
                z = zx[j % 2]
                ln = LN(s, z, g_in, b_in, z, hbb[j % 2], lps1, lps2, lntmp, 0)
                pcs = [(lambda k=k: ln.stats(k)) for k in range(8)] + [ln.rstd]
                pcs += [(lambda k=k: ln.apply(k)) for k in range(8)]
                pcs.append(lambda: dma(s, hv_[:, :, j * T:(j + 1) * T], z.t[:, :, :], z.all(), ()))
                return pcs

            def load_x(j):
                dma(s, zx[j % 2].t[:, :, :], xv_[:, :, j * T:(j + 1) * T], (), zx[j % 2].all())

        def load(j):
            sl_ = slice(j * T, (j + 1) * T)
            if not fuse_ln_in:
                dma(s, hbb[j % 2].t[:, :, :], hbv[:, :, sl_], (), hbb[j % 2].all())
            dma(s, cosb[j % 2].t[:, :], dr["rope_cos"][:, sl_], (), cosb[j % 2].all())
            dma(s, sinb[j % 2].t[:, :], dr["rope_sin"][:, sl_], (), sinb[j % 2].all())
        load(0)
        if fuse_ln_in:
            load_x(0)
            for f_ in ln_pieces(0):
                f_()
            if NTILE > 1:
                load_x(1)
        for j in range(NTILE):
            hb = hbb[j % 2]
            cos, sin = cosb[j % 2], sinb[j % 2]
            sl = slice(j * T, (j + 1) * T)
            if j + 1 < NTILE:
                load(j + 1)
                if fuse_ln_in:
                    pending.extend(ln_pieces(j + 1))

            def hin(c0, c1):
                return [(win.t[:, k, c0:c1], hb.t[:, k, :], [win.r[0], hb.r[k]]) for k in range(8)]

            for m in range(3):
                p = pp.get()
                proj(p, hin(OFF_CQ + m * 128, OFF_CQ + (m + 1) * 128))
                cp(s, "act", cq.t[:, m, :], p.t[:, :], p.all(), [cq.r[m]])
                q_ = sqb[m]
                act(s, q_.t[:, :], p.t[:, :], AF.Square, p.all(), q_.all())
                mm(s, pssq.t[:, :], ones32.t[:, :], q_.t[:, :], m == 0, m == 2,
                   q_.all() + ones32.all(), pssq.all())
            act(s, rq.t[:, :], pssq.t[:, :], AF.Ln, pssq.all() + epsr.all(), rq.all(), bias=epsr.t[:, 0:1],
                scale=1.0 / QR)
            act(s, rq.t[:, :], rq.t[:, :], AF.Exp, rq.all() + lnsc.all(), rq.all(), bias=lnsc.t[:, 0:1], scale=-0.5)
            tt(s, "dve", cs.t[:, :], cos.t[:, :], rq.t[:, :], ALU.mult, cos.all() + rq.all(), cs.all())
            tt(s, "pool", sn.t[:, :], sin.t[:, :], rq.t[:, :], ALU.mult, sin.all() + rq.all(), sn.all())

            def cqin(w_, c0, c1):
                return [(w_.t[:, k, c0:c1], cq.t[:, k, :], [w_.r[0], cq.r[k]]) for k in range(3)]

            for h in range(NH):
                p = pp.get()
                proj(p, cqin(wuq, h * QK, h * QK + NOPE))
                o = o16.get()
                tt(s, "dve", o.t[:, :], p.t[:, :], rq.t[:, :], ALU.mult, p.all() + rq.all(), o.all())
                dma(s, dr["QN"][h, :, sl], o.t[:, :], o.all(), ())
            for pr in range(2):
                pa, pb = pp.get(), pp.get()
                proj(pa, cqin(wqr, pr * 128, (pr + 1) * 128))
                proj(pb, cqin(wqx, pr * 128, (pr + 1) * 128))
                a_, b_ = t1.get(), t2.get()
                tt(s, "dve", a_.t[:, :], pa.t[:, :], cs.t[:, :], ALU.mult, pa.all() + cs.all(), a_.all())
                tt(s, "dve", b_.t[:, :], pb.t[:, :], sn.t[:, :], ALU.mult, pb.all() + sn.all(), b_.all())
                o = o16.get()
                tt(s, "pool", o.t[:, :], a_.t[:, :], b_.t[:, :], ALU.add, a_.all() + b_.all(), o.all())
                dma(s, dr["QR"][pr * 128:(pr + 1) * 128, sl], o.t[:, :], o.all(), ())
            for m in range(2):
                p = pp.get()
                proj(p, hin(OFF_CKV + m * 128, OFF_CKV + (m + 1) * 128))
                cp(s, "act", ckv.t[:, m, :], p.t[:, :], p.all(), [ckv.r[m]])
                q_ = sqb[m]
                act(s, q_.t[:, :], p.t[:, :], AF.Square, p.all(), q_.all())
                mm(s, pssq.t[:, :], ones32.t[:, :], q_.t[:, :], m == 0, m == 1,
                   q_.all() + ones32.all(), pssq.all())
            for sub in range(4):
                for m in range(2):
                    mm(s, pcol.t[:, 2 * sub:2 * sub + 2], sqb[m].t[:, sub * 128:(sub + 1) * 128],
                       ones32.t[:, 0:2], m == 0, m == 1, sqb[m].all() + ones32.all(), pcol.all())
            act(s, rkv.t[:, :], pssq.t[:, :], AF.Ln, pssq.all() + epsr.all(), rkv.all(), bias=epsr.t[:, 0:1],
                scale=1.0 / KVR)
            act(s, rkv.t[:, :], rkv.t[:, :], AF.Exp, rkv.all(), rkv.all(), scale=-0.5)
            act(s, rcol.t[:, :], pcol.t[:, :], AF.Ln, pcol.all() + epsr.all(), rcol.all(), bias=epsr.t[:, 0:1],
                scale=1.0 / KVR)
            act(s, rcol.t[:, :], rcol.t[:, :], AF.Exp, rcol.all(), rcol.all(), scale=-0.5)

            def kvin(w_, c0, c1):
                return [(w_.t[:, k, c0:c1], ckv.t[:, k, :], [w_.r[0], ckv.r[k]]) for k in range(2)]

            for h in range(NH):
                p = pp.get()
                proj(p, kvin(wukv, h * 256, h * 256 + NOPE))
                o = o16.get()
                tt(s, "dve", o.t[:, :], p.t[:, :], rkv.t[:, :], ALU.mult, p.all() + rkv.all(), o.all())
                dma(s, dr["KN"][h, :, sl], o.t[:, :], o.all(), ())
            for sub in range(4):
                p = pp.get()
                proj(p, [(ckv.t[:, k, sub * 128:(sub + 1) * 128], wvv.t[:, k, :], [wvv.r[0], ckv.r[k]])
                         for k in range(2)])
                o = o16.get()
                act(s, o.t[:, :], p.t[:, :], AF.Copy, p.all() + rcol.all(), o.all(),
                    scale=rcol.t[:, 2 * sub:2 * sub + 1])
                t0 = j * T + sub * 128
                dma(s, dr["V"][t0:t0 + 128, :], o.t[:, :], o.all(), ())
            pa, pb = pp.get(), pp.get()
            proj(pa, hin(OFF_KR, OFF_KR + 64), M=64)
            proj(pb, [(wrotk.t[:, k, :], hb.t[:, k, :], [wrotk.r[0], hb.r[k]]) for k in range(8)], M=64)
            a_, b_ = t1.get(), t2.get()
            tt(s, "dve", a_.t[0:64, :], pa.t[0:64, :], cos.t[0:64, :], ALU.mult, pa.all() + cos.all(), a_.all())
            tt(s, "dve", b_.t[0:64, :], pb.t[0:64, :], sin.t[0:64, :], ALU.mult, pb.all() + sin.all(), b_.all())
            o = o16.get()
            tt(s, "pool", o.t[0:64, :], a_.t[0:64, :], b_.t[0:64, :], ALU.add, a_.all() + b_.all(), o.all())
            dma(s, dr["KR"][:, sl], o.t[0:64, :], o.all(), ())
            for name, off in (("HGQ", OFF_HQ), ("HGF", OFF_HF), ("HGG", OFF_HG), ("SU", OFF_SU)):
                for m in range(2):
                    p = pp.get()
                    proj(p, hin(off + m * 128, off + (m + 1) * 128))
                    o = o32.get()
                    evac_i[0] += 1
                    cp(s, "act" if evac_i[0] % 2 else "dve", o.t[:, :], p.t[:, :], p.all(), o.all())
                    dma(s, dr[name][m * 128:(m + 1) * 128, sl], o.t[:, :], o.all(), ())
            for sub in range(4):
                p = pp.get()
                proj(p, [(hb.t[:, k, sub * 128:(sub + 1) * 128], win.t[:, k, OFF_HF:OFF_HF + 512],
                          [win.r[0], hb.r[k]]) for k in range(8)])
                o = o32.get()
                evac_i[0] += 1
                cp(s, "act" if evac_i[0] % 2 else "dve", o.t[:, :], p.t[:, :], p.all(), o.all())
                t0 = j * T + sub * 128
                dma(s, dr["HGT"][t0:t0 + 128, :], o.t[:, :], o.all(), ())
            while pending:
                tick()
            if fuse_ln_in and j + 2 < NTILE:
                load_x(j + 2)
        s.emit(f"a_{l}")


def stage_b1(nc, sync, dr, S, l, with_s5=True):
    with ExitStack() as es:
        c = Ctx(nc, es)
        s = Sched(sync)
        NTILE = S // T
        NCH = S // 128
        vb_ = [c.sb("vh", (128, NCH, DV), BF16) for _ in range(2)]
        knb = [c.sb("kn", (128, S), BF16) for _ in range(2)]
        kr = c.sb("kr", (128, S), BF16)
        vview = dr["V"].rearrange("(c p) (h d) -> h p c d", p=128, h=NH)

        def load_head(h):
            dma(s, knb[h % 2].t[:, :], dr["KN"][h], (), knb[h % 2].all())
            for c0 in range(0, NCH, 16):
                c1 = min(NCH, c0 + 16)
                dma(s, vb_[h % 2].t[:, c0:c1, :], vview[h][:, c0:c1, :], (), vb_[h % 2].all())
        memset(s, "pool", kr.t[64:128, :], 0.0, kr.all())
        dma(s, kr.t[0:64, :], dr["KR"][:, :], (), kr.all())
        load_head(0)
        onesb = c.sb("onesb", (128, 128), BF16)
        memset(s, "pool", onesb.t[:, :], 1.0, onesb.all())
        onesm = c.sb("onesm", (128, T), BF16)
        memset(s, "pool", onesm.t[:, :], 1.0, onesm.all())
        masks = []
        for d_ in range(4):
            m_ = c.sb("mask", (128, T), BF16)

            def fn(eng, m_=m_, d_=d_):
                return eng.affine_select(m_.t[:, :], onesm.t[:, :], pattern=[[1, T]],
                                         compare_op=ALU.is_ge, fill=0.0, base=-128 * d_,
                                         channel_multiplier=-1)
            s.pool(fn, onesm.all(), m_.all())
            ts(s, "dve", m_.t[:, :], m_.t[:, :], 30000.0, -30000.0, ALU.mult, ALU.add, m_.all(), m_.all())
            masks.append(m_)
        ident = c.sb("ident", (128, 128), BF16)

        def fid(eng):
            return eng.affine_select(ident.t[:, :], onesb.t[:, :], pattern=[[-1, 128]], compare_op=ALU.is_equal,
                                     fill=0.0, base=0, channel_multiplier=1)
        s.pool(fid, onesb.all(), ident.all())
        qnb = [c.sb("qn", (128, T), BF16) for _ in range(2)]
        qrb = [c.sb("qr", (128, T), BF16) for _ in range(2)]
        for q_ in qrb:
            memset(s, "pool", q_.t[64:128, :], 0.0, q_.all())
        pss = Rot([c.ps("pss") for _ in range(4)])
        pso = c.ps("pso")
        psl = c.ps("psl")
        ptb = Rot([c.sb("pt", (128, T), BF16) for _ in range(4)])
        rl = c.sb("rl", (128, T), F32)
        ob = Rot([c.sb("ob", (128, T), BF16) for _ in range(2)])
        g3 = gen_b3(s, c, dr, S, l, c.ps("pyr"), None, c.ps("pyg")) if with_s5 else iter(())
        g3_alive = [True]

        def step_s5():
            if g3_alive[0]:
                try:
                    next(g3)
                except StopIteration:
                    g3_alive[0] = False
        blocks = []
        for h in range(NH):
            for i in range(NTILE):
                nb = 4 * i + 4
                for cc in range(nb):
                    blocks.append((h, i, cc, nb))
        state = {}
        tiles = [(h, i) for h in range(NH) for i in range(NTILE)]

        def load_q(tix):
            h, i = tiles[tix]
            sl = slice(i * T, (i + 1) * T)
            dma(s, qnb[tix % 2].t[:, :], dr["QN"][h, :, sl], (), qnb[tix % 2].all())
            dma(s, qrb[tix % 2].t[0:64, :], dr["QR"][h * 64:(h + 1) * 64, sl], (), qrb[tix % 2].all())
        load_q(0)

        def issue_s(n):
            h, i, cc, nb = blocks[n]
            kn = knb[h % 2]
            tix = h * NTILE + i
            qn, qr = qnb[tix % 2], qrb[tix % 2]
            if cc == 0:
                if tix + 1 < len(tiles):
                    load_q(tix + 1)
            p = pss.get()
            ks = slice(cc * 128, (cc + 1) * 128)
            diag = cc >= 4 * i
            qs = slice(128 * (cc - 4 * i), T) if diag else slice(0, T)
            mm(s, p.t[:, qs], kn.t[:, ks], qn.t[:, qs], True, False, kn.all() + qn.all(), p.all())
            mm(s, p.t[:, qs], kr.t[:, ks], qr.t[:, qs], False, not diag, kr.all() + qr.all(), p.all())
            if diag:
                mk = masks[cc - 4 * i]
                mm(s, p.t[:, qs], ident.t[:, :], mk.t[:, qs], False, True, ident.all() + mk.all(), p.all())
            state[n] = (p, qs)

        def issue_pv(n):
            h, i, cc, nb = blocks[n]
            if i == 0 and cc == 0 and h + 1 < NH:
                load_head(h + 1)
            p, qs = state.pop(n)
            pt = ptb.get()
            act(s, pt.t[:, qs], p.t[:, qs], AF.Exp, p.all(), pt.all())
            vh = vb_[h % 2]
            mm(s, pso.t[:, qs], vh.t[:, cc, :], pt.t[:, qs], cc == 0, cc == nb - 1,
               vh.all() + pt.all(), pso.all())
            mm(s, psl.t[:, qs], onesb.t[:, :], pt.t[:, qs], cc == 0, cc == nb - 1,
               onesb.all() + pt.all(), psl.all())
            if cc == nb - 1:
                act(s, rl.t[:, :], psl.t[:, :], AF.Ln, psl.all(), rl.all())
                act(s, rl.t[:, :], rl.t[:, :], AF.Exp, rl.all(), rl.all(), scale=-1.0)
                o = ob.get()
                tt(s, "dve", o.t[:, :], pso.t[:, :], rl.t[:, :], ALU.mult, pso.all() + rl.all(), o.all())
                dma(s, dr["MIX"][h * DV:(h + 1) * DV, i * T:(i + 1) * T], o.t[:, :], o.all(), ())

        NB = len(blocks)
        LOOK = 3
        n_s5 = (S // TS5) * 13 + 24
        pace = max(1, NB // n_s5) if B13_PACE is None else B13_PACE
        for n in range(min(LOOK, NB)):
            issue_s(n)
        for n in range(NB):
            issue_pv(n)
            if n + LOOK < NB:
                issue_s(n + LOOK)
            if n % pace == 0:
                step_s5()
        while g3_alive[0]:
            step_s5()
        s.emit(f"b1_{l}")


def build_masks(s, c):
    ones = c.sb("mones", (128, 128), F32)
    memset(s, "pool", ones.t[:, :], 1.0, ones.all())
    U = c.sb("U", (128, 128), F32)
    Lm = c.sb("Lm", (128, 128), F32)
    BD = c.sb("BD", (128, 128), F32)

    def fu(eng):
        return eng.affine_select(U.t[:, :], ones.t[:, :], pattern=[[1, 128]], compare_op=ALU.is_ge,
                                 fill=0.0, base=0, channel_multiplier=-1)
    s.pool(fu, ones.all(), U.all())
    memset(s, "pool", U.t[0:64, 64:128], 0.0, U.all())

    def fl(eng):
        return eng.affine_select(Lm.t[:, :], ones.t[:, :], pattern=[[-1, 128]], compare_op=ALU.is_gt,
                                 fill=0.0, base=0, channel_multiplier=1)
    s.pool(fl, ones.all(), Lm.all())
    memset(s, "pool", Lm.t[64:128, 0:64], 0.0, Lm.all())
    memset(s, "pool", BD.t[:, :], 1.0, BD.all())
    memset(s, "pool", BD.t[0:64, 64:128], 0.0, BD.all())
    memset(s, "pool", BD.t[64:128, 0:64], 0.0, BD.all())
    return U, Lm, BD


def stage_b2(nc, sync, dr, S, l):
    with ExitStack() as es:
        c = Ctx(nc, es)
        s = Sched(sync)
        NTILE = S // T
        U, Lm, BD = build_masks(s, c)
        Ub = c.sb("Ub", (128, 128), BF16)
        cp(s, "pool", Ub.t[:, :], U.t[:, :], U.all(), Ub.all())
        ng = c.sb("ng", (128, 1), F32)
        dma(s, ng.t[:, :], dr["hg_ng"][l], (), ng.all())
        eps = c.sb("eps", (128, 1), F32)
        memset(s, "pool", eps.t[:, :], RMS_EPS, eps.all())
        use_lb = l > 0
        if use_lb:
            assert l == 1 and DEPTH == 2
            lbf = c.sb("lbf", (128, 2, 2), F32)
            dma(s, lbf.t[:, 0, :], dr["hg_lb_fm"][0], (), lbf.all())
            dma(s, lbf.t[:, 1, :], dr["hg_lb_fm"][1], (), lbf.all())
            lb = c.sb("lb", (128, 2), F32)
            oml = c.sb("oml", (128, 2), F32)
            tt(s, "dve", lb.t[:, :], lbf.t[:, 1, :], lbf.t[:, 0, :], ALU.subtract, lbf.all(), lb.all())
            act(s, lb.t[:, :], lb.t[:, :], AF.Sigmoid, lb.all(), lb.all())
            ts(s, "dve", oml.t[:, :], lb.t[:, :], -1.0, 1.0, ALU.mult, ALU.add, lb.all(), oml.all())
            lbr2 = c.sb("lbr2", (128, 2, 256), F32)
            dma(s, lbr2.t[:, 0, :], dr["hg_lower_bound"][0:1, :].partition_broadcast(128), (), lbr2.all())
            dma(s, lbr2.t[:, 1, :], dr["hg_lower_bound"][1:2, :].partition_broadcast(128), (), lbr2.all())
            lbrow = c.sb("lbrow", (128, 4, 256), F32)
            omlrow = c.sb("omlrow", (128, 4, 256), F32)
            tt(s, "dve", lbrow.t[:, 0, :], lbr2.t[:, 1, :], lbr2.t[:, 0, :], ALU.subtract, lbr2.all(), lbrow.all())
            act(s, lbrow.t[:, 0, :], lbrow.t[:, 0, :], AF.Sigmoid, lbrow.all(), lbrow.all())
            for u in range(1, 4):
                cp(s, "dve", lbrow.t[:, u, :], lbrow.t[:, 0, :], lbrow.all(), lbrow.all())
            ts(s, "dve", omlrow.t[:, :, :], lbrow.t[:, :, :], -1.0, 1.0, ALU.mult, ALU.add,
               lbrow.all(), omlrow.all())
        S32 = [c.sb("S32", (128, 128), F32) for _ in range(2)]
        Sb = [c.sb("Sb", (128, 128), BF16) for _ in range(2)]
        for g in range(2):
            memset(s, "pool", S32[g].t[:, :], 0.0, S32[g].all())
            memset(s, "pool", Sb[g].t[:, :], 0.0, Sb[g].all())
        vpad = [c.sb("vpad", (128, 4, 128), BF16) for _ in range(NH)]
        for h in range(NH):
            memset(s, "pool", vpad[h].t[:, :, :], 0.0, vpad[h].all())
        def two(f):
            return [f(), f()]
        vpad2 = two(lambda: [c.sb("vpad", (128, 4, 128), BF16) for _ in range(NH)])
        for sl_ in range(2):
            for h in range(NH):
                memset(s, "pool", vpad2[sl_][h].t[:, :, :], 0.0, vpad2[sl_][h].all())
        qf2 = two(lambda: c.sb("qf", (128, 2, T), F32, 2))
        zf2 = two(lambda: c.sb("zf", (128, 2, T), F32, 2))
        gf2 = two(lambda: c.sb("gf", (128, 2, T), F32, 2))
        tm2 = two(lambda: c.sb("tm", (128, 4, 512), F32))
        ft2 = two(lambda: c.sb("ft", (128, 4, 256), F32))
        lft2 = two(lambda: c.sb("lft", (128, 4, 256), F32))
        ecs2 = two(lambda: c.sb("ecs", (128, 4, 256), F32))
        khat2 = two(lambda: c.sb("khat", (128, 4, 256), BF16))
        vb2 = two(lambda: c.sb("vb", (128, 4, 256), BF16))
        eb2 = two(lambda: [c.sb("eb", (128, T), F32) for _ in range(2)])
        enb2 = two(lambda: [c.sb("enb", (128, T), F32) for _ in range(2)])
        qt2 = two(lambda: [c.sb("qt", (128, T), BF16) for _ in range(2)])
        kt2 = two(lambda: [c.sb("kt", (128, T), BF16) for _ in range(2)])
        attb = Rot([c.sb("attb", (128, 128), BF16) for _ in range(3)])
        o32 = [c.sb("o32", (128, T), F32) for _ in range(2)]
        sq = [c.sb("sq", (128, T), F32) for _ in range(2)]
        rs = [c.sb("rs", (128, T), F32) for _ in range(2)]
        ob = [c.sb("ob", (128, T), BF16) for _ in range(2)]
        pb = [c.ps("pb") for _ in range(2)]
        pc = c.ps("pc", (128, 1024))
        po = [c.ps("po") for _ in range(2)]
        patt = c.ps("patt", (128, 128))
        pu = c.ps("pu", (128, 128))
        tmv = dr["HGT"].rearrange("(j u p) n -> j p u n", p=128, u=4)

        def load(j):
            sl = slice(j * T, (j + 1) * T)
            k_ = j % 2
            dma(s, qf2[k_].t[:, :, :], fm(dr["HGQ"])[:, :, sl], (), qf2[k_].all())
            dma(s, zf2[k_].t[:, :, :], fm(dr["HGF"])[:, :, sl], (), zf2[k_].all())
            dma(s, gf2[k_].t[:, :, :], fm(dr["HGG"])[:, :, sl], (), gf2[k_].all())
            dma(s, tm2[k_].t[:, :, :], tmv[j], (), tm2[k_].all())

        def prep_steps(j):
            k_ = j % 2
            qf, zf, gf, tm, ft, lft, ecs, khat, vb = (qf2[k_], zf2[k_], gf2[k_], tm2[k_], ft2[k_], lft2[k_],
                                                      ecs2[k_], khat2[k_], vb2[k_])
            eb, enb, qt, kt, vpad = eb2[k_], enb2[k_], qt2[k_], kt2[k_], vpad2[k_]
            st = []

            def a1():
                act(s, zf.t[:, :, :], zf.t[:, :, :], AF.Sigmoid, zf.all(), zf.all())
                act(s, ft.t[:, :, :], tm.t[:, :, 0:256], AF.Sigmoid, tm.all(), ft.all())
                act(s, gf.t[:, :, :], gf.t[:, :, :], AF.Silu, gf.all(), gf.all())
            st.append(a1)

            def a2():
                for g in range(2):
                    if use_lb:
                        ts(s, "dve", zf.t[:, g, :], zf.t[:, g, :], oml.t[:, g:g + 1], lb.t[:, g:g + 1],
                           ALU.mult, ALU.add, [zf.r[g]] + oml.all() + lb.all(), [zf.r[g]])
                    ts(s, "dve", zf.t[:, g, :], zf.t[:, g, :], -1.0, 1.0, ALU.mult, ALU.add, [zf.r[g]], [zf.r[g]])
                if use_lb:
                    tt(s, "dve", ft.t[:, :, :], ft.t[:, :, :], omlrow.t[:, :, :], ALU.mult, ft.all() + omlrow.all(), ft.all())
                    tt(s, "dve", ft.t[:, :, :], ft.t[:, :, :], lbrow.t[:, :, :], ALU.add, ft.all() + lbrow.all(), ft.all())
            st.append(a2)

            def a3():
                act(s, lft.t[:, :, :], ft.t[:, :, :], AF.Ln, ft.all(), lft.all())
                cp(s, "pool", vb.t[:, :, :], tm.t[:, :, 256:512], tm.all(), vb.all())
            st.append(a3)

            def a4():
                ts(s, "dve", ft.t[:, :, :], ft.t[:, :, :], -1.0, 1.0, ALU.mult, ALU.add, ft.all(), ft.all())
                for h in range(NH):
                    cp(s, "pool", vpad[h].t[:, :, (h % 2) * 64:(h % 2) * 64 + 64], vb.t[:, :, h * 64:(h + 1) * 64],
                       vb.all(), vpad[h].all())
            st.append(a4)

            def a5():
                for g in range(2):
                    for u in range(4):
                        mm(s, pb[g].t[:, u * 128:(u + 1) * 128], lft.t[:, u, g * 128:(g + 1) * 128], U.t[:, :],
                           u == 0, u == 3, lft.all() + U.all(), pb[g].all())
            st.append(a5)

            def a6():
                for u in range(4):
                    mm(s, pc.t[:, u * 256:(u + 1) * 256], Lm.t[:, :], lft.t[:, u, :], u % 2 == 0, u % 2 == 1,
                       lft.all() + Lm.all(), pc.all())
            st.append(a6)

            def a7():
                for g in range(2):
                    act(s, eb[g].t[:, :], pb[g].t[:, :], AF.Exp, pb[g].all(), eb[g].all())
                    act(s, enb[g].t[:, :], pb[g].t[:, :], AF.Exp, pb[g].all(), enb[g].all(), scale=-1.0)
                act(s, ecs.t[:, :, :], pc.t[:, :], AF.Exp, pc.all(), ecs.all())
            st.append(a7)

            def a8():
                for g in range(2):
                    tt(s, "dve", qt[g].t[:, :], qf.t[:, g, :], eb[g].t[:, :], ALU.mult, [qf.r[g]] + eb[g].all(), qt[g].all())
                    tt(s, "dve", kt[g].t[:, :], zf.t[:, g, :], enb[g].t[:, :], ALU.mult, [zf.r[g]] + enb[g].all(), kt[g].all())
                tt(s, "dve", khat.t[:, :, :], ft.t[:, :, :], ecs.t[:, :, :], ALU.mult, ft.all() + ecs.all(), khat.all())
            st.append(a8)
            return st

        def rec_steps(j):
            k_ = j % 2
            gf, khat, vb = gf2[k_], khat2[k_], vb2[k_]
            eb, qt, kt, vpad = eb2[k_], qt2[k_], kt2[k_], vpad2[k_]
            sl = slice(j * T, (j + 1) * T)
            st = []
            first = [True, True]
            for u in range(4):
                us = slice(u * 128, (u + 1) * 128)

                def r_att(u=u, us=us):
                    for h in range(NH):
                        g, hp = h // 2, (h % 2) * 64
                        mm(s, patt.t[:, :], kt[g].t[hp:hp + 64, us], qt[g].t[hp:hp + 64, us], True, True,
                           kt[g].all() + qt[g].all(), patt.all())
                        ab = attb.get()
                        tt(s, "dve", ab.t[:, :], patt.t[:, :], Ub.t[:, :], ALU.mult, patt.all() + Ub.all(), ab.all())
                        mm(s, po[g].t[:, us], vpad[h].t[:, u, :], ab.t[:, :], first[g], False,
                           vpad[h].all() + ab.all(), po[g].all())
                        first[g] = False
                st.append(r_att)
                for ch in range(2):
                    def r_upd(u=u, ch=ch):
                        cs_ = slice(u * 128 + ch * 64, u * 128 + ch * 64 + 64)
                        rows = slice(ch * 64, ch * 64 + 64)
                        for g in range(2):
                            last = (u == 3 and ch == 1)
                            mm(s, po[g].t[:, cs_], Sb[g].t[:, :], qt[g].t[:, cs_], False, last,
                               Sb[g].all() + qt[g].all(), po[g].all())
                            mm(s, pu.t[:, :], khat.t[rows, u, g * 128:(g + 1) * 128], vb.t[rows, u, g * 128:(g + 1) * 128],
                               True, True, khat.all() + vb.all(), pu.all())
                            col = u * 128 + ch * 64 + 63
                            stt(s, S32[g].t[:, :], S32[g].t[:, :], eb[g].t[:, col:col + 1], pu.t[:, :], ALU.mult, ALU.add,
                                S32[g].all() + eb[g].all() + pu.all(), S32[g].all())
                            tt(s, "dve", Sb[g].t[:, :], S32[g].t[:, :], BD.t[:, :], ALU.mult, S32[g].all() + BD.all(), Sb[g].all())
                    st.append(r_upd)

            def fin():
                for g in range(2):
                    cp(s, "act", o32[g].t[:, :], po[g].t[:, :], po[g].all(), o32[g].all())
                    act(s, sq[g].t[:, :], po[g].t[:, :], AF.Square, po[g].all(), sq[g].all())
                    mm(s, pb[g].t[:, :], BD.t[:, :], sq[g].t[:, :], True, True, BD.all() + sq[g].all(), pb[g].all())
                    act(s, rs[g].t[:, :], pb[g].t[:, :], AF.Ln, pb[g].all() + eps.all(), rs[g].all(), bias=eps.t[:, 0:1], scale=1.0 / HDV)
                    act(s, rs[g].t[:, :], rs[g].t[:, :], AF.Exp, rs[g].all(), rs[g].all(), scale=-0.5)
                    tt(s, "dve", o32[g].t[:, :], o32[g].t[:, :], rs[g].t[:, :], ALU.mult, o32[g].all() + rs[g].all(), o32[g].all())
                    stt(s, ob[g].t[:, :], o32[g].t[:, :], ng.t[:, 0:1], gf.t[:, g, :], ALU.mult, ALU.mult,
                        o32[g].all() + ng.all() + [gf.r[g]], ob[g].all())
                    r0 = NH * DV + g * 128
                    dma(s, dr["MIX"][r0:r0 + 128, sl], ob[g].t[:, :], ob[g].all(), ())
            return st, fin

        load(0)
        for f_ in prep_steps(0):
            f_()
        for j in range(NTILE):
            if j + 1 < NTILE:
                load(j + 1)
                nxt = prep_steps(j + 1)
            else:
                nxt = []
            rsteps, fin = rec_steps(j)
            n_r = len(rsteps)
            early = nxt[:4]
            late = nxt[4:]
            for i_, r_ in enumerate(rsteps):
                r_()
                if i_ < len(early):
                    early[i_]()
            fin()
            for f_ in late:
                f_()
        s.emit(f"b2_{l}")


TWO_PI = 2.0 * math.pi


def emit_sin(s, c, out, ang, shape, tag):
    ki = c.sb(f"ki{tag}", shape, I32)
    kf = c.sb(f"kf{tag}", shape, F32)
    r = c.sb(f"r{tag}", shape, F32)
    m = c.sb(f"m{tag}", shape, F32)
    ts(s, "dve", kf.t[:, :], ang.t[:, :], 1.0 / TWO_PI, None, ALU.mult, None, ang.all(), kf.all())
    cp(s, "dve", ki.t[:, :], kf.t[:, :], kf.all(), ki.all())
    cp(s, "dve", kf.t[:, :], ki.t[:, :], ki.all(), kf.all())
    stt(s, r.t[:, :], kf.t[:, :], -TWO_PI, ang.t[:, :], ALU.mult, ALU.add, kf.all() + ang.all(), r.all())
    ts(s, "dve", m.t[:, :], r.t[:, :], math.pi, -TWO_PI, ALU.is_gt, ALU.mult, r.all(), m.all())
    tt(s, "dve", r.t[:, :], r.t[:, :], m.t[:, :], ALU.add, r.all() + m.all(), r.all())
    ts(s, "dve", m.t[:, :], r.t[:, :], -math.pi, TWO_PI, ALU.is_lt, ALU.mult, r.all(), m.all())
    tt(s, "dve", r.t[:, :], r.t[:, :], m.t[:, :], ALU.add, r.all() + m.all(), r.all())
    ts(s, "dve", r.t[:, :], r.t[:, :], math.pi, -math.pi, ALU.min, ALU.max, r.all(), r.all())
    act(s, out.t[:, :], r.t[:, :], AF.Sin, r.all(), out.all())


TS5 = 256
S5_MAXACT = 16
B13_PACE = None
FUSE_LN_IN = False


def gen_b3(s, c, dr, S, l, pyr, pyi, pyg):
    TT = TS5
    NTILE = S // TT
    NK = 8
    sh = (128, NK)
    ar = c.sb("ar", sh, F32)
    ai = c.sb("ai", sh, F32)
    dt = c.sb("dt", sh, F32)
    dma(s, ar.t[:, :], dr["s5_are_pp"][l], (), ar.all())
    dma(s, ai.t[:, :], dr["s5_aim_pp"][l], (), ai.all())
    dma(s, dt.t[:, :], dr["s5_ldt_pp"][l], (), dt.all())
    act(s, dt.t[:, :], dt.t[:, :], AF.Exp, dt.all(), dt.all())
    mag = c.sb("mag", sh, F32)
    th = c.sb("th", sh, F32)
    th2 = c.sb("th2", sh, F32)
    tt(s, "dve", mag.t[:, :], dt.t[:, :], ar.t[:, :], ALU.mult, dt.all() + ar.all(), mag.all())
    act(s, mag.t[:, :], mag.t[:, :], AF.Exp, mag.all(), mag.all())
    tt(s, "dve", th.t[:, :], dt.t[:, :], ai.t[:, :], ALU.mult, dt.all() + ai.all(), th.all())
    ts(s, "dve", th2.t[:, :], th.t[:, :], math.pi / 2, None, ALU.add, None, th.all(), th2.all())
    sn1 = c.sb("sn1", sh, F32)
    cs1 = c.sb("cs1", sh, F32)
    emit_sin(s, c, sn1, th, sh, "a")
    emit_sin(s, c, cs1, th2, sh, "b")
    yield
    nre = c.sb("nre", sh, F32)
    nim = c.sb("nim", sh, F32)
    tt(s, "dve", nre.t[:, :], mag.t[:, :], cs1.t[:, :], ALU.mult, mag.all() + cs1.all(), nre.all())
    ts(s, "dve", nre.t[:, :], nre.t[:, :], -1.0, None, ALU.add, None, nre.all(), nre.all())
    tt(s, "dve", nim.t[:, :], mag.t[:, :], sn1.t[:, :], ALU.mult, mag.all() + sn1.all(), nim.all())
    den = c.sb("den", sh, F32)
    t_a = c.sb("t_a", sh, F32)
    t_b = c.sb("t_b", sh, F32)
    tt(s, "dve", den.t[:, :], ar.t[:, :], ar.t[:, :], ALU.mult, ar.all(), den.all())
    tt(s, "dve", t_a.t[:, :], ai.t[:, :], ai.t[:, :], ALU.mult, ai.all(), t_a.all())
    tt(s, "dve", den.t[:, :], den.t[:, :], t_a.t[:, :], ALU.add, den.all() + t_a.all(), den.all())

    def frecip(eng):
        return eng.reciprocal(den.t[:, :], den.t[:, :])
    s.dve(frecip, den.all(), den.all())
    cre = c.sb("cre", sh, F32)
    cim = c.sb("cim", sh, F32)
    tt(s, "dve", t_a.t[:, :], nre.t[:, :], ar.t[:, :], ALU.mult, nre.all() + ar.all(), t_a.all())
    tt(s, "dve", t_b.t[:, :], nim.t[:, :], ai.t[:, :], ALU.mult, nim.all() + ai.all(), t_b.all())
    tt(s, "dve", cre.t[:, :], t_a.t[:, :], t_b.t[:, :], ALU.add, t_a.all() + t_b.all(), cre.all())
    tt(s, "dve", cre.t[:, :], cre.t[:, :], den.t[:, :], ALU.mult, cre.all() + den.all(), cre.all())
    tt(s, "dve", t_a.t[:, :], nim.t[:, :], ar.t[:, :], ALU.mult, nim.all() + ar.all(), t_a.all())
    tt(s, "dve", t_b.t[:, :], nre.t[:, :], ai.t[:, :], ALU.mult, nre.all() + ai.all(), t_b.all())
    tt(s, "dve", cim.t[:, :], t_a.t[:, :], t_b.t[:, :], ALU.subtract, t_a.all() + t_b.all(), cim.all())
    tt(s, "dve", cim.t[:, :], cim.t[:, :], den.t[:, :], ALU.mult, cim.all() + den.all(), cim.all())
    yield
    Ct = c.sb("Ct", (128, NK, TT), F32, NK)
    St = c.sb("St", (128, NK, TT), F32, NK)
    memset(s, "pool", Ct.t[:, :, 0:1], 1.0, Ct.all())
    memset(s, "pool", St.t[:, :, 0:1], 0.0, St.all())
    kre = [cs1]
    kim = [sn1]
    nstep = int(math.log2(TT))
    for k in range(1, nstep + 1):
        a_, b_ = c.sb(f"kre{k}", sh, F32), c.sb(f"kim{k}", sh, F32)
        p_, q_ = kre[-1], kim[-1]
        tt(s, "dve", t_a.t[:, :], p_.t[:, :], p_.t[:, :], ALU.mult, p_.all(), t_a.all())
        tt(s, "dve", t_b.t[:, :], q_.t[:, :], q_.t[:, :], ALU.mult, q_.all(), t_b.all())
        tt(s, "dve", a_.t[:, :], t_a.t[:, :], t_b.t[:, :], ALU.subtract, t_a.all() + t_b.all(), a_.all())
        stt(s, b_.t[:, :], p_.t[:, :], 2.0, q_.t[:, :], ALU.mult, ALU.mult, p_.all() + q_.all(), b_.all())
        kre.append(a_)
        kim.append(b_)
    yield
    tmpd = [c.sb("tmpd", (128, TT // 2), F32) for _ in range(2)]
    for k in range(nstep):
        n = 1 << k
        for kc in range(NK):
            cr, ci = kre[k].t[:, kc:kc + 1], kim[k].t[:, kc:kc + 1]
            rd = [Ct.r[kc], St.r[kc]] + kre[k].all() + kim[k].all()
            ta, tb = tmpd[0], tmpd[1]
            ts(s, "dve", ta.t[:, 0:n], St.t[:, kc, 0:n], ci, None, ALU.mult, None, rd, ta.all())
            stt(s, Ct.t[:, kc, n:2 * n], Ct.t[:, kc, 0:n], cr, ta.t[:, 0:n], ALU.mult, ALU.subtract,
                rd + ta.all(), [Ct.r[kc]])
            ts(s, "dve", tb.t[:, 0:n], Ct.t[:, kc, 0:n], ci, None, ALU.mult, None, rd, tb.all())
            stt(s, St.t[:, kc, n:2 * n], St.t[:, kc, 0:n], cr, tb.t[:, 0:n], ALU.mult, ALU.add,
                rd + tb.all(), [St.r[kc]])
        yield
    ETr, ETi = kre[nstep], kim[nstep]
    Pr = c.sb("Pr", (128, NK, TT), F32, NK)
    Pi = c.sb("Pi", (128, NK, TT), F32, NK)
    magt = c.sb("magt", (128, NK, TT), F32, NK)
    tfull = c.sb("tfull", (128, TT), F32)
    for kc in range(NK):
        cr, ci = cre.t[:, kc:kc + 1], cim.t[:, kc:kc + 1]
        rd = [Ct.r[kc], St.r[kc]] + cre.all() + cim.all()
        ts(s, "dve", tfull.t[:, :], St.t[:, kc, :], ci, None, ALU.mult, None, rd, tfull.all())
        stt(s, Pr.t[:, kc, :], Ct.t[:, kc, :], cr, tfull.t[:, :], ALU.mult, ALU.add, rd + tfull.all(), [Pr.r[kc]])
        ts(s, "dve", tfull.t[:, :], St.t[:, kc, :], cr, None, ALU.mult, None, rd, tfull.all())
        stt(s, Pi.t[:, kc, :], Ct.t[:, kc, :], ci, tfull.t[:, :], ALU.mult, ALU.subtract, rd + tfull.all(), [Pi.r[kc]])
        memset(s, "pool", magt.t[:, kc, :], 1.0, [magt.r[kc]])
        ts(s, "pool", magt.t[:, kc, :], magt.t[:, kc, :], mag.t[:, kc:kc + 1], None, ALU.mult, None,
           [magt.r[kc]] + mag.all(), [magt.r[kc]])
        if kc % 2 == 1:
            yield
    bre = c.sb("bre", (128, NK, 128), BF16)
    bim = c.sb("bim", (128, NK, 128), BF16)
    ctr = c.sb("ctr", (128, NK, 128), BF16)
    cti = c.sb("cti", (128, NK, 128), BF16)
    st = c.sb("stw", (128, NK, 128), F32)
    dma(s, bre.t[:, :, :], dr["s5_bT_re"][l], (), bre.all(), q="pool")
    dma(s, bim.t[:, :, :], dr["s5_bT_im"][l], (), bim.all(), q="pool")
    dma(s, ctr.t[:, :, :], dr["s5_cT_re"][l], (), ctr.all(), q="pool")
    dma(s, st.t[:, :, :], dr["s5_cT_im"][l], (), st.all())
    ts(s, "dve", cti.t[:, :, :], st.t[:, :, :], -1.0, None, ALU.mult, None, st.all(), cti.all())
    wg = c.sb("wg", (128, 2, 256), BF16)
    load_w_bf16(s, wg, wv(dr["s5_w_glu"][l]), 2, 256)
    dsk = c.sb("dsk", (128, 2), F32)
    bg = c.sb("bg", (128, 2), F32)
    dma(s, dsk.t[:, :], dr["s5_d_fm"][l], (), dsk.all())
    dma(s, bg.t[:, :], dr["s5_bglu_fm"][l], (), bg.all())
    gl_re = c.sb("gl_re", sh, F32)
    gl_im = c.sb("gl_im", sh, F32)
    memset(s, "pool", gl_re.t[:, :], 0.0, gl_re.all())
    memset(s, "pool", gl_im.t[:, :], 0.0, gl_im.all())
    ini_re = c.sb("ini_re", sh, F32)
    ini_im = c.sb("ini_im", sh, F32)
    u32 = [c.sb("u32", (128, 2, TT), F32, 2) for _ in range(3)]
    ubb = [c.sb("ub", (128, 2, TT), BF16, 2) for _ in range(3)]
    DEPTH_R = 2
    mA = Rot([c.sb("mA", (128, TT), F32) for _ in range(DEPTH_R)])
    mB = Rot([c.sb("mB", (128, TT), F32) for _ in range(DEPTH_R)])
    mC = Rot([c.sb("mC", (128, TT), F32) for _ in range(DEPTH_R)])
    mD = Rot([c.sb("mD", (128, TT), F32) for _ in range(DEPTH_R)])
    xre_r = Rot([c.sb("xre", (128, TT), F32) for _ in range(DEPTH_R)])
    xim_r = Rot([c.sb("xim", (128, TT), F32) for _ in range(DEPTH_R)])
    gre_r = Rot([c.sb("gre", (128, TT), F32) for _ in range(DEPTH_R)])
    gim_r = Rot([c.sb("gim", (128, TT), F32) for _ in range(DEPTH_R)])
    hreb = [c.sb("hre", (128, NK, TT), BF16, NK) for _ in range(2)]
    himb = [c.sb("him", (128, NK, TT), BF16, NK) for _ in range(2)]
    yv = [c.sb("yv", (128, TT), F32) for _ in range(2)]
    yt = [c.sb("yt", (128, TT), F32) for _ in range(2)]
    ygb = c.sb("ygb", (128, 2, TT), BF16, 2)
    sg = [c.sb("sg", (128, TT), F32) for _ in range(2)]
    ob = [c.sb("ob", (128, TT), BF16) for _ in range(2)]
    yield

    def nop():
        pass

    def pre_item(j):
        def f1():
            dma(s, u32[j % 3].t[:, :, :], fm(dr["SU"])[:, :, j * TT:(j + 1) * TT], (), u32[j % 3].all())

        def f2():
            cp(s, "act", ubb[j % 3].t[:, :, :], u32[j % 3].t[:, :, :], u32[j % 3].all(), ubb[j % 3].all())
        return [f1, nop, f2]

    item_no = [0]

    def kc_item(j, kc):
        n_ = item_no[0]
        item_no[0] += 1
        half = 0
        hs = slice(0, TT)
        ub = ubb[j % 3]
        hre, him = hreb[j % 2], himb[j % 2]
        ic = kc // 4
        st_ = {}

        def s1():
            mm(s, pyr.t[:, 0:TT], bre.t[:, kc, :], ub.t[:, ic, :], True, True, bre.all() + [ub.r[ic]], pyr.all())
            mm(s, pyr.t[:, TT:2 * TT], bim.t[:, kc, :], ub.t[:, ic, :], True, True, bim.all() + [ub.r[ic]], pyr.all())

        def s2():
            tA, tB, tC, tD = mA.get(), mB.get(), mC.get(), mD.get()
            st_["m"] = (tA, tB, tC, tD)
            y_re, y_im = pyr.t[:, 0:TT], pyr.t[:, TT:2 * TT]
            tt(s, "dve", tA.t[:, :], y_re, Pr.t[:, kc, :], ALU.mult, pyr.all() + [Pr.r[kc]], tA.all())
            tt(s, "dve", tB.t[:, :], y_im, Pi.t[:, kc, :], ALU.mult, pyr.all() + [Pi.r[kc]], tB.all())
            tt(s, "dve", tC.t[:, :], y_im, Pr.t[:, kc, :], ALU.mult, pyr.all() + [Pr.r[kc]], tC.all())
            tt(s, "dve", tD.t[:, :], y_re, Pi.t[:, kc, :], ALU.mult, pyr.all() + [Pi.r[kc]], tD.all())

        def s3():
            pass

        def s4():
            tA, tB, tC, tD = st_["m"]
            xre, xim = xre_r.get(), xim_r.get()
            st_["x"] = (xre, xim)
            tt(s, "pool", xre.t[:, :], tA.t[:, :], tB.t[:, :], ALU.subtract, tA.all() + tB.all(), xre.all())
            tt(s, "pool", xim.t[:, :], tC.t[:, :], tD.t[:, :], ALU.add, tC.all() + tD.all(), xim.all())

        def s5():
            xre, xim = st_["x"]
            gre, gim = gre_r.get(), gim_r.get()
            st_["g"] = (gre, gim)
            for (xx, gg, ini) in ((xre, gre, ini_re), (xim, gim, ini_im)):
                def fscan(eng, xx=xx, gg=gg, ini=ini):
                    return eng.tensor_tensor_scan(gg.t[:, :], magt.t[:, kc, :], xx.t[:, :],
                                                  ini.t[:, kc:kc + 1], ALU.mult, ALU.add)
                s.dve(fscan, [magt.r[kc]] + xx.all() + ini.all(), gg.all())

        def s6():
            gre, gim = st_["g"]
            cp(s, "pool", gl_re.t[:, kc:kc + 1], gre.t[:, TT - 1:TT], gre.all(), gl_re.all())
            cp(s, "pool", gl_im.t[:, kc:kc + 1], gim.t[:, TT - 1:TT], gim.all(), gl_im.all())
            tA, tB, tC, tD = mA.get(), mB.get(), mC.get(), mD.get()
            st_["m2"] = (tA, tB, tC, tD)
            tt(s, "dve", tA.t[:, :], gre.t[:, :], Ct.t[:, kc, :], ALU.mult, gre.all() + [Ct.r[kc]], tA.all())
            tt(s, "pool", tB.t[:, :], gim.t[:, :], St.t[:, kc, :], ALU.mult, gim.all() + [St.r[kc]], tB.all())
            tt(s, "dve", tC.t[:, :], gre.t[:, :], St.t[:, kc, :], ALU.mult, gre.all() + [St.r[kc]], tC.all())
            tt(s, "pool", tD.t[:, :], gim.t[:, :], Ct.t[:, kc, :], ALU.mult, gim.all() + [Ct.r[kc]], tD.all())

        def s7():
            tA, tB, tC, tD = st_["m2"]
            tt(s, "pool", hre.t[:, kc, :], tA.t[:, :], tB.t[:, :], ALU.subtract, tA.all() + tB.all(), [hre.r[kc]])
            tt(s, "pool", him.t[:, kc, :], tC.t[:, :], tD.t[:, :], ALU.add, tC.all() + tD.all(), [him.r[kc]])
        return [s1, s2, s4, s5, s6, s7]

    def fin_item(j):
        sl = slice(j * TT, (j + 1) * TT)
        u = u32[j % 3]
        hre, him = hreb[j % 2], himb[j % 2]
        hv_ = [slice(0, TT), slice(TT, 2 * TT)]

        def f1():
            for oc in range(2):
                for i_, kc in enumerate(range(4 * oc, 4 * oc + 4)):
                    mm(s, pyg.t[:, hv_[oc]], ctr.t[:, kc, :], hre.t[:, kc, :], i_ == 0, False,
                       ctr.all() + [hre.r[kc]], pyg.all())
                    mm(s, pyg.t[:, hv_[oc]], cti.t[:, kc, :], him.t[:, kc, :], False, i_ == 3,
                       cti.all() + [him.r[kc]], pyg.all())

        def f2():
            for oc in range(2):
                stt(s, yv[oc].t[:, :], u.t[:, oc, :], dsk.t[:, oc:oc + 1], pyg.t[:, hv_[oc]], ALU.mult, ALU.add,
                    [u.r[oc]] + dsk.all() + pyg.all(), yv[oc].all())

        def f3():
            for oc in range(2):
                tt(s, "pool", yt[oc].t[:, :], yv[oc].t[:, :], yv[oc].t[:, :], ALU.mult, yv[oc].all(), yt[oc].all())

        def f4():
            for oc in range(2):
                t_, y_ = yt[oc], yv[oc]
                ts(s, "dve", t_.t[:, :], t_.t[:, :], 0.044715, 1.0, ALU.mult, ALU.add, t_.all(), t_.all())
                tt(s, "dve", t_.t[:, :], t_.t[:, :], y_.t[:, :], ALU.mult, t_.all() + y_.all(), t_.all())

        def f5():
            for oc in range(2):
                t_ = yt[oc]
                act(s, t_.t[:, :], t_.t[:, :], AF.Sigmoid, t_.all(), t_.all(), scale=2.0 * math.sqrt(2.0 / math.pi))

        def f6():
            for oc in range(2):
                t_, y_ = yt[oc], yv[oc]
                tt(s, "dve", y_.t[:, :], y_.t[:, :], t_.t[:, :], ALU.mult, y_.all() + t_.all(), y_.all())
                cp(s, "pool", ygb.t[:, oc, :], y_.t[:, :], y_.all(), [ygb.r[oc]])

        def g1():
            for oc in range(2):
                for ic in range(2):
                    mm(s, pyg.t[:, hv_[oc]], wg.t[:, ic, oc * 128:(oc + 1) * 128], ygb.t[:, ic, :], ic == 0, ic == 1,
                       wg.all() + [ygb.r[ic]], pyg.all())

        def g2():
            for oc in range(2):
                act(s, sg[oc].t[:, :], pyg.t[:, hv_[oc]], AF.Sigmoid, pyg.all() + bg.all(), sg[oc].all(),
                    bias=bg.t[:, oc:oc + 1])

        def g3_():
            for oc in range(2):
                tt(s, "dve", ob[oc].t[:, :], yv[oc].t[:, :], sg[oc].t[:, :], ALU.mult,
                   yv[oc].all() + sg[oc].all(), ob[oc].all())
                r0 = NH * DV + 256 + oc * 128
                dma(s, dr["MIX"][r0:r0 + 128, sl], ob[oc].t[:, :], ob[oc].all(), ())
        return [nop] * 6 + [f1, f2, f3, f4, f5, f6, g1, g2, g3_]

    def ini_ops():
        tt(s, "dve", t_a.t[:, :], gl_re.t[:, :], ETr.t[:, :], ALU.mult, gl_re.all() + ETr.all(), t_a.all())
        tt(s, "dve", t_b.t[:, :], gl_im.t[:, :], ETi.t[:, :], ALU.mult, gl_im.all() + ETi.all(), t_b.all())
        tt(s, "dve", ini_re.t[:, :], t_a.t[:, :], t_b.t[:, :], ALU.subtract, t_a.all() + t_b.all(), ini_re.all())
        tt(s, "dve", t_a.t[:, :], gl_re.t[:, :], ETi.t[:, :], ALU.mult, gl_re.all() + ETi.all(), t_a.all())
        tt(s, "dve", t_b.t[:, :], gl_im.t[:, :], ETr.t[:, :], ALU.mult, gl_im.all() + ETr.all(), t_b.all())
        tt(s, "dve", ini_im.t[:, :], t_a.t[:, :], t_b.t[:, :], ALU.add, t_a.all() + t_b.all(), ini_im.all())

    memset(s, "pool", ini_re.t[:, :], 0.0, ini_re.all())
    memset(s, "pool", ini_im.t[:, :], 0.0, ini_im.all())
    items = [pre_item(0)]
    if NTILE > 1:
        items.append(pre_item(1))
    for j in range(NTILE):
        for kc in range(NK):
            items.append(kc_item(j, kc))
            if kc == 5 and j + 2 < NTILE:
                items.append(pre_item(j + 2))
        items.append(fin_item(j))
        if j + 1 < NTILE:
            items.append([nop] * 5 + [ini_ops])
            items.extend([[nop]] * 3)
    active = []
    it = iter(items)
    while True:
        nxt = next(it, None) if len(active) < S5_MAXACT else None
        if nxt is not None:
            active.append([nxt, 0])
        if not active:
            break
        for a_ in list(active):
            a_[0][a_[1]]()
            a_[1] += 1
            if a_[1] == len(a_[0]):
                active.remove(a_)
        yield


def stage_b3(nc, sync, dr, S, l):
    with ExitStack() as es:
        c = Ctx(nc, es)
        s = Sched(sync)
        for _ in gen_b3(s, c, dr, S, l, c.ps("pyr"), None, c.ps("pyg")):
            pass
        s.emit(f"b3_{l}")


def build(S=SEQ, nlayers=DEPTH, stages=None, debug_out=(), ext_in=()):
    nc = bass.Bass("TRN2", target_bir_lowering=False)
    dr = {}

    def din(name, shape, dtype=F32):
        dr[name] = nc.dram_tensor(name, list(shape), dtype, kind="ExternalInput").ap()

    def dscr(name, shape, dtype):
        kind = "Internal"
        if name in debug_out:
            kind = "ExternalOutput"
        if name in ext_in:
            kind = "ExternalInput"
        dr[name] = nc.dram_tensor(name, list(shape), dtype, kind=kind).ap()

    if stages is None:
        stages = ("ln_in", "a", "b13", "b2", "c1", "c2")

    def want(st):
        return st in stages

    din("xT", (D, S))
    din("ln_in_g", (128, 8))
    din("ln_in_b", (128, 8))
    for k_, shp in LAYER_W.items():
        din(k_, (DEPTH,) + shp)
    for k_, n in LAYER_V128.items():
        din(k_, (DEPTH, 128, n))
    dscr("H", (D, S), F32)
    dscr("Hb", (D, S), BF16)
    dscr("H1", (D, S), F32)
    dscr("H1b", (D, S), BF16)
    dscr("MIX", (D, S), BF16)
    din("rope_cos", (128, S))
    din("rope_sin", (128, S))
    dscr("QN", (NH, NOPE, S), BF16)
    dscr("QR", (NH * ROPE, S), BF16)
    dscr("KN", (NH, NOPE, S), BF16)
    dscr("KR", (ROPE, S), BF16)
    dscr("V", (S, NH * DV), BF16)
    dscr("HGQ", (256, S), F32)
    dscr("HGF", (256, S), F32)
    dscr("HGG", (256, S), F32)
    dscr("HGT", (S, 512), F32)
    dscr("SU", (256, S), F32)
    din("hg_ng", (DEPTH, 128, 1))
    for nm_ in ("s5_are_pp", "s5_aim_pp", "s5_ldt_pp"):
        din(nm_, (DEPTH, 128, 8))
    for nm_ in ("s5_bT_re", "s5_bT_im", "s5_cT_re", "s5_cT_im"):
        din(nm_, (DEPTH, 128, 8, 128))
    din("s5_d_fm", (DEPTH, 128, 2))
    din("s5_bglu_fm", (DEPTH, 128, 2))
    din("hg_lb_fm", (DEPTH, 128, 2))
    din("hg_lower_bound", (DEPTH, 256))
    dr["outT"] = nc.dram_tensor("outT", [D, S], F32, kind="ExternalOutput").ap()
    with ExitStack() as es:
        sync = Sync(nc, es)
        fuse0 = FUSE_LN_IN and want("ln_in") and want("a")
        if want("ln_in") and not fuse0:
            stage_ln_in(nc, sync, dr, S)
        for l in range(nlayers):
            if want("a"):
                stage_a(nc, sync, dr, S, l, fuse_ln_in=(fuse0 and l == 0))
            if want("b1"):
                stage_b1(nc, sync, dr, S, l, with_s5=False)
            if want("b2"):
                stage_b2(nc, sync, dr, S, l)
            if want("b3"):
                stage_b3(nc, sync, dr, S, l)
            if want("b13"):
                stage_b1(nc, sync, dr, S, l, with_s5=True)
            last = l == nlayers - 1
            if want("c1") and want("c2") and S // T >= 2:
                with ExitStack() as es2:
                    w1b = Ctx(nc, es2).sb("w1p", (128, 8, DFF), BF16)
                    stage_c1(nc, sync, dr, S, l, w1_pref=w1b)
                    stage_c2(nc, sync, dr, S, l, dr["outT"] if last else dr["H"],
                             None if last else dr["Hb"], w1_pref=w1b)
            else:
                if want("c1"):
                    stage_c1(nc, sync, dr, S, l)
                if want("c2"):
                    stage_c2(nc, sync, dr, S, l, dr["outT"] if last else dr["H"],
                             None if last else dr["Hb"])
    return nc


def _fmv(v):
    v = np.asarray(v, np.float32)
    return np.ascontiguousarray(v.reshape(v.shape[0], -1, 128).transpose(0, 2, 1))


def _rope_tables(S):
    freqs = (ROPE_THETA ** (-np.arange(0, ROPE, 2, dtype=np.float32) / ROPE)).astype(np.float32)
    ang = np.arange(S, dtype=np.float32)[:, None] * freqs[None, :]
    cos = np.cos(ang).astype(np.float32).T
    sin = np.sin(ang).astype(np.float32).T
    return (np.ascontiguousarray(np.concatenate([cos] * 4, 0)),
            np.ascontiguousarray(np.concatenate([sin] * 4, 0)))


def _s5_layouts(inp):
    L = inp["s5_a_re"].shape[0]
    out = {}

    def pp(a):
        a = np.asarray(a, np.float32)
        return np.ascontiguousarray(a.reshape(L, 8, 2, 64).transpose(0, 2, 3, 1).reshape(L, 128, 8))
    out["s5_are_pp"] = pp(inp["s5_a_re"])
    out["s5_aim_pp"] = pp(inp["s5_a_im"])
    out["s5_ldt_pp"] = pp(np.repeat(np.asarray(inp["s5_log_dt"], np.float32)[:, :, None], 64, axis=2))

    def bT(b):
        b = np.asarray(b, np.float32)
        o = np.zeros((L, 128, 8, 128), np.float32)
        for g in range(16):
            o[:, (g % 8) * 16:(g % 8) * 16 + 16, g // 2, (g % 2) * 64:(g % 2) * 64 + 64] = b[:, g].transpose(0, 2, 1)
        return o

    def cT(cc):
        cc = np.asarray(cc, np.float32)
        o = np.zeros((L, 128, 8, 128), np.float32)
        for g in range(16):
            o[:, (g % 2) * 64:(g % 2) * 64 + 64, g // 2, (g % 8) * 16:(g % 8) * 16 + 16] = cc[:, g].transpose(0, 2, 1)
        return o
    out["s5_bT_re"] = bT(inp["s5_b_re"])
    out["s5_bT_im"] = bT(inp["s5_b_im"])
    out["s5_cT_re"] = cT(inp["s5_c_re"])
    out["s5_cT_im"] = cT(inp["s5_c_im"])
    out["s5_d_fm"] = _fmv(inp["s5_d"])
    out["s5_bglu_fm"] = _fmv(inp["s5_b_glu"])
    return out


def make_shared_inputs(inp, S):
    cos, sin = _rope_tables(S)
    im = {"ln_in_g": _fmv(np.asarray(inp["ln_in_g"])[None])[0],
          "ln_in_b": _fmv(np.asarray(inp["ln_in_b"])[None])[0],
          "rope_cos": cos, "rope_sin": sin}
    for k in LAYER_W:
        im[k] = np.ascontiguousarray(np.asarray(inp[k], np.float32))
    for k in LAYER_V128:
        im[k] = _fmv(inp[k])
    im["hg_ng"] = np.ascontiguousarray(np.tile(np.asarray(inp["hg_norm_g"], np.float32), (1, 2))[:, :, None])
    im["hg_lb_fm"] = _fmv(inp["hg_lower_bound"])
    im["hg_lower_bound"] = np.ascontiguousarray(np.asarray(inp["hg_lower_bound"], np.float32))
    im.update(_s5_layouts(inp))
    return im


N_CORES = 8


def kernel(**inputs):
    x = np.asarray(inputs["x"], np.float32)
    B, S, _ = x.shape
    shared = make_shared_inputs(inputs, S)
    nc = build(S=S, nlayers=DEPTH)
    in_maps = []
    for core in range(N_CORES):
        m = dict(shared)
        m["xT"] = np.ascontiguousarray(x[core % B].T)
        in_maps.append(m)
    res = run_bass_kernel_spmd(nc, in_maps, core_ids=list(range(N_CORES)))
    out = np.stack([np.asarray(res.results[b]["outT"]).T for b in range(B)], 0)
    return np.ascontiguousarray(out.astype(np.float32))
```

```python
import math
from contextlib import ExitStack

import numpy as np
import ml_dtypes
import concourse.bass as bass
import concourse.mybir as mybir
from concourse.bass_utils import run_bass_kernel_spmd

F32 = mybir.dt.float32
BF16 = mybir.dt.bfloat16
I32 = mybir.dt.int32
AF = mybir.ActivationFunctionType
ALU = mybir.AluOpType

D = 1024
DEPTH = 2
SEQ = 8192
BATCH = 4
NH = 4
NOPE = 128
ROPE = 64
QK = 192
DV = 128
QR = 384
KVR = 256
HGH = 4
HDK = 64
HDV = 64
S5G = 16
S5P = 16
S5N = 64
DFF = 4096
DIN = 1984
ALPHA = (2 * DEPTH) ** 0.25
LN_EPS = 1e-5
RMS_EPS = 1e-6
ROPE_THETA = 10000.0
T = 512
OFF_CQ, OFF_CKV, OFF_KR, OFF_HQ, OFF_HF, OFF_HI, OFF_HG, OFF_SU = (
    0, 384, 640, 704, 960, 1216, 1472, 1728)


class Res:
    __slots__ = ("name", "last_w", "readers", "psum")

    def __init__(self, name, psum=False):
        self.name = name
        self.last_w = None
        self.readers = []
        self.psum = psum


class Op:
    __slots__ = ("eng", "fn", "deps", "dma", "flag", "token", "prewait", "idx")


class Sync:
    SEM_LIMIT = 30000

    def __init__(self, nc, es, n_dma_sems=20):
        self.nc = nc
        self.es = es
        self.eng_sem = {}
        self.eng_cnt = {}
        n_sw = 6
        self.dma_sems = [es.enter_context(nc.semaphore(f"dma{i}")) for i in range(n_dma_sems + n_sw)]
        self.dma_cnt = [0] * (n_dma_sems + n_sw)
        self.dma_pool = {"hw": list(range(n_dma_sems)), "sw": list(range(n_dma_sems, n_dma_sems + n_sw))}
        self.dma_rr = {"hw": 0, "sw": 0}
        self.waited = {e: {} for e in ("pe", "act", "dve", "pool", "sp")}
        self.nsem = 0

    def new_eng_sem(self, e):
        self.nsem += 1
        s = self.es.enter_context(self.nc.semaphore(f"s_{e}_{self.nsem}"))
        self.eng_sem[e] = s
        self.eng_cnt[e] = 0
        return s


class Sched:
    def __init__(self, sync):
        self.sync = sync
        self.nc = sync.nc
        self.ops = []

    def add(self, eng, fn, reads=(), writes=(), dma=False):
        op = Op()
        op.eng, op.fn, op.dma = eng, fn, dma
        op.flag = False
        op.token = None
        op.prewait = None
        op.idx = len(self.ops)
        deps = []
        xw = [r for r in reads if r.psum]
        if xw:
            writes = list(writes) + [r for r in xw if r not in writes]
        for r in reads:
            w = r.last_w
            if w is not None:
                if w.dma or w.eng != eng or dma or eng != "pe":
                    deps.append(w)
            r.readers.append(op)
        for wres in writes:
            w = wres.last_w
            if w is not None and (w.dma or dma or w.eng != eng or eng != "pe"):
                deps.append(w)
            for rd in wres.readers:
                if rd is op:
                    continue
                if rd.dma or dma or rd.eng != eng or eng != "pe":
                    deps.append(rd)
            wres.last_w = op
            wres.readers = []
        for d_ in deps:
            d_.flag = True
        op.deps = deps
        self.ops.append(op)
        return op

    def pe(self, fn, reads=(), writes=()):
        return self.add("pe", fn, reads, writes)

    def act(self, fn, reads=(), writes=()):
        return self.add("act", fn, reads, writes)

    def dve(self, fn, reads=(), writes=()):
        return self.add("dve", fn, reads, writes)

    def pool(self, fn, reads=(), writes=()):
        return self.add("pool", fn, reads, writes)

    def dma(self, fn, reads=(), writes=(), q="sp"):
        return self.add(q, fn, reads, writes, dma=True)

    def emit(self, name):
        sy = self.sync
        nc = self.nc
        ops = self.ops
        last = {}
        for op in ops:
            if not op.dma:
                last[op.eng] = op
        for op in last.values():
            op.flag = True
        for op in ops:
            if op.dma:
                kind = "sw" if op.eng == "pool" else "hw"
                pool_ = sy.dma_pool[kind]
                j = pool_[sy.dma_rr[kind] % len(pool_)]
                sy.dma_rr[kind] += 1
                op.prewait = (sy.dma_sems[j], sy.dma_cnt[j])
                sy.dma_cnt[j] += 16
                op.token = (sy.dma_sems[j], sy.dma_cnt[j])
            elif op.flag:
                e = op.eng
                if e not in sy.eng_sem or sy.eng_cnt[e] >= sy.SEM_LIMIT:
                    sy.new_eng_sem(e)
                sy.eng_cnt[e] += 1
                op.token = (sy.eng_sem[e], sy.eng_cnt[e])
        end_tokens = [op.token for op in last.values()]
        end_tokens += [(s, c) for s, c in zip(sy.dma_sems, sy.dma_cnt) if c > 0]

        def stream(e):
            def body(eng):
                waited = sy.waited[e]

                def wait(tok):
                    s, v = tok
                    if v <= 0:
                        return
                    key = id(s)
                    if waited.get(key, 0) >= v:
                        return
                    eng.wait_ge(s, v)
                    waited[key] = v

                for op in ops:
                    if op.eng != e:
                        continue
                    if op.dma:
                        wait(op.prewait)
                    need = {}
                    for d_ in op.deps:
                        s, v = d_.token
                        k = id(s)
                        if k not in need or need[k][1] < v:
                            need[k] = (s, v)
                    for tok in need.values():
                        wait(tok)
                    inst = op.fn(eng)
                    if op.dma:
                        inst.then_inc(op.token[0], 16)
                    elif op.flag:
                        inst.then_inc(op.token[0], 1)
                for tok in end_tokens:
                    wait(tok)
            return body

        with nc.Block(name) as block:
            block.tensor(stream("pe"))
            block.scalar(stream("act"))
            block.vector(stream("dve"))
            block.gpsimd(stream("pool"))
            block.sync(stream("sp"))
        self.ops = []


class Buf:
    def __init__(self, t, name, nchunk=1, psum=False):
        self.t = t
        self.r = [Res(f"{name}.{i}", psum) for i in range(nchunk)]

    def all(self):
        return list(self.r)


class Ctx:
    N = [0]

    def __init__(self, nc, es):
        self.nc = nc
        self.es = es

    def sb(self, name, shape, dtype, nchunk=1):
        Ctx.N[0] += 1
        t = self.es.enter_context(self.nc.sbuf_tensor(f"{name}_{Ctx.N[0]}", list(shape), dtype))
        return Buf(t, name, nchunk)

    def ps(self, name, shape=(128, 512), dtype=F32, nchunk=1):
        Ctx.N[0] += 1
        t = self.es.enter_context(self.nc.psum_tensor(f"{name}_{Ctx.N[0]}", list(shape), dtype))
        return Buf(t, name, nchunk, psum=True)


def mm(s, out_ap, lhsT, rhs, start, stop, reads, writes):
    def fn(eng):
        return eng.matmul(out_ap, lhsT, rhs, start=start, stop=stop)
    return s.pe(fn, reads, writes)


def act(s, out_ap, in_ap, func, reads, writes, bias=None, scale=None):
    def fn(eng):
        kw = {}
        if bias is not None:
            kw["bias"] = bias
        if scale is not None:
            kw["scale"] = scale
        return eng.activation(out_ap, in_ap, func, **kw)
    return s.act(fn, reads, writes)


def tt(s, e, out_ap, in0, in1, op, reads, writes):
    def fn(eng):
        return eng.tensor_tensor(out_ap, in0, in1, op)
    return s.add(e, fn, reads, writes)


def ts(s, e, out_ap, in0, s1, s2, op0, op1, reads, writes):
    def fn(eng):
        if op1 is None:
            return eng.tensor_scalar(out_ap, in0, s1, None, op0)
        return eng.tensor_scalar(out_ap, in0, s1, s2, op0, op1)
    return s.add(e, fn, reads, writes)


def stt(s, out_ap, in0, scalar, in1, op0, op1, reads, writes):
    def fn(eng):
        return eng.scalar_tensor_tensor(out_ap, in0, scalar, in1, op0, op1)
    return s.dve(fn, reads, writes)


def cp(s, e, out_ap, in_ap, reads, writes):
    if e == "act":
        def fn(eng):
            return eng.copy(out_ap, in_ap)
    else:
        def fn(eng):
            return eng.tensor_copy(out_ap, in_ap)
    return s.add(e, fn, reads, writes)


def dma(s, out_ap, in_ap, reads, writes, q="sp"):
    def fn(eng):
        return eng.dma_start(out_ap, in_ap)
    return s.dma(fn, reads, writes, q=q)


def memset(s, e, ap, val, writes):
    def fn(eng):
        return eng.memset(ap, val)
    return s.add(e, fn, (), writes)


def load_w_bf16(s, dst, src_view, nk, ncols, res=None):
    res = dst.all() if res is None else res
    for k in range(nk):
        for c0 in range(0, ncols, 2048):
            c1 = min(ncols, c0 + 2048)
            dma(s, dst.t[:, k, c0:c1], src_view[:, k, c0:c1], (), res, q="pool")


class LNState:
    pass


class LN:
    def __init__(self, s, z, gbuf, bbuf, out32, outb, ps1, ps2, tmp, slot=0):
        self.s, self.z, self.g, self.b = s, z, gbuf, bbuf
        self.out32, self.outb, self.ps1, self.ps2, self.tmp, self.slot = out32, outb, ps1, ps2, tmp, slot

    def stats(self, k):
        s, z, tmp, ps1, ps2 = self.s, self.z, self.tmp, self.ps1, self.ps2
        KC = D // 128
        onesb = tmp["onesb"]
        zb = tmp["zb"][k % 2]
        sq = tmp["sq"][k % 2]
        cp(s, "act", zb.t[:, :], z.t[:, k, :], [z.r[k]], zb.all())
        act(s, sq.t[:, :], z.t[:, k, :], AF.Square, [z.r[k]], sq.all())
        mm(s, ps1.t[:, :], onesb.t[:, :], zb.t[:, :], k == 0, k == KC - 1, zb.all() + onesb.all(), ps1.all())
        mm(s, ps2.t[:, :], onesb.t[:, :], sq.t[:, :], k == 0, k == KC - 1, sq.all() + onesb.all(), ps2.all())

    def rstd(self):
        s, tmp, ps1, ps2 = self.s, self.tmp, self.ps1, self.ps2
        mean, rstd, nm = tmp["mean"][self.slot], tmp["rstd"][self.slot], tmp["nm"][self.slot]
        act(s, mean.t[:, :], ps1.t[:, :], AF.Copy, ps1.all(), mean.all(), scale=1.0 / D)
        tt(s, "dve", nm.t[:, :], mean.t[:, :], mean.t[:, :], ALU.mult, mean.all(), nm.all())
        stt(s, rstd.t[:, :], ps2.t[:, :], 1.0 / D, nm.t[:, :], ALU.mult, ALU.subtract,
            ps2.all() + nm.all(), rstd.all())
        act(s, rstd.t[:, :], rstd.t[:, :], AF.Ln, rstd.all() + tmp["eps"].all(), rstd.all(),
            bias=tmp["eps"].t[:, 0:1])
        act(s, rstd.t[:, :], rstd.t[:, :], AF.Exp, rstd.all(), rstd.all(), scale=-0.5)
        stt(s, nm.t[:, :], mean.t[:, :], -1.0, rstd.t[:, :], ALU.mult, ALU.mult,
            mean.all() + rstd.all(), nm.all())

    def apply(self, k):
        s, z, tmp = self.s, self.z, self.tmp
        rstd, nm = tmp["rstd"][self.slot], tmp["nm"][self.slot]
        g_ap, b_ap = self.g.t, self.b.t
        gb = self.g.all() + self.b.all()
        t_ = tmp["t"][k % 2]
        tt(s, "dve", t_.t[:, :], z.t[:, k, :], rstd.t[:, :], ALU.mult, [z.r[k]] + rstd.all(), t_.all())
        tt(s, "pool", t_.t[:, :], t_.t[:, :], nm.t[:, :], ALU.add, t_.all() + nm.all(), t_.all())
        act(s, self.out32.t[:, k, :], t_.t[:, :], AF.Identity, t_.all() + gb, [self.out32.r[k]],
            bias=b_ap[:, k:k + 1], scale=g_ap[:, k:k + 1])
        ts(s, "dve", self.outb.t[:, k, :], t_.t[:, :], g_ap[:, k:k + 1], b_ap[:, k:k + 1], ALU.mult, ALU.add,
           t_.all() + gb, [self.outb.r[k]])


def emit_layernorm(s, c, z, gbuf, bbuf, out32, outb, onesb, ps1, ps2, tmp, slot=0):
    ln = LN(s, z, gbuf, bbuf, out32, outb, ps1, ps2, tmp, slot)
    for k in range(D // 128):
        ln.stats(k)
    ln.rstd()
    for k in range(D // 128):
        ln.apply(k)


def ln_tmp(c, s, nslot=2):
    tmp = {
        "sq": [c.sb("lnsq", (128, T), BF16) for _ in range(2)],
        "zb": [c.sb("lnzb", (128, T), BF16) for _ in range(2)],
        "t": [c.sb("lnt", (128, T), F32) for _ in range(2)],
        "mean": [c.sb("lnmean", (128, T), F32) for _ in range(nslot)],
        "rstd": [c.sb("lnrstd", (128, T), F32) for _ in range(nslot)],
        "nm": [c.sb("lnnm", (128, T), F32) for _ in range(nslot)],
        "eps": c.sb("lneps", (128, 1), F32),
        "onesb": c.sb("lnones", (128, 128), BF16),
    }
    memset(s, "pool", tmp["onesb"].t[:, :], 1.0, tmp["onesb"].all())
    memset(s, "pool", tmp["eps"].t[:, :], LN_EPS, tmp["eps"].all())
    return tmp


def fm(ap):
    return ap.rearrange("(k p) s -> p k s", p=128)


def stage_ln_in(nc, sync, dr, S):
    with ExitStack() as es:
        c = Ctx(nc, es)
        s = Sched(sync)
        NTILE = S // T
        zb = [c.sb("z", (128, 8, T), F32, 8) for _ in range(3)]
        ob = [c.sb("ob", (128, 8, T), BF16, 8) for _ in range(2)]
        g = c.sb("g", (128, 8), F32)
        b = c.sb("b", (128, 8), F32)
        ps1, ps2 = [c.ps("ps1") for _ in range(2)], [c.ps("ps2") for _ in range(2)]
        tmp = ln_tmp(c, s)
        dma(s, g.t[:, :], dr["ln_in_g"][:, :], (), g.all())
        dma(s, b.t[:, :], dr["ln_in_b"][:, :], (), b.all())
        xv, hv, hbv = fm(dr["xT"]), fm(dr["H"]), fm(dr["Hb"])

        def load(j):
            dma(s, zb[j % 3].t[:, :, :], xv[:, :, j * T:(j + 1) * T], (), zb[j % 3].all())

        def store(j):
            sl_ = slice(j * T, (j + 1) * T)
            dma(s, hv[:, :, sl_], zb[j % 3].t[:, :, :], zb[j % 3].all(), ())
            dma(s, hbv[:, :, sl_], ob[j % 2].t[:, :, :], ob[j % 2].all(), ())
        load(0)
        prev = None
        for j in range(NTILE):
            z, o = zb[j % 3], ob[j % 2]
            if j + 1 < NTILE:
                load(j + 1)
            ln = LN(s, z, g, b, z, o, ps1[j % 2], ps2[j % 2], tmp, j % 2)
            for k in range(8):
                ln.stats(k)
                if prev is not None:
                    prev.apply(k)
            if prev is not None:
                store(j - 1)
            ln.rstd()
            prev = ln
        for k in range(8):
            prev.apply(k)
        store(NTILE - 1)
        s.emit("ln_in")


def wv(ap):
    return ap.rearrange("(k p) n -> p k n", p=128)


def stage_c1(nc, sync, dr, S, l, w1_pref=None):
    with ExitStack() as es:
        c = Ctx(nc, es)
        s = Sched(sync)
        NTILE = S // T
        w = c.sb("wout", (128, 8, D), BF16)
        load_w_bf16(s, w, wv(dr["w_out"][l]), 8, D)
        g = c.sb("g", (128, 8), F32)
        b = c.sb("b", (128, 8), F32)
        dma(s, g.t[:, :], dr["ln1_g"][l], (), g.all())
        dma(s, b.t[:, :], dr["ln1_b"][l], (), b.all())
        tmp = ln_tmp(c, s)
        hb_ = [c.sb("h", (128, 8, T), F32, 8) for _ in range(3)]
        mb = [c.sb("mix", (128, 8, T), BF16, 8) for _ in range(3)]
        ob = [c.sb("ob", (128, 8, T), BF16, 8) for _ in range(2)]
        pm = [c.ps("pm") for _ in range(4)]
        ps1, ps2 = [c.ps("ps1") for _ in range(2)], [c.ps("ps2") for _ in range(2)]
        hv, mv, h1v, h1bv = fm(dr["H"]), fm(dr["MIX"]), fm(dr["H1"]), fm(dr["H1b"])

        def load(j):
            sl_ = slice(j * T, (j + 1) * T)
            dma(s, mb[j % 3].t[:, :, :], mv[:, :, sl_], (), mb[j % 3].all())
            dma(s, hb_[j % 3].t[:, :, :], hv[:, :, sl_], (), hb_[j % 3].all())

        def store(j):
            sl_ = slice(j * T, (j + 1) * T)
            dma(s, h1v[:, :, sl_], hb_[j % 3].t[:, :, :], hb_[j % 3].all(), ())
            dma(s, h1bv[:, :, sl_], ob[j % 2].t[:, :, :], ob[j % 2].all(), ())
        load(0)
        prev = None
        for j in range(NTILE):
            h, mx, o = hb_[j % 3], mb[j % 3], ob[j % 2]
            if j + 1 < NTILE:
                load(j + 1)
            if j == 1 and w1_pref is not None:
                load_w_bf16(s, w1_pref, wv(dr["w_ff1"][l]), 8, DFF)
            ln = LN(s, h, g, b, h, o, ps1[j % 2], ps2[j % 2], tmp, j % 2)
            for m in range(8):
                p = pm[m % 4]
                for k in range(8):
                    mm(s, p.t[:, :], w.t[:, k, m * 128:(m + 1) * 128], mx.t[:, k, :],
                       k == 0, k == 7, [w.r[0], mx.r[k]], p.all())
                stt(s, h.t[:, m, :], h.t[:, m, :], ALPHA, p.t[:, :], ALU.mult, ALU.add,
                    [h.r[m]] + p.all(), [h.r[m]])
                if m >= 2:
                    ln.stats(m - 2)
                if prev is not None:
                    prev.apply(m)
            if prev is not None:
                store(j - 1)
            ln.stats(6)
            ln.stats(7)
            ln.rstd()
            prev = ln
        for m in range(8):
            prev.apply(m)
        store(NTILE - 1)
        s.emit(f"c1_{l}")


def stage_c2(nc, sync, dr, S, l, out32, outb, w1_pref=None):
    with ExitStack() as es:
        c = Ctx(nc, es)
        s = Sched(sync)
        NTILE = S // T
        if w1_pref is not None:
            w1 = w1_pref
        else:
            w1 = c.sb("w1", (128, 8, DFF), BF16)
            load_w_bf16(s, w1, wv(dr["w_ff1"][l]), 8, DFF)
        w2 = c.sb("w2", (128, 32, D), BF16)
        load_w_bf16(s, w2, wv(dr["w_ff2"][l]), 32, D)
        g = c.sb("g", (128, 8), F32)
        b = c.sb("b", (128, 8), F32)
        dma(s, g.t[:, :], dr["ln2_g"][l], (), g.all())
        dma(s, b.t[:, :], dr["ln2_b"][l], (), b.all())
        tmp = ln_tmp(c, s, nslot=1)
        h = c.sb("h", (128, 8, T), F32, 8)
        hbb = [c.sb("hb", (128, 8, T), BF16, 8) for _ in range(2)]
        a = c.sb("a", (128, 32, T), BF16, 32)
        sq = tmp["zb"]
        pf = [c.ps("pf") for _ in range(4)]
        pz = [c.ps("pz") for _ in range(2)]
        ps1, ps2 = c.ps("ps1"), c.ps("ps2")
        h1v, h1bv = fm(dr["H1"]), fm(dr["H1b"])
        o32v = fm(out32)
        obv = fm(outb) if outb is not None else None

        def load_hb(j):
            dma(s, hbb[j % 2].t[:, :, :], h1bv[:, :, j * T:(j + 1) * T], (), hbb[j % 2].all())

        def load_h(j):
            dma(s, h.t[:, :, :], h1v[:, :, j * T:(j + 1) * T], (), h.all())

        def store(j):
            sl_ = slice(j * T, (j + 1) * T)
            dma(s, o32v[:, :, sl_], h.t[:, :, :], h.all(), ())
            if obv is not None:
                dma(s, obv[:, :, sl_], hbb[j % 2].t[:, :, :], hbb[j % 2].all(), ())
        load_hb(0)
        load_h(0)
        if NTILE > 1:
            load_hb(1)
        prev = None
        for j in range(NTILE):
            hbt = hbb[j % 2]
            for m in range(32):
                p = pf[m % 4]
                for k in range(8):
                    mm(s, p.t[:, :], w1.t[:, k, m * 128:(m + 1) * 128], hbt.t[:, k, :],
                       k == 0, k == 7, [w1.r[0], hbt.r[k]], p.all())
                q = sq[m % 2]
                act(s, q.t[:, :], p.t[:, :], AF.Square, p.all(), q.all())
                stt(s, a.t[:, m, :], p.t[:, :], 0.0, q.t[:, :], ALU.is_gt, ALU.mult,
                    p.all() + q.all(), [a.r[m]])
                if prev is not None and m % 4 == 3:
                    prev.apply(m // 4)
            if prev is not None:
                store(j - 1)
                load_h(j)
                if j + 1 < NTILE:
                    load_hb(j + 1)
            ln = LN(s, h, g, b, h, hbt, ps1, ps2, tmp, 0)
            for m in range(8):
                p = pz[m % 2]
                for k in range(32):
                    mm(s, p.t[:, :], w2.t[:, k, m * 128:(m + 1) * 128], a.t[:, k, :],
                       k == 0, k == 31, [w2.r[0], a.r[k]], p.all())
                stt(s, h.t[:, m, :], h.t[:, m, :], ALPHA, p.t[:, :], ALU.mult, ALU.add,
                    [h.r[m]] + p.all(), [h.r[m]])
                if m >= 1:
                    ln.stats(m - 1)
            ln.stats(7)
            ln.rstd()
            prev = ln
        for m in range(8):
            prev.apply(m)
        store(NTILE - 1)
        s.emit(f"c2_{l}")


LAYER_W = {
    "w_in": (D, DIN), "mla_w_uq": (QR, NH * QK), "mla_w_ukv": (KVR, NH * (NOPE + DV)),
    "w_out": (D, D), "w_ff1": (D, DFF), "w_ff2": (DFF, D), "s5_w_glu": (256, 256),
}
LAYER_V128 = {"ln1_g": 8, "ln1_b": 8, "ln2_g": 8, "ln2_b": 8,
              "mla_q_norm_g": 3, "mla_kv_norm_g": 2}


class Rot:
    def __init__(self, bufs):
        self.bufs = bufs
        self.i = 0

    def get(self):
        b = self.bufs[self.i % len(self.bufs)]
        self.i += 1
        return b


def stage_a(nc, sync, dr, S, l, fuse_ln_in=False):
    with ExitStack() as es:
        c = Ctx(nc, es)
        s = Sched(sync)
        NTILE = S // T
        win = c.sb("win", (128, 8, DIN), BF16)
        load_w_bf16(s, win, wv(dr["w_in"][l]), 8, DIN)
        wrotk = c.sb("wrotk", (128, 8, 64), BF16)
        for k in range(8):
            ts(s, "dve", wrotk.t[:, k, 0:32], win.t[:, k, OFF_KR + 32:OFF_KR + 64], -1.0, None,
               ALU.mult, None, win.all(), wrotk.all())
            cp(s, "dve", wrotk.t[:, k, 32:64], win.t[:, k, OFF_KR:OFF_KR + 32], win.all(), wrotk.all())
        gq = c.sb("gq", (128, 3), F32)
        gkv = c.sb("gkv", (128, 2), F32)
        dma(s, gq.t[:, :], dr["mla_q_norm_g"][l], (), gq.all())
        dma(s, gkv.t[:, :], dr["mla_kv_norm_g"][l], (), gkv.all())
        stq = c.sb("stq", (128, 3, NH * QK), F32)
        dma(s, stq.t[:, :, :], wv(dr["mla_w_uq"][l]), (), stq.all())
        wuq = c.sb("wuq", (128, 3, NH * QK), BF16)
        for k in range(3):
            ts(s, "dve", wuq.t[:, k, :], stq.t[:, k, :], gq.t[:, k:k + 1], None, ALU.mult, None,
               stq.all() + gq.all(), wuq.all())
        wqr = c.sb("wqr", (128, 3, 256), BF16)
        wqx = c.sb("wqx", (128, 3, 256), BF16)
        for k in range(3):
            for h in range(NH):
                b0 = h * QK + NOPE
                cp(s, "pool", wqr.t[:, k, h * 64:(h + 1) * 64], wuq.t[:, k, b0:b0 + 64],
                   wuq.all(), wqr.all())
                ts(s, "dve", wqx.t[:, k, h * 64:h * 64 + 32], wuq.t[:, k, b0 + 32:b0 + 64], -1.0, None,
                   ALU.mult, None, wuq.all(), wqx.all())
                cp(s, "dve", wqx.t[:, k, h * 64 + 32:h * 64 + 64], wuq.t[:, k, b0:b0 + 32],
                   wuq.all(), wqx.all())
        stkv = c.sb("stkv", (128, 2, NH * 256), F32)
        dma(s, stkv.t[:, :, :], wv(dr["mla_w_ukv"][l]), (), stkv.all())
        wukv = c.sb("wukv", (128, 2, NH * 256), BF16)
        for k in range(2):
            ts(s, "dve", wukv.t[:, k, :], stkv.t[:, k, :], gkv.t[:, k:k + 1], None, ALU.mult, None,
               stkv.all() + gkv.all(), wukv.all())
        wvv = c.sb("wvv", (128, 2, NH * DV), BF16)
        for k in range(2):
            for h in range(NH):
                cp(s, "pool", wvv.t[:, k, h * DV:(h + 1) * DV],
                   wukv.t[:, k, h * 256 + NOPE:h * 256 + 256], wukv.all(), wvv.all())
        ones32 = c.sb("ones16", (128, 128), BF16)
        memset(s, "pool", ones32.t[:, :], 1.0, ones32.all())
        epsr = c.sb("epsr", (128, 1), F32)
        memset(s, "pool", epsr.t[:, :], RMS_EPS, epsr.all())
        lnsc = c.sb("lnsc", (128, 1), F32)
        memset(s, "pool", lnsc.t[:, :], math.log(QK ** -0.5), lnsc.all())
        hbb = [c.sb("hb", (128, 8, T), BF16, 8) for _ in range(2)]
        cq = c.sb("cq", (128, 3, T), BF16, 3)
        ckv = c.sb("ckv", (128, 2, T), BF16, 2)
        sqb = [c.sb("sq", (128, T), BF16) for _ in range(3)]
        rq = c.sb("rq", (128, T), F32)
        rkv = c.sb("rkv", (128, T), F32)
        rcol = c.sb("rcol", (128, 8), F32)
        cosb = [c.sb("cos", (128, T), F32) for _ in range(2)]
        sinb = [c.sb("sin", (128, T), F32) for _ in range(2)]
        cs = c.sb("cs", (128, T), F32)
        sn = c.sb("sn", (128, T), F32)
        t1 = Rot([c.sb("t1", (128, T), F32) for _ in range(2)])
        t2 = Rot([c.sb("t2", (128, T), F32) for _ in range(2)])
        o16 = Rot([c.sb("o16", (128, T), BF16) for _ in range(6)])
        o32 = Rot([c.sb("o32", (128, T), F32) for _ in range(6)])
        pp = Rot([c.ps("pp") for _ in range(4 if fuse_ln_in else 6)])
        pssq = c.ps("pssq")
        pcol = c.ps("pcol", (128, 8))
        hbv = fm(dr["Hb"])
        evac_i = [0]

        pending = []

        def tick():
            if pending:
                pending.pop(0)()

        def proj(p, pairs, M=128):
            n = len(pairs)
            for i, (lh, rh, rd) in enumerate(pairs):
                mm(s, p.t[0:M, :], lh, rh, i == 0, i == n - 1, rd, p.all())
            tick()

        if fuse_ln_in:
            zx = [c.sb("zx", (128, 8, T), F32, 8) for _ in range(2)]
            g_in = c.sb("g_in", (128, 8), F32)
            b_in = c.sb("b_in", (128, 8), F32)
            dma(s, g_in.t[:, :], dr["ln_in_g"][:, :], (), g_in.all())
            dma(s, b_in.t[:, :], dr["ln_in_b"][:, :], (), b_in.all())
            lntmp = ln_tmp(c, s, nslot=1)
            lps1, lps2 = c.ps("lps1"), c.ps("lps2")
            xv_, hv_ = fm(dr["xT"]), fm(dr["H"])

            def ln_pieces(j):
                z = zx[j % 2]
                ln = LN(s, z, g_in, b_in, z, hbb[j % 2], lps1, lps2, lntmp, 0)
                pcs = [(lambda k=k: ln.stats(k)) for k in range(8)] + [ln.rstd]
                pcs += [(lambda k=k: ln.apply(k)) for k in range(8)]
                pcs.append(lambda: dma(s, hv_[:, :, j * T:(j + 1) * T], z.t[:, :, :], z.all(), ()))
                return pcs

            def load_x(j):
                dma(s, zx[j % 2].t[:, :, :], xv_[:, :, j * T:(j + 1) * T], (), zx[j % 2].all())

        def load(j):
            sl_ = slice(j * T, (j + 1) * T)
            if not fuse_ln_in:
                dma(s, hbb[j % 2].t[:, :, :], hbv[:, :, sl_], (), hbb[j % 2].all())
            dma(s, cosb[j % 2].t[:, :], dr["rope_cos"][:, sl_], (), cosb[j % 2].all())
            dma(s, sinb[j % 2].t[:, :], dr["rope_sin"][:, sl_], (), sinb[j % 2].all())
        load(0)
        if fuse_ln_in:
            load_x(0)
            for f_ in ln_pieces(0):
                f_()
            if NTILE > 1:
                load_x(1)
        for j in range(NTILE):
            hb = hbb[j % 2]
            cos, sin = cosb[j % 2], sinb[j % 2]
            sl = slice(j * T, (j + 1) * T)
            if j + 1 < NTILE:
                load(j + 1)
                if fuse_ln_in:
                    pending.extend(ln_pieces(j + 1))

            def hin(c0, c1):
                return [(win.t[:, k, c0:c1], hb.t[:, k, :], [win.r[0], hb.r[k]]) for k in range(8)]

            for m in range(3):
                p = pp.get()
                proj(p, hin(OFF_CQ + m * 128, OFF_CQ + (m + 1) * 128))
                cp(s, "act", cq.t[:, m, :], p.t[:, :], p.all(), [cq.r[m]])
                q_ = sqb[m]
                act(s, q_.t[:, :], p.t[:, :], AF.Square, p.all(), q_.all())
                mm(s, pssq.t[:, :], ones32.t[:, :], q_.t[:, :], m == 0, m == 2,
                   q_.all() + ones32.all(), pssq.all())
            act(s, rq.t[:, :], pssq.t[:, :], AF.Ln, pssq.all() + epsr.all(), rq.all(), bias=epsr.t[:, 0:1],
                scale=1.0 / QR)
            act(s, rq.t[:, :], rq.t[:, :], AF.Exp, rq.all() + lnsc.all(), rq.all(), bias=lnsc.t[:, 0:1], scale=-0.5)
            tt(s, "dve", cs.t[:, :], cos.t[:, :], rq.t[:, :], ALU.mult, cos.all() + rq.all(), cs.all())
            tt(s, "pool", sn.t[:, :], sin.t[:, :], rq.t[:, :], ALU.mult, sin.all() + rq.all(), sn.all())

            def cqin(w_, c0, c1):
                return [(w_.t[:, k, c0:c1], cq.t[:, k, :], [w_.r[0], cq.r[k]]) for k in range(3)]

            for h in range(NH):
                p = pp.get()
                proj(p, cqin(wuq, h * QK, h * QK + NOPE))
                o = o16.get()
                tt(s, "dve", o.t[:, :], p.t[:, :], rq.t[:, :], ALU.mult, p.all() + rq.all(), o.all())
                dma(s, dr["QN"][h, :, sl], o.t[:, :], o.all(), ())
            for pr in range(2):
                pa, pb = pp.get(), pp.get()
                proj(pa, cqin(wqr, pr * 128, (pr + 1) * 128))
                proj(pb, cqin(wqx, pr * 128, (pr + 1) * 128))
                a_, b_ = t1.get(), t2.get()
                tt(s, "dve", a_.t[:, :], pa.t[:, :], cs.t[:, :], ALU.mult, pa.all() + cs.all(), a_.all())
                tt(s, "dve", b_.t[:, :], pb.t[:, :], sn.t[:, :], ALU.mult, pb.all() + sn.all(), b_.all())
                o = o16.get()
                tt(s, "pool", o.t[:, :], a_.t[:, :], b_.t[:, :], ALU.add, a_.all() + b_.all(), o.all())
                dma(s, dr["QR"][pr * 128:(pr + 1) * 128, sl], o.t[:, :], o.all(), ())
            for m in range(2):
                p = pp.get()
                proj(p, hin(OFF_CKV + m * 128, OFF_CKV + (m + 1) * 128))
                cp(s, "act", ckv.t[:, m, :], p.t[:, :], p.all(), [ckv.r[m]])
                q_ = sqb[m]
                act(s, q_.t[:, :], p.t[:, :], AF.Square, p.all(), q_.all())
                mm(s, pssq.t[:, :], ones32.t[:, :], q_.t[:, :], m == 0, m == 1,
                   q_.all() + ones32.all(), pssq.all())
            for sub in range(4):
                for m in range(2):
                    mm(s, pcol.t[:, 2 * sub:2 * sub + 2], sqb[m].t[:, sub * 128:(sub + 1) * 128],
                       ones32.t[:, 0:2], m == 0, m == 1, sqb[m].all() + ones32.all(), pcol.all())
            act(s, rkv.t[:, :], pssq.t[:, :], AF.Ln, pssq.all() + epsr.all(), rkv.all(), bias=epsr.t[:, 0:1],
                scale=1.0 / KVR)
            act(s, rkv.t[:, :], rkv.t[:, :], AF.Exp, rkv.all(), rkv.all(), scale=-0.5)
            act(s, rcol.t[:, :], pcol.t[:, :], AF.Ln, pcol.all() + epsr.all(), rcol.all(), bias=epsr.t[:, 0:1],
                scale=1.0 / KVR)
            act(s, rcol.t[:, :], rcol.t[:, :], AF.Exp, rcol.all(), rcol.all(), scale=-0.5)

            def kvin(w_, c0, c1):
                return [(w_.t[:, k, c0:c1], ckv.t[:, k, :], [w_.r[0], ckv.r[k]]) for k in range(2)]

            for h in range(NH):
                p = pp.get()
                proj(p, kvin(wukv, h * 256, h * 256 + NOPE))
                o = o16.get()
                tt(s, "dve", o.t[:, :], p.t[:, :], rkv.t[:, :], ALU.mult, p.all() + rkv.all(), o.all())
                dma(s, dr["KN"][h, :, sl], o.t[:, :], o.all(), ())
            for sub in range(4):
                p = pp.get()
                proj(p, [(ckv.t[:, k, sub * 128:(sub + 1) * 128], wvv.t[:, k, :], [wvv.r[0], ckv.r[k]])
                         for k in range(2)])
                o = o16.get()
                act(s, o.t[:, :], p.t[:, :], AF.Copy, p.all() + rcol.all(), o.all(),
                    scale=rcol.t[:, 2 * sub:2 * sub + 1])
                t0 = j * T + sub * 128
                dma(s, dr["V"][t0:t0 + 128, :], o.t[:, :], o.all(), ())
            pa, pb = pp.get(), pp.get()
            proj(pa, hin(OFF_KR, OFF_KR + 64), M=64)
            proj(pb, [(wrotk.t[:, k, :], hb.t[:, k, :], [wrotk.r[0], hb.r[k]]) for k in range(8)], M=64)
            a_, b_ = t1.get(), t2.get()
            tt(s, "dve", a_.t[0:64, :], pa.t[0:64, :], cos.t[0:64, :], ALU.mult, pa.all() + cos.all(), a_.all())
            tt(s, "dve", b_.t[0:64, :], pb.t[0:64, :], sin.t[0:64, :], ALU.mult, pb.all() + sin.all(), b_.all())
            o = o16.get()
            tt(s, "pool", o.t[0:64, :], a_.t[0:64, :], b_.t[0:64, :], ALU.add, a_.all() + b_.all(), o.all())
            dma(s, dr["KR"][:, sl], o.t[0:64, :], o.all(), ())
            for name, off in (("HGQ", OFF_HQ), ("HGF", OFF_HF), ("HGG", OFF_HG), ("SU", OFF_SU)):
                for m in range(2):
                    p = pp.get()
                    proj(p, hin(off + m * 128, off + (m + 1) * 128))
                    o = o32.get()
                    evac_i[0] += 1
                    cp(s, "act" if evac_i[0] % 2 else "dve", o.t[:, :], p.t[:, :], p.all(), o.all())
                    dma(s, dr[name][m * 128:(m + 1) * 128, sl], o.t[:, :], o.all(), ())
            for sub in range(4):
                p = pp.get()
                proj(p, [(hb.t[:, k, sub * 128:(sub + 1) * 128], win.t[:, k, OFF_HF:OFF_HF + 512],
                          [win.r[0], hb.r[k]]) for k in range(8)])
                o = o32.get()
                evac_i[0] += 1
                cp(s, "act" if evac_i[0] % 2 else "dve", o.t[:, :], p.t[:, :], p.all(), o.all())
                t0 = j * T + sub * 128
                dma(s, dr["HGT"][t0:t0 + 128, :], o.t[:, :], o.all(), ())
            while pending:
                tick()
            if fuse_ln_in and j + 2 < NTILE:
                load_x(j + 2)
        s.emit(f"a_{l}")


def stage_b1(nc, sync, dr, S, l, with_s5=True):
    with ExitStack() as es:
        c = Ctx(nc, es)
        s = Sched(sync)
        NTILE = S // T
        NCH = S // 128
        vb_ = [c.sb("vh", (128, NCH, DV), BF16) for _ in range(2)]
        knb = [c.sb("kn", (128, S), BF16) for _ in range(2)]
        kr = c.sb("kr", (128, S), BF16)
        vview = dr["V"].rearrange("(c p) (h d) -> h p c d", p=128, h=NH)

        def load_head(h):
            dma(s, knb[h % 2].t[:, :], dr["KN"][h], (), knb[h % 2].all())
            for c0 in range(0, NCH, 16):
                c1 = min(NCH, c0 + 16)
                dma(s, vb_[h % 2].t[:, c0:c1, :], vview[h][:, c0:c1, :], (), vb_[h % 2].all())
        memset(s, "pool", kr.t[64:128, :], 0.0, kr.all())
        dma(s, kr.t[0:64, :], dr["KR"][:, :], (), kr.all())
        load_head(0)
        onesb = c.sb("onesb", (128, 128), BF16)
        memset(s, "pool", onesb.t[:, :], 1.0, onesb.all())
        onesm = c.sb("onesm", (128, T), BF16)
        memset(s, "pool", onesm.t[:, :], 1.0, onesm.all())
        masks = []
        for d_ in range(4):
            m_ = c.sb("mask", (128, T), BF16)

            def fn(eng, m_=m_, d_=d_):
                return eng.affine_select(m_.t[:, :], onesm.t[:, :], pattern=[[1, T]],
                                         compare_op=ALU.is_ge, fill=0.0, base=-128 * d_,
                                         channel_multiplier=-1)
            s.pool(fn, onesm.all(), m_.all())
            ts(s, "dve", m_.t[:, :], m_.t[:, :], 30000.0, -30000.0, ALU.mult, ALU.add, m_.all(), m_.all())
            masks.append(m_)
        ident = c.sb("ident", (128, 128), BF16)

        def fid(eng):
            return eng.affine_select(ident.t[:, :], onesb.t[:, :], pattern=[[-1, 128]], compare_op=ALU.is_equal,
                                     fill=0.0, base=0, channel_multiplier=1)
        s.pool(fid, onesb.all(), ident.all())
        qnb = [c.sb("qn", (128, T), BF16) for _ in range(2)]
        qrb = [c.sb("qr", (128, T), BF16) for _ in range(2)]
        for q_ in qrb:
            memset(s, "pool", q_.t[64:128, :], 0.0, q_.all())
        pss = Rot([c.ps("pss") for _ in range(4)])
        pso = c.ps("pso")
        psl = c.ps("psl")
        ptb = Rot([c.sb("pt", (128, T), BF16) for _ in range(4)])
        rl = c.sb("rl", (128, T), F32)
        ob = Rot([c.sb("ob", (128, T), BF16) for _ in range(2)])
        g3 = gen_b3(s, c, dr, S, l, c.ps("pyr"), None, c.ps("pyg")) if with_s5 else iter(())
        g3_alive = [True]

        def step_s5():
            if g3_alive[0]:
                try:
                    next(g3)
                except StopIteration:
                    g3_alive[0] = False
        blocks = []
        for h in range(NH):
            for i in range(NTILE):
                nb = 4 * i + 4
                for cc in range(nb):
                    blocks.append((h, i, cc, nb))
        state = {}
        tiles = [(h, i) for h in range(NH) for i in range(NTILE)]

        def load_q(tix):
            h, i = tiles[tix]
            sl = slice(i * T, (i + 1) * T)
            dma(s, qnb[tix % 2].t[:, :], dr["QN"][h, :, sl], (), qnb[tix % 2].all())
            dma(s, qrb[tix % 2].t[0:64, :], dr["QR"][h * 64:(h + 1) * 64, sl], (), qrb[tix % 2].all())
        load_q(0)

        def issue_s(n):
            h, i, cc, nb = blocks[n]
            kn = knb[h % 2]
            tix = h * NTILE + i
            qn, qr = qnb[tix % 2], qrb[tix % 2]
            if cc == 0:
                if tix + 1 < len(tiles):
                    load_q(tix + 1)
            p = pss.get()
            ks = slice(cc * 128, (cc + 1) * 128)
            diag = cc >= 4 * i
            qs = slice(128 * (cc - 4 * i), T) if diag else slice(0, T)
            mm(s, p.t[:, qs], kn.t[:, ks], qn.t[:, qs], True, False, kn.all() + qn.all(), p.all())
            mm(s, p.t[:, qs], kr.t[:, ks], qr.t[:, qs], False, not diag, kr.all() + qr.all(), p.all())
            if diag:
                mk = masks[cc - 4 * i]
                mm(s, p.t[:, qs], ident.t[:, :], mk.t[:, qs], False, True, ident.all() + mk.all(), p.all())
            state[n] = (p, qs)

        def issue_pv(n):
            h, i, cc, nb = blocks[n]
            if i == 0 and cc == 0 and h + 1 < NH:
                load_head(h + 1)
            p, qs = state.pop(n)
            pt = ptb.get()
            act(s, pt.t[:, qs], p.t[:, qs], AF.Exp, p.all(), pt.all())
            vh = vb_[h % 2]
            mm(s, pso.t[:, qs], vh.t[:, cc, :], pt.t[:, qs], cc == 0, cc == nb - 1,
               vh.all() + pt.all(), pso.all())
            mm(s, psl.t[:, qs], onesb.t[:, :], pt.t[:, qs], cc == 0, cc == nb - 1,
               onesb.all() + pt.all(), psl.all())
            if cc == nb - 1:
                act(s, rl.t[:, :], psl.t[:, :], AF.Ln, psl.all(), rl.all())
                act(s, rl.t[:, :], rl.t[:, :], AF.Exp, rl.all(), rl.all(), scale=-1.0)
                o = ob.get()
                tt(s, "dve", o.t[:, :], pso.t[:, :], rl.t[:, :], ALU.mult, pso.all() + rl.all(), o.all())
                dma(s, dr["MIX"][h * DV:(h + 1) * DV, i * T:(i + 1) * T], o.t[:, :], o.all(), ())

        NB = len(blocks)
        LOOK = 3
        n_s5 = (S // TS5) * 13 + 24
        pace = max(1, -(-NB // n_s5)) if B13_PACE is None else B13_PACE
        for n in range(min(LOOK, NB)):
            issue_s(n)
        for n in range(NB):
            issue_pv(n)
            if n + LOOK < NB:
                issue_s(n + LOOK)
            if n % pace == 0:
                step_s5()
        while g3_alive[0]:
            step_s5()
        s.emit(f"b1_{l}")


def build_masks(s, c):
    ones = c.sb("mones", (128, 128), F32)
    memset(s, "pool", ones.t[:, :], 1.0, ones.all())
    U = c.sb("U", (128, 128), F32)
    Lm = c.sb("Lm", (128, 128), F32)
    BD = c.sb("BD", (128, 128), F32)

    def fu(eng):
        return eng.affine_select(U.t[:, :], ones.t[:, :], pattern=[[1, 128]], compare_op=ALU.is_ge,
                                 fill=0.0, base=0, channel_multiplier=-1)
    s.pool(fu, ones.all(), U.all())
    memset(s, "pool", U.t[0:64, 64:128], 0.0, U.all())

    def fl(eng):
        return eng.affine_select(Lm.t[:, :], ones.t[:, :], pattern=[[-1, 128]], compare_op=ALU.is_gt,
                                 fill=0.0, base=0, channel_multiplier=1)
    s.pool(fl, ones.all(), Lm.all())
    memset(s, "pool", Lm.t[64:128, 0:64], 0.0, Lm.all())
    memset(s, "pool", BD.t[:, :], 1.0, BD.all())
    memset(s, "pool", BD.t[0:64, 64:128], 0.0, BD.all())
    memset(s, "pool", BD.t[64:128, 0:64], 0.0, BD.all())
    return U, Lm, BD


def stage_b2(nc, sync, dr, S, l):
    with ExitStack() as es:
        c = Ctx(nc, es)
        s = Sched(sync)
        NTILE = S // T
        U, Lm, BD = build_masks(s, c)
        Ub = c.sb("Ub", (128, 128), BF16)
        cp(s, "pool", Ub.t[:, :], U.t[:, :], U.all(), Ub.all())
        ng = c.sb("ng", (128, 1), F32)
        dma(s, ng.t[:, :], dr["hg_ng"][l], (), ng.all())
        eps = c.sb("eps", (128, 1), F32)
        memset(s, "pool", eps.t[:, :], RMS_EPS, eps.all())
        use_lb = l > 0
        if use_lb:
            assert l == 1 and DEPTH == 2
            lbf = c.sb("lbf", (128, 2, 2), F32)
            dma(s, lbf.t[:, 0, :], dr["hg_lb_fm"][0], (), lbf.all())
            dma(s, lbf.t[:, 1, :], dr["hg_lb_fm"][1], (), lbf.all())
            lb = c.sb("lb", (128, 2), F32)
            oml = c.sb("oml", (128, 2), F32)
            tt(s, "dve", lb.t[:, :], lbf.t[:, 1, :], lbf.t[:, 0, :], ALU.subtract, lbf.all(), lb.all())
            act(s, lb.t[:, :], lb.t[:, :], AF.Sigmoid, lb.all(), lb.all())
            ts(s, "dve", oml.t[:, :], lb.t[:, :], -1.0, 1.0, ALU.mult, ALU.add, lb.all(), oml.all())
            lbr2 = c.sb("lbr2", (128, 2, 256), F32)
            dma(s, lbr2.t[:, 0, :], dr["hg_lower_bound"][0:1, :].partition_broadcast(128), (), lbr2.all())
            dma(s, lbr2.t[:, 1, :], dr["hg_lower_bound"][1:2, :].partition_broadcast(128), (), lbr2.all())
            lbrow = c.sb("lbrow", (128, 4, 256), F32)
            omlrow = c.sb("omlrow", (128, 4, 256), F32)
            tt(s, "dve", lbrow.t[:, 0, :], lbr2.t[:, 1, :], lbr2.t[:, 0, :], ALU.subtract, lbr2.all(), lbrow.all())
            act(s, lbrow.t[:, 0, :], lbrow.t[:, 0, :], AF.Sigmoid, lbrow.all(), lbrow.all())
            for u in range(1, 4):
                cp(s, "dve", lbrow.t[:, u, :], lbrow.t[:, 0, :], lbrow.all(), lbrow.all())
            ts(s, "dve", omlrow.t[:, :, :], lbrow.t[:, :, :], -1.0, 1.0, ALU.mult, ALU.add,
               lbrow.all(), omlrow.all())
        S32 = [c.sb("S32", (128, 128), F32) for _ in range(2)]
        Sb = [c.sb("Sb", (128, 128), BF16) for _ in range(2)]
        for g in range(2):
            memset(s, "pool", S32[g].t[:, :], 0.0, S32[g].all())
            memset(s, "pool", Sb[g].t[:, :], 0.0, Sb[g].all())
        vpad = [c.sb("vpad", (128, 4, 128), BF16) for _ in range(NH)]
        for h in range(NH):
            memset(s, "pool", vpad[h].t[:, :, :], 0.0, vpad[h].all())
        def two(f):
            return [f(), f()]
        vpad2 = two(lambda: [c.sb("vpad", (128, 4, 128), BF16) for _ in range(NH)])
        for sl_ in range(2):
            for h in range(NH):
                memset(s, "pool", vpad2[sl_][h].t[:, :, :], 0.0, vpad2[sl_][h].all())
        qf2 = two(lambda: c.sb("qf", (128, 2, T), F32, 2))
        zf2 = two(lambda: c.sb("zf", (128, 2, T), F32, 2))
        gf2 = two(lambda: c.sb("gf", (128, 2, T), F32, 2))
        tm2 = two(lambda: c.sb("tm", (128, 4, 512), F32))
        ft2 = two(lambda: c.sb("ft", (128, 4, 256), F32))
        lft2 = two(lambda: c.sb("lft", (128, 4, 256), F32))
        ecs2 = two(lambda: c.sb("ecs", (128, 4, 256), F32))
        khat2 = two(lambda: c.sb("khat", (128, 4, 256), BF16))
        vb2 = two(lambda: c.sb("vb", (128, 4, 256), BF16))
        eb2 = two(lambda: [c.sb("eb", (128, T), F32) for _ in range(2)])
        enb2 = two(lambda: [c.sb("enb", (128, T), F32) for _ in range(2)])
        qt2 = two(lambda: [c.sb("qt", (128, T), BF16) for _ in range(2)])
        kt2 = two(lambda: [c.sb("kt", (128, T), BF16) for _ in range(2)])
        attb = Rot([c.sb("attb", (128, 128), BF16) for _ in range(3)])
        o32 = [c.sb("o32", (128, T), F32) for _ in range(2)]
        sq = [c.sb("sq", (128, T), F32) for _ in range(2)]
        rs = [c.sb("rs", (128, T), F32) for _ in range(2)]
        ob = [c.sb("ob", (128, T), BF16) for _ in range(2)]
        pb = [c.ps("pb") for _ in range(2)]
        pc = c.ps("pc", (128, 1024))
        po = [c.ps("po") for _ in range(2)]
        patt = c.ps("patt", (128, 128))
        pu = c.ps("pu", (128, 128))
        tmv = dr["HGT"].rearrange("(j u p) n -> j p u n", p=128, u=4)

        def load(j):
            sl = slice(j * T, (j + 1) * T)
            k_ = j % 2
            dma(s, qf2[k_].t[:, :, :], fm(dr["HGQ"])[:, :, sl], (), qf2[k_].all())
            dma(s, zf2[k_].t[:, :, :], fm(dr["HGF"])[:, :, sl], (), zf2[k_].all())
            dma(s, gf2[k_].t[:, :, :], fm(dr["HGG"])[:, :, sl], (), gf2[k_].all())
            dma(s, tm2[k_].t[:, :, :], tmv[j], (), tm2[k_].all())

        def prep_steps(j):
            k_ = j % 2
            qf, zf, gf, tm, ft, lft, ecs, khat, vb = (qf2[k_], zf2[k_], gf2[k_], tm2[k_], ft2[k_], lft2[k_],
                                                      ecs2[k_], khat2[k_], vb2[k_])
            eb, enb, qt, kt, vpad = eb2[k_], enb2[k_], qt2[k_], kt2[k_], vpad2[k_]
            st = []

            def a1():
                act(s, zf.t[:, :, :], zf.t[:, :, :], AF.Sigmoid, zf.all(), zf.all())
                act(s, ft.t[:, :, :], tm.t[:, :, 0:256], AF.Sigmoid, tm.all(), ft.all())
                act(s, gf.t[:, :, :], gf.t[:, :, :], AF.Silu, gf.all(), gf.all())
            st.append(a1)

            def a2():
                for g in range(2):
                    if use_lb:
                        ts(s, "dve", zf.t[:, g, :], zf.t[:, g, :], oml.t[:, g:g + 1], lb.t[:, g:g + 1],
                           ALU.mult, ALU.add, [zf.r[g]] + oml.all() + lb.all(), [zf.r[g]])
                    ts(s, "dve", zf.t[:, g, :], zf.t[:, g, :], -1.0, 1.0, ALU.mult, ALU.add, [zf.r[g]], [zf.r[g]])
                if use_lb:
                    tt(s, "dve", ft.t[:, :, :], ft.t[:, :, :], omlrow.t[:, :, :], ALU.mult, ft.all() + omlrow.all(), ft.all())
                    tt(s, "dve", ft.t[:, :, :], ft.t[:, :, :], lbrow.t[:, :, :], ALU.add, ft.all() + lbrow.all(), ft.all())
            st.append(a2)

            def a3():
                act(s, lft.t[:, :, :], ft.t[:, :, :], AF.Ln, ft.all(), lft.all())
                cp(s, "pool", vb.t[:, :, :], tm.t[:, :, 256:512], tm.all(), vb.all())
            st.append(a3)

            def a4():
                ts(s, "dve", ft.t[:, :, :], ft.t[:, :, :], -1.0, 1.0, ALU.mult, ALU.add, ft.all(), ft.all())
                for h in range(NH):
                    cp(s, "pool", vpad[h].t[:, :, (h % 2) * 64:(h % 2) * 64 + 64], vb.t[:, :, h * 64:(h + 1) * 64],
                       vb.all(), vpad[h].all())
            st.append(a4)

            def a5():
                for g in range(2):
                    for u in range(4):
                        mm(s, pb[g].t[:, u * 128:(u + 1) * 128], lft.t[:, u, g * 128:(g + 1) * 128], U.t[:, :],
                           u == 0, u == 3, lft.all() + U.all(), pb[g].all())
            st.append(a5)

            def a6():
                for u in range(4):
                    mm(s, pc.t[:, u * 256:(u + 1) * 256], Lm.t[:, :], lft.t[:, u, :], u % 2 == 0, u % 2 == 1,
                       lft.all() + Lm.all(), pc.all())
            st.append(a6)

            def a7():
                for g in range(2):
                    act(s, eb[g].t[:, :], pb[g].t[:, :], AF.Exp, pb[g].all(), eb[g].all())
                    act(s, enb[g].t[:, :], pb[g].t[:, :], AF.Exp, pb[g].all(), enb[g].all(), scale=-1.0)
                act(s, ecs.t[:, :, :], pc.t[:, :], AF.Exp, pc.all(), ecs.all())
            st.append(a7)

            def a8():
                for g in range(2):
                    tt(s, "dve", qt[g].t[:, :], qf.t[:, g, :], eb[g].t[:, :], ALU.mult, [qf.r[g]] + eb[g].all(), qt[g].all())
                    tt(s, "dve", kt[g].t[:, :], zf.t[:, g, :], enb[g].t[:, :], ALU.mult, [zf.r[g]] + enb[g].all(), kt[g].all())
                tt(s, "dve", khat.t[:, :, :], ft.t[:, :, :], ecs.t[:, :, :], ALU.mult, ft.all() + ecs.all(), khat.all())
            st.append(a8)
            return st

        def rec_steps(j):
            k_ = j % 2
            gf, khat, vb = gf2[k_], khat2[k_], vb2[k_]
            eb, qt, kt, vpad = eb2[k_], qt2[k_], kt2[k_], vpad2[k_]
            sl = slice(j * T, (j + 1) * T)
            st = []
            first = [True, True]
            for u in range(4):
                us = slice(u * 128, (u + 1) * 128)

                def r_att(u=u, us=us):
                    for h in range(NH):
                        g, hp = h // 2, (h % 2) * 64
                        mm(s, patt.t[:, :], kt[g].t[hp:hp + 64, us], qt[g].t[hp:hp + 64, us], True, True,
                           kt[g].all() + qt[g].all(), patt.all())
                        ab = attb.get()
                        tt(s, "dve", ab.t[:, :], patt.t[:, :], Ub.t[:, :], ALU.mult, patt.all() + Ub.all(), ab.all())
                        mm(s, po[g].t[:, us], vpad[h].t[:, u, :], ab.t[:, :], first[g], False,
                           vpad[h].all() + ab.all(), po[g].all())
                        first[g] = False
                st.append(r_att)
                for ch in range(2):
                    def r_upd(u=u, ch=ch):
                        cs_ = slice(u * 128 + ch * 64, u * 128 + ch * 64 + 64)
                        rows = slice(ch * 64, ch * 64 + 64)
                        for g in range(2):
                            last = (u == 3 and ch == 1)
                            mm(s, po[g].t[:, cs_], Sb[g].t[:, :], qt[g].t[:, cs_], False, last,
                               Sb[g].all() + qt[g].all(), po[g].all())
                            mm(s, pu.t[:, :], khat.t[rows, u, g * 128:(g + 1) * 128], vb.t[rows, u, g * 128:(g + 1) * 128],
                               True, True, khat.all() + vb.all(), pu.all())
                            col = u * 128 + ch * 64 + 63
                            stt(s, S32[g].t[:, :], S32[g].t[:, :], eb[g].t[:, col:col + 1], pu.t[:, :], ALU.mult, ALU.add,
                                S32[g].all() + eb[g].all() + pu.all(), S32[g].all())
                            tt(s, "dve", Sb[g].t[:, :], S32[g].t[:, :], BD.t[:, :], ALU.mult, S32[g].all() + BD.all(), Sb[g].all())
                    st.append(r_upd)

            def fin():
                for g in range(2):
                    cp(s, "act", o32[g].t[:, :], po[g].t[:, :], po[g].all(), o32[g].all())
                    act(s, sq[g].t[:, :], po[g].t[:, :], AF.Square, po[g].all(), sq[g].all())
                    mm(s, pb[g].t[:, :], BD.t[:, :], sq[g].t[:, :], True, True, BD.all() + sq[g].all(), pb[g].all())
                    act(s, rs[g].t[:, :], pb[g].t[:, :], AF.Ln, pb[g].all() + eps.all(), rs[g].all(), bias=eps.t[:, 0:1], scale=1.0 / HDV)
                    act(s, rs[g].t[:, :], rs[g].t[:, :], AF.Exp, rs[g].all(), rs[g].all(), scale=-0.5)
                    tt(s, "dve", o32[g].t[:, :], o32[g].t[:, :], rs[g].t[:, :], ALU.mult, o32[g].all() + rs[g].all(), o32[g].all())
                    stt(s, ob[g].t[:, :], o32[g].t[:, :], ng.t[:, 0:1], gf.t[:, g, :], ALU.mult, ALU.mult,
                        o32[g].all() + ng.all() + [gf.r[g]], ob[g].all())
                    r0 = NH * DV + g * 128
                    dma(s, dr["MIX"][r0:r0 + 128, sl], ob[g].t[:, :], ob[g].all(), ())
            return st, fin

        load(0)
        for f_ in prep_steps(0):
            f_()
        for j in range(NTILE):
            if j + 1 < NTILE:
                load(j + 1)
                nxt = prep_steps(j + 1)
            else:
                nxt = []
            rsteps, fin = rec_steps(j)
            n_r = len(rsteps)
            early = nxt[:4]
            late = nxt[4:]
            for i_, r_ in enumerate(rsteps):
                r_()
                if i_ < len(early):
                    early[i_]()
            fin()
            for f_ in late:
                f_()
        s.emit(f"b2_{l}")


TWO_PI = 2.0 * math.pi


def emit_sin(s, c, out, ang, shape, tag):
    ki = c.sb(f"ki{tag}", shape, I32)
    kf = c.sb(f"kf{tag}", shape, F32)
    r = c.sb(f"r{tag}", shape, F32)
    m = c.sb(f"m{tag}", shape, F32)
    ts(s, "dve", kf.t[:, :], ang.t[:, :], 1.0 / TWO_PI, None, ALU.mult, None, ang.all(), kf.all())
    cp(s, "dve", ki.t[:, :], kf.t[:, :], kf.all(), ki.all())
    cp(s, "dve", kf.t[:, :], ki.t[:, :], ki.all(), kf.all())
    stt(s, r.t[:, :], kf.t[:, :], -TWO_PI, ang.t[:, :], ALU.mult, ALU.add, kf.all() + ang.all(), r.all())
    ts(s, "dve", m.t[:, :], r.t[:, :], math.pi, -TWO_PI, ALU.is_gt, ALU.mult, r.all(), m.all())
    tt(s, "dve", r.t[:, :], r.t[:, :], m.t[:, :], ALU.add, r.all() + m.all(), r.all())
    ts(s, "dve", m.t[:, :], r.t[:, :], -math.pi, TWO_PI, ALU.is_lt, ALU.mult, r.all(), m.all())
    tt(s, "dve", r.t[:, :], r.t[:, :], m.t[:, :], ALU.add, r.all() + m.all(), r.all())
    ts(s, "dve", r.t[:, :], r.t[:, :], math.pi, -math.pi, ALU.min, ALU.max, r.all(), r.all())
    act(s, out.t[:, :], r.t[:, :], AF.Sin, r.all(), out.all())


TS5 = 256
S5_MAXACT = 16
B13_PACE = None
FUSE_LN_IN = False


def gen_b3(s, c, dr, S, l, pyr, pyi, pyg):
    TT = TS5
    NTILE = S // TT
    NK = 8
    sh = (128, NK)
    ar = c.sb("ar", sh, F32)
    ai = c.sb("ai", sh, F32)
    dt = c.sb("dt", sh, F32)
    dma(s, ar.t[:, :], dr["s5_are_pp"][l], (), ar.all())
    dma(s, ai.t[:, :], dr["s5_aim_pp"][l], (), ai.all())
    dma(s, dt.t[:, :], dr["s5_ldt_pp"][l], (), dt.all())
    act(s, dt.t[:, :], dt.t[:, :], AF.Exp, dt.all(), dt.all())
    mag = c.sb("mag", sh, F32)
    th = c.sb("th", sh, F32)
    th2 = c.sb("th2", sh, F32)
    tt(s, "dve", mag.t[:, :], dt.t[:, :], ar.t[:, :], ALU.mult, dt.all() + ar.all(), mag.all())
    act(s, mag.t[:, :], mag.t[:, :], AF.Exp, mag.all(), mag.all())
    tt(s, "dve", th.t[:, :], dt.t[:, :], ai.t[:, :], ALU.mult, dt.all() + ai.all(), th.all())
    ts(s, "dve", th2.t[:, :], th.t[:, :], math.pi / 2, None, ALU.add, None, th.all(), th2.all())
    sn1 = c.sb("sn1", sh, F32)
    cs1 = c.sb("cs1", sh, F32)
    emit_sin(s, c, sn1, th, sh, "a")
    emit_sin(s, c, cs1, th2, sh, "b")
    yield
    nre = c.sb("nre", sh, F32)
    nim = c.sb("nim", sh, F32)
    tt(s, "dve", nre.t[:, :], mag.t[:, :], cs1.t[:, :], ALU.mult, mag.all() + cs1.all(), nre.all())
    ts(s, "dve", nre.t[:, :], nre.t[:, :], -1.0, None, ALU.add, None, nre.all(), nre.all())
    tt(s, "dve", nim.t[:, :], mag.t[:, :], sn1.t[:, :], ALU.mult, mag.all() + sn1.all(), nim.all())
    den = c.sb("den", sh, F32)
    t_a = c.sb("t_a", sh, F32)
    t_b = c.sb("t_b", sh, F32)
    tt(s, "dve", den.t[:, :], ar.t[:, :], ar.t[:, :], ALU.mult, ar.all(), den.all())
    tt(s, "dve", t_a.t[:, :], ai.t[:, :], ai.t[:, :], ALU.mult, ai.all(), t_a.all())
    tt(s, "dve", den.t[:, :], den.t[:, :], t_a.t[:, :], ALU.add, den.all() + t_a.all(), den.all())

    def frecip(eng):
        return eng.reciprocal(den.t[:, :], den.t[:, :])
    s.dve(frecip, den.all(), den.all())
    cre = c.sb("cre", sh, F32)
    cim = c.sb("cim", sh, F32)
    tt(s, "dve", t_a.t[:, :], nre.t[:, :], ar.t[:, :], ALU.mult, nre.all() + ar.all(), t_a.all())
    tt(s, "dve", t_b.t[:, :], nim.t[:, :], ai.t[:, :], ALU.mult, nim.all() + ai.all(), t_b.all())
    tt(s, "dve", cre.t[:, :], t_a.t[:, :], t_b.t[:, :], ALU.add, t_a.all() + t_b.all(), cre.all())
    tt(s, "dve", cre.t[:, :], cre.t[:, :], den.t[:, :], ALU.mult, cre.all() + den.all(), cre.all())
    tt(s, "dve", t_a.t[:, :], nim.t[:, :], ar.t[:, :], ALU.mult, nim.all() + ar.all(), t_a.all())
    tt(s, "dve", t_b.t[:, :], nre.t[:, :], ai.t[:, :], ALU.mult, nre.all() + ai.all(), t_b.all())
    tt(s, "dve", cim.t[:, :], t_a.t[:, :], t_b.t[:, :], ALU.subtract, t_a.all() + t_b.all(), cim.all())
    tt(s, "dve", cim.t[:, :], cim.t[:, :], den.t[:, :], ALU.mult, cim.all() + den.all(), cim.all())
    yield
    Ct = c.sb("Ct", (128, NK, TT), F32, NK)
    St = c.sb("St", (128, NK, TT), F32, NK)
    memset(s, "pool", Ct.t[:, :, 0:1], 1.0, Ct.all())
    memset(s, "pool", St.t[:, :, 0:1], 0.0, St.all())
    kre = [cs1]
    kim = [sn1]
    nstep = int(math.log2(TT))
    for k in range(1, nstep + 1):
        a_, b_ = c.sb(f"kre{k}", sh, F32), c.sb(f"kim{k}", sh, F32)
        p_, q_ = kre[-1], kim[-1]
        tt(s, "dve", t_a.t[:, :], p_.t[:, :], p_.t[:, :], ALU.mult, p_.all(), t_a.all())
        tt(s, "dve", t_b.t[:, :], q_.t[:, :], q_.t[:, :], ALU.mult, q_.all(), t_b.all())
        tt(s, "dve", a_.t[:, :], t_a.t[:, :], t_b.t[:, :], ALU.subtract, t_a.all() + t_b.all(), a_.all())
        stt(s, b_.t[:, :], p_.t[:, :], 2.0, q_.t[:, :], ALU.mult, ALU.mult, p_.all() + q_.all(), b_.all())
        kre.append(a_)
        kim.append(b_)
    yield
    tmpd = [c.sb("tmpd", (128, TT // 2), F32) for _ in range(2)]
    for k in range(nstep):
        n = 1 << k
        for kc in range(NK):
            cr, ci = kre[k].t[:, kc:kc + 1], kim[k].t[:, kc:kc + 1]
            rd = [Ct.r[kc], St.r[kc]] + kre[k].all() + kim[k].all()
            ta, tb = tmpd[0], tmpd[1]
            ts(s, "dve", ta.t[:, 0:n], St.t[:, kc, 0:n], ci, None, ALU.mult, None, rd, ta.all())
            stt(s, Ct.t[:, kc, n:2 * n], Ct.t[:, kc, 0:n], cr, ta.t[:, 0:n], ALU.mult, ALU.subtract,
                rd + ta.all(), [Ct.r[kc]])
            ts(s, "dve", tb.t[:, 0:n], Ct.t[:, kc, 0:n], ci, None, ALU.mult, None, rd, tb.all())
            stt(s, St.t[:, kc, n:2 * n], St.t[:, kc, 0:n], cr, tb.t[:, 0:n], ALU.mult, ALU.add,
                rd + tb.all(), [St.r[kc]])
        yield
    ETr, ETi = kre[nstep], kim[nstep]
    Pr = c.sb("Pr", (128, NK, TT), F32, NK)
    Pi = c.sb("Pi", (128, NK, TT), F32, NK)
    magt = c.sb("magt", (128, NK, TT), F32, NK)
    tfull = c.sb("tfull", (128, TT), F32)
    for kc in range(NK):
        cr, ci = cre.t[:, kc:kc + 1], cim.t[:, kc:kc + 1]
        rd = [Ct.r[kc], St.r[kc]] + cre.all() + cim.all()
        ts(s, "dve", tfull.t[:, :], St.t[:, kc, :], ci, None, ALU.mult, None, rd, tfull.all())
        stt(s, Pr.t[:, kc, :], Ct.t[:, kc, :], cr, tfull.t[:, :], ALU.mult, ALU.add, rd + tfull.all(), [Pr.r[kc]])
        ts(s, "dve", tfull.t[:, :], St.t[:, kc, :], cr, None, ALU.mult, None, rd, tfull.all())
        stt(s, Pi.t[:, kc, :], Ct.t[:, kc, :], ci, tfull.t[:, :], ALU.mult, ALU.subtract, rd + tfull.all(), [Pi.r[kc]])
        memset(s, "pool", magt.t[:, kc, :], 1.0, [magt.r[kc]])
        ts(s, "pool", magt.t[:, kc, :], magt.t[:, kc, :], mag.t[:, kc:kc + 1], None, ALU.mult, None,
           [magt.r[kc]] + mag.all(), [magt.r[kc]])
        if kc % 2 == 1:
            yield
    bre = c.sb("bre", (128, NK, 128), BF16)
    bim = c.sb("bim", (128, NK, 128), BF16)
    ctr = c.sb("ctr", (128, NK, 128), BF16)
    cti = c.sb("cti", (128, NK, 128), BF16)
    st = c.sb("stw", (128, NK, 128), F32)
    dma(s, bre.t[:, :, :], dr["s5_bT_re"][l], (), bre.all(), q="pool")
    dma(s, bim.t[:, :, :], dr["s5_bT_im"][l], (), bim.all(), q="pool")
    dma(s, ctr.t[:, :, :], dr["s5_cT_re"][l], (), ctr.all(), q="pool")
    dma(s, st.t[:, :, :], dr["s5_cT_im"][l], (), st.all())
    ts(s, "dve", cti.t[:, :, :], st.t[:, :, :], -1.0, None, ALU.mult, None, st.all(), cti.all())
    wg = c.sb("wg", (128, 2, 256), BF16)
    load_w_bf16(s, wg, wv(dr["s5_w_glu"][l]), 2, 256)
    dsk = c.sb("dsk", (128, 2), F32)
    bg = c.sb("bg", (128, 2), F32)
    dma(s, dsk.t[:, :], dr["s5_d_fm"][l], (), dsk.all())
    dma(s, bg.t[:, :], dr["s5_bglu_fm"][l], (), bg.all())
    gl_re = c.sb("gl_re", sh, F32)
    gl_im = c.sb("gl_im", sh, F32)
    memset(s, "pool", gl_re.t[:, :], 0.0, gl_re.all())
    memset(s, "pool", gl_im.t[:, :], 0.0, gl_im.all())
    ini_re = c.sb("ini_re", sh, F32)
    ini_im = c.sb("ini_im", sh, F32)
    u32 = [c.sb("u32", (128, 2, TT), F32, 2) for _ in range(3)]
    ubb = [c.sb("ub", (128, 2, TT), BF16, 2) for _ in range(3)]
    DEPTH_R = 2
    mA = Rot([c.sb("mA", (128, TT), F32) for _ in range(DEPTH_R)])
    mB = Rot([c.sb("mB", (128, TT), F32) for _ in range(DEPTH_R)])
    mC = Rot([c.sb("mC", (128, TT), F32) for _ in range(DEPTH_R)])
    mD = Rot([c.sb("mD", (128, TT), F32) for _ in range(DEPTH_R)])
    xre_r = Rot([c.sb("xre", (128, TT), F32) for _ in range(DEPTH_R)])
    xim_r = Rot([c.sb("xim", (128, TT), F32) for _ in range(DEPTH_R)])
    gre_r = Rot([c.sb("gre", (128, TT), F32) for _ in range(DEPTH_R)])
    gim_r = Rot([c.sb("gim", (128, TT), F32) for _ in range(DEPTH_R)])
    hreb = [c.sb("hre", (128, NK, TT), BF16, NK) for _ in range(2)]
    himb = [c.sb("him", (128, NK, TT), BF16, NK) for _ in range(2)]
    yv = [c.sb("yv", (128, TT), F32) for _ in range(2)]
    yt = [c.sb("yt", (128, TT), F32) for _ in range(2)]
    ygb = c.sb("ygb", (128, 2, TT), BF16, 2)
    sg = [c.sb("sg", (128, TT), F32) for _ in range(2)]
    ob = [c.sb("ob", (128, TT), BF16) for _ in range(2)]
    yield

    def nop():
        pass

    def pre_item(j):
        def f1():
            dma(s, u32[j % 3].t[:, :, :], fm(dr["SU"])[:, :, j * TT:(j + 1) * TT], (), u32[j % 3].all())

        def f2():
            cp(s, "act", ubb[j % 3].t[:, :, :], u32[j % 3].t[:, :, :], u32[j % 3].all(), ubb[j % 3].all())
        return [f1, nop, f2]

    item_no = [0]

    def kc_item(j, kc):
        n_ = item_no[0]
        item_no[0] += 1
        half = 0
        hs = slice(0, TT)
        ub = ubb[j % 3]
        hre, him = hreb[j % 2], himb[j % 2]
        ic = kc // 4
        st_ = {}

        def s1():
            mm(s, pyr.t[:, 0:TT], bre.t[:, kc, :], ub.t[:, ic, :], True, True, bre.all() + [ub.r[ic]], pyr.all())
            mm(s, pyr.t[:, TT:2 * TT], bim.t[:, kc, :], ub.t[:, ic, :], True, True, bim.all() + [ub.r[ic]], pyr.all())

        def s2():
            tA, tB, tC, tD = mA.get(), mB.get(), mC.get(), mD.get()
            st_["m"] = (tA, tB, tC, tD)
            y_re, y_im = pyr.t[:, 0:TT], pyr.t[:, TT:2 * TT]
            tt(s, "dve", tA.t[:, :], y_re, Pr.t[:, kc, :], ALU.mult, pyr.all() + [Pr.r[kc]], tA.all())
            tt(s, "dve", tB.t[:, :], y_im, Pi.t[:, kc, :], ALU.mult, pyr.all() + [Pi.r[kc]], tB.all())
            tt(s, "dve", tC.t[:, :], y_im, Pr.t[:, kc, :], ALU.mult, pyr.all() + [Pr.r[kc]], tC.all())
            tt(s, "dve", tD.t[:, :], y_re, Pi.t[:, kc, :], ALU.mult, pyr.all() + [Pi.r[kc]], tD.all())

        def s3():
            pass

        def s4():
            tA, tB, tC, tD = st_["m"]
            xre, xim = xre_r.get(), xim_r.get()
            st_["x"] = (xre, xim)
            tt(s, "pool", xre.t[:, :], tA.t[:, :], tB.t[:, :], ALU.subtract, tA.all() + tB.all(), xre.all())
            tt(s, "pool", xim.t[:, :], tC.t[:, :], tD.t[:, :], ALU.add, tC.all() + tD.all(), xim.all())

        def s5():
            xre, xim = st_["x"]
            gre, gim = gre_r.get(), gim_r.get()
            st_["g"] = (gre, gim)
            for (xx, gg, ini) in ((xre, gre, ini_re), (xim, gim, ini_im)):
                def fscan(eng, xx=xx, gg=gg, ini=ini):
                    return eng.tensor_tensor_scan(gg.t[:, :], magt.t[:, kc, :], xx.t[:, :],
                                                  ini.t[:, kc:kc + 1], ALU.mult, ALU.add)
                s.dve(fscan, [magt.r[kc]] + xx.all() + ini.all(), gg.all())

        def s6():
            gre, gim = st_["g"]
            cp(s, "pool", gl_re.t[:, kc:kc + 1], gre.t[:, TT - 1:TT], gre.all(), gl_re.all())
            cp(s, "pool", gl_im.t[:, kc:kc + 1], gim.t[:, TT - 1:TT], gim.all(), gl_im.all())
            tA, tB, tC, tD = mA.get(), mB.get(), mC.get(), mD.get()
            st_["m2"] = (tA, tB, tC, tD)
            tt(s, "dve", tA.t[:, :], gre.t[:, :], Ct.t[:, kc, :], ALU.mult, gre.all() + [Ct.r[kc]], tA.all())
            tt(s, "pool", tB.t[:, :], gim.t[:, :], St.t[:, kc, :], ALU.mult, gim.all() + [St.r[kc]], tB.all())
            tt(s, "dve", tC.t[:, :], gre.t[:, :], St.t[:, kc, :], ALU.mult, gre.all() + [St.r[kc]], tC.all())
            tt(s, "pool", tD.t[:, :], gim.t[:, :], Ct.t[:, kc, :], ALU.mult, gim.all() + [Ct.r[kc]], tD.all())

        def s7():
            tA, tB, tC, tD = st_["m2"]
            tt(s, "pool", hre.t[:, kc, :], tA.t[:, :], tB.t[:, :], ALU.subtract, tA.all() + tB.all(), [hre.r[kc]])
            tt(s, "pool", him.t[:, kc, :], tC.t[:, :], tD.t[:, :], ALU.add, tC.all() + tD.all(), [him.r[kc]])
        return [s1, s2, s4, s5, s6, s7]

    def fin_item(j):
        sl = slice(j * TT, (j + 1) * TT)
        u = u32[j % 3]
        hre, him = hreb[j % 2], himb[j % 2]
        hv_ = [slice(0, TT), slice(TT, 2 * TT)]

        def f1():
            for oc in range(2):
                for i_, kc in enumerate(range(4 * oc, 4 * oc + 4)):
                    mm(s, pyg.t[:, hv_[oc]], ctr.t[:, kc, :], hre.t[:, kc, :], i_ == 0, False,
                       ctr.all() + [hre.r[kc]], pyg.all())
                    mm(s, pyg.t[:, hv_[oc]], cti.t[:, kc, :], him.t[:, kc, :], False, i_ == 3,
                       cti.all() + [him.r[kc]], pyg.all())

        def f2():
            for oc in range(2):
                stt(s, yv[oc].t[:, :], u.t[:, oc, :], dsk.t[:, oc:oc + 1], pyg.t[:, hv_[oc]], ALU.mult, ALU.add,
                    [u.r[oc]] + dsk.all() + pyg.all(), yv[oc].all())

        def f3():
            for oc in range(2):
                tt(s, "pool", yt[oc].t[:, :], yv[oc].t[:, :], yv[oc].t[:, :], ALU.mult, yv[oc].all(), yt[oc].all())

        def f4():
            for oc in range(2):
                t_, y_ = yt[oc], yv[oc]
                ts(s, "dve", t_.t[:, :], t_.t[:, :], 0.044715, 1.0, ALU.mult, ALU.add, t_.all(), t_.all())
                tt(s, "dve", t_.t[:, :], t_.t[:, :], y_.t[:, :], ALU.mult, t_.all() + y_.all(), t_.all())

        def f5():
            for oc in range(2):
                t_ = yt[oc]
                act(s, t_.t[:, :], t_.t[:, :], AF.Sigmoid, t_.all(), t_.all(), scale=2.0 * math.sqrt(2.0 / math.pi))

        def f6():
            for oc in range(2):
                t_, y_ = yt[oc], yv[oc]
                tt(s, "dve", y_.t[:, :], y_.t[:, :], t_.t[:, :], ALU.mult, y_.all() + t_.all(), y_.all())
                cp(s, "pool", ygb.t[:, oc, :], y_.t[:, :], y_.all(), [ygb.r[oc]])

        def g1():
            for oc in range(2):
                for ic in range(2):
                    mm(s, pyg.t[:, hv_[oc]], wg.t[:, ic, oc * 128:(oc + 1) * 128], ygb.t[:, ic, :], ic == 0, ic == 1,
                       wg.all() + [ygb.r[ic]], pyg.all())

        def g2():
            for oc in range(2):
                act(s, sg[oc].t[:, :], pyg.t[:, hv_[oc]], AF.Sigmoid, pyg.all() + bg.all(), sg[oc].all(),
                    bias=bg.t[:, oc:oc + 1])

        def g3_():
            for oc in range(2):
                tt(s, "dve", ob[oc].t[:, :], yv[oc].t[:, :], sg[oc].t[:, :], ALU.mult,
                   yv[oc].all() + sg[oc].all(), ob[oc].all())
                r0 = NH * DV + 256 + oc * 128
                dma(s, dr["MIX"][r0:r0 + 128, sl], ob[oc].t[:, :], ob[oc].all(), ())
        return [nop] * 6 + [f1, f2, f3, f4, f5, f6, g1, g2, g3_]

    def ini_ops():
        tt(s, "dve", t_a.t[:, :], gl_re.t[:, :], ETr.t[:, :], ALU.mult, gl_re.all() + ETr.all(), t_a.all())
        tt(s, "dve", t_b.t[:, :], gl_im.t[:, :], ETi.t[:, :], ALU.mult, gl_im.all() + ETi.all(), t_b.all())
        tt(s, "dve", ini_re.t[:, :], t_a.t[:, :], t_b.t[:, :], ALU.subtract, t_a.all() + t_b.all(), ini_re.all())
        tt(s, "dve", t_a.t[:, :], gl_re.t[:, :], ETi.t[:, :], ALU.mult, gl_re.all() + ETi.all(), t_a.all())
        tt(s, "dve", t_b.t[:, :], gl_im.t[:, :], ETr.t[:, :], ALU.mult, gl_im.all() + ETr.all(), t_b.all())
        tt(s, "dve", ini_im.t[:, :], t_a.t[:, :], t_b.t[:, :], ALU.add, t_a.all() + t_b.all(), ini_im.all())

    memset(s, "pool", ini_re.t[:, :], 0.0, ini_re.all())
    memset(s, "pool", ini_im.t[:, :], 0.0, ini_im.all())
    items = [pre_item(0)]
    if NTILE > 1:
        items.append(pre_item(1))
    for j in range(NTILE):
        for kc in range(NK):
            items.append(kc_item(j, kc))
            if kc == 5 and j + 2 < NTILE:
                items.append(pre_item(j + 2))
        items.append(fin_item(j))
        if j + 1 < NTILE:
            items.append([nop] * 5 + [ini_ops])
            items.extend([[nop]] * 3)
    active = []
    it = iter(items)
    while True:
        nxt = next(it, None) if len(active) < S5_MAXACT else None
        if nxt is not None:
            active.append([nxt, 0])
        if not active:
            break
        for a_ in list(active):
            a_[0][a_[1]]()
            a_[1] += 1
            if a_[1] == len(a_[0]):
                active.remove(a_)
        yield


def stage_b3(nc, sync, dr, S, l):
    with ExitStack() as es:
        c = Ctx(nc, es)
        s = Sched(sync)
        for _ in gen_b3(s, c, dr, S, l, c.ps("pyr"), None, c.ps("pyg")):
            pass
        s.emit(f"b3_{l}")


def build(S=SEQ, nlayers=DEPTH, stages=None, debug_out=(), ext_in=()):
    nc = bass.Bass("TRN2", target_bir_lowering=False)
    dr = {}

    def din(name, shape, dtype=F32):
        dr[name] = nc.dram_tensor(name, list(shape), dtype, kind="ExternalInput").ap()

    def dscr(name, shape, dtype):
        kind = "Internal"
        if name in debug_out:
            kind = "ExternalOutput"
        if name in ext_in:
            kind = "ExternalInput"
        dr[name] = nc.dram_tensor(name, list(shape), dtype, kind=kind).ap()

    if stages is None:
        stages = ("ln_in", "a", "b13", "b2", "c1", "c2")

    def want(st):
        return st in stages

    din("xT", (D, S))
    din("ln_in_g", (128, 8))
    din("ln_in_b", (128, 8))
    for k_, shp in LAYER_W.items():
        din(k_, (DEPTH,) + shp)
    for k_, n in LAYER_V128.items():
        din(k_, (DEPTH, 128, n))
    dscr("H", (D, S), F32)
    dscr("Hb", (D, S), BF16)
    dscr("H1", (D, S), F32)
    dscr("H1b", (D, S), BF16)
    dscr("MIX", (D, S), BF16)
    din("rope_cos", (128, S))
    din("rope_sin", (128, S))
    dscr("QN", (NH, NOPE, S), BF16)
    dscr("QR", (NH * ROPE, S), BF16)
    dscr("KN", (NH, NOPE, S), BF16)
    dscr("KR", (ROPE, S), BF16)
    dscr("V", (S, NH * DV), BF16)
    dscr("HGQ", (256, S), F32)
    dscr("HGF", (256, S), F32)
    dscr("HGG", (256, S), F32)
    dscr("HGT", (S, 512), F32)
    dscr("SU", (256, S), F32)
    din("hg_ng", (DEPTH, 128, 1))
    for nm_ in ("s5_are_pp", "s5_aim_pp", "s5_ldt_pp"):
        din(nm_, (DEPTH, 128, 8))
    for nm_ in ("s5_bT_re", "s5_bT_im", "s5_cT_re", "s5_cT_im"):
        din(nm_, (DEPTH, 128, 8, 128))
    din("s5_d_fm", (DEPTH, 128, 2))
    din("s5_bglu_fm", (DEPTH, 128, 2))
    din("hg_lb_fm", (DEPTH, 128, 2))
    din("hg_lower_bound", (DEPTH, 256))
    dr["outT"] = nc.dram_tensor("outT", [D, S], F32, kind="ExternalOutput").ap()
    with ExitStack() as es:
        sync = Sync(nc, es)
        fuse0 = FUSE_LN_IN and want("ln_in") and want("a")
        if want("ln_in") and not fuse0:
            stage_ln_in(nc, sync, dr, S)
        for l in range(nlayers):
            if want("a"):
                stage_a(nc, sync, dr, S, l, fuse_ln_in=(fuse0 and l == 0))
            if want("b1"):
                stage_b1(nc, sync, dr, S, l, with_s5=False)
            if want("b2"):
                stage_b2(nc, sync, dr, S, l)
            if want("b3"):
                stage_b3(nc, sync, dr, S, l)
            if want("b13"):
                stage_b1(nc, sync, dr, S, l, with_s5=True)
            last = l == nlayers - 1
            if want("c1") and want("c2") and S // T >= 2:
                with ExitStack() as es2:
                    w1b = Ctx(nc, es2).sb("w1p", (128, 8, DFF), BF16)
                    stage_c1(nc, sync, dr, S, l, w1_pref=w1b)
                    stage_c2(nc, sync, dr, S, l, dr["outT"] if last else dr["H"],
                             None if last else dr["Hb"], w1_pref=w1b)
            else:
                if want("c1"):
                    stage_c1(nc, sync, dr, S, l)
                if want("c2"):
                    stage_c2(nc, sync, dr, S, l, dr["outT"] if last else dr["H"],
                             None if last else dr["Hb"])
    return nc


def _fmv(v):
    v = np.asarray(v, np.float32)
    return np.ascontiguousarray(v.reshape(v.shape[0], -1, 128).transpose(0, 2, 1))


def _rope_tables(S):
    freqs = (ROPE_THETA ** (-np.arange(0, ROPE, 2, dtype=np.float32) / ROPE)).astype(np.float32)
    ang = np.arange(S, dtype=np.float32)[:, None] * freqs[None, :]
    cos = np.cos(ang).astype(np.float32).T
    sin = np.sin(ang).astype(np.float32).T
    return (np.ascontiguousarray(np.concatenate([cos] * 4, 0)),
            np.ascontiguousarray(np.concatenate([sin] * 4, 0)))


def _s5_layouts(inp):
    L = inp["s5_a_re"].shape[0]
    out = {}

    def pp(a):
        a = np.asarray(a, np.float32)
        return np.ascontiguousarray(a.reshape(L, 8, 2, 64).transpose(0, 2, 3, 1).reshape(L, 128, 8))
    out["s5_are_pp"] = pp(inp["s5_a_re"])
    out["s5_aim_pp"] = pp(inp["s5_a_im"])
    out["s5_ldt_pp"] = pp(np.repeat(np.asarray(inp["s5_log_dt"], np.float32)[:, :, None], 64, axis=2))

    def bT(b):
        b = np.asarray(b, np.float32)
        o = np.zeros((L, 128, 8, 128), np.float32)
        for g in range(16):
            o[:, (g % 8) * 16:(g % 8) * 16 + 16, g // 2, (g % 2) * 64:(g % 2) * 64 + 64] = b[:, g].transpose(0, 2, 1)
        return o

    def cT(cc):
        cc = np.asarray(cc, np.float32)
        o = np.zeros((L, 128, 8, 128), np.float32)
        for g in range(16):
            o[:, (g % 2) * 64:(g % 2) * 64 + 64, g // 2, (g % 8) * 16:(g % 8) * 16 + 16] = cc[:, g].transpose(0, 2, 1)
        return o
    out["s5_bT_re"] = bT(inp["s5_b_re"])
    out["s5_bT_im"] = bT(inp["s5_b_im"])
    out["s5_cT_re"] = cT(inp["s5_c_re"])
    out["s5_cT_im"] = cT(inp["s5_c_im"])
    out["s5_d_fm"] = _fmv(inp["s5_d"])
    out["s5_bglu_fm"] = _fmv(inp["s5_b_glu"])
    return out


def make_shared_inputs(inp, S):
    cos, sin = _rope_tables(S)
    im = {"ln_in_g": _fmv(np.asarray(inp["ln_in_g"])[None])[0],
          "ln_in_b": _fmv(np.asarray(inp["ln_in_b"])[None])[0],
          "rope_cos": cos, "rope_sin": sin}
    for k in LAYER_W:
        im[k] = np.ascontiguousarray(np.asarray(inp[k], np.float32))
    for k in LAYER_V128:
        im[k] = _fmv(inp[k])
    im["hg_ng"] = np.ascontiguousarray(np.tile(np.asarray(inp["hg_norm_g"], np.float32), (1, 2))[:, :, None])
    im["hg_lb_fm"] = _fmv(inp["hg_lower_bound"])
    im["hg_lower_bound"] = np.ascontiguousarray(np.asarray(inp["hg_lower_bound"], np.float32))
    im.update(_s5_layouts(inp))
    return im


N_CORES = 8


def kernel(**inputs):
    x = np.asarray(inputs["x"], np.float32)
    B, S, _ = x.shape
    shared = make_shared_inputs(inputs, S)
    nc = build(S=S, nlayers=DEPTH)
    in_maps = []
    for core in range(N_CORES):
        m = dict(shared)
        m["xT"] = np.ascontiguousarray(x[core % B].T)
        in_maps.append(m)
    res = run_bass_kernel_spmd(nc, in_maps, core_ids=list(range(N_CORES)))
    out = np.stack([np.asarray(res.results[b]["outT"]).T for b in range(B)], 0)
    return np.ascontiguousarray(out.astype(np.float32))
```

```python
import math
from contextlib import ExitStack

import numpy as np
import ml_dtypes
import concourse.bass as bass
import concourse.mybir as mybir
from concourse.bass_utils import run_bass_kernel_spmd

F32 = mybir.dt.float32
BF16 = mybir.dt.bfloat16
I32 = mybir.dt.int32
AF = mybir.ActivationFunctionType
ALU = mybir.AluOpType

D = 1024
DEPTH = 2
SEQ = 8192
BATCH = 4
NH = 4
NOPE = 128
ROPE = 64
QK = 192
DV = 128
QR = 384
KVR = 256
HGH = 4
HDK = 64
HDV = 64
S5G = 16
S5P = 16
S5N = 64
DFF = 4096
DIN = 1984
ALPHA = (2 * DEPTH) ** 0.25
LN_EPS = 1e-5
RMS_EPS = 1e-6
ROPE_THETA = 10000.0
T = 512
OFF_CQ, OFF_CKV, OFF_KR, OFF_HQ, OFF_HF, OFF_HI, OFF_HG, OFF_SU = (
    0, 384, 640, 704, 960, 1216, 1472, 1728)


class Res:
    __slots__ = ("name", "last_w", "readers", "psum")

    def __init__(self, name, psum=False):
        self.name = name
        self.last_w = None
        self.readers = []
        self.psum = psum


class Op:
    __slots__ = ("eng", "fn", "deps", "dma", "flag", "token", "prewait", "idx")


class Sync:
    SEM_LIMIT = 30000

    def __init__(self, nc, es, n_dma_sems=20):
        self.nc = nc
        self.es = es
        self.eng_sem = {}
        self.eng_cnt = {}
        n_sw = 6
        self.dma_sems = [es.enter_context(nc.semaphore(f"dma{i}")) for i in range(n_dma_sems + n_sw)]
        self.dma_cnt = [0] * (n_dma_sems + n_sw)
        self.dma_pool = {"hw": list(range(n_dma_sems)), "sw": list(range(n_dma_sems, n_dma_sems + n_sw))}
        self.dma_rr = {"hw": 0, "sw": 0}
        self.waited = {e: {} for e in ("pe", "act", "dve", "pool", "sp")}
        self.nsem = 0

    def new_eng_sem(self, e):
        self.nsem += 1
        s = self.es.enter_context(self.nc.semaphore(f"s_{e}_{self.nsem}"))
        self.eng_sem[e] = s
        self.eng_cnt[e] = 0
        return s


class Sched:
    def __init__(self, sync):
        self.sync = sync
        self.nc = sync.nc
        self.ops = []

    def add(self, eng, fn, reads=(), writes=(), dma=False):
        op = Op()
        op.eng, op.fn, op.dma = eng, fn, dma
        op.flag = False
        op.token = None
        op.prewait = None
        op.idx = len(self.ops)
        deps = []
        xw = [r for r in reads if r.psum]
        if xw:
            writes = list(writes) + [r for r in xw if r not in writes]
        for r in reads:
            w = r.last_w
            if w is not None:
                if w.dma or w.eng != eng or dma or eng != "pe":
                    deps.append(w)
            r.readers.append(op)
        for wres in writes:
            w = wres.last_w
            if w is not None and (w.dma or dma or w.eng != eng or eng != "pe"):
                deps.append(w)
            for rd in wres.readers:
                if rd is op:
                    continue
                if rd.dma or dma or rd.eng != eng or eng != "pe":
                    deps.append(rd)
            wres.last_w = op
            wres.readers = []
        for d_ in deps:
            d_.flag = True
        op.deps = deps
        self.ops.append(op)
        return op

    def pe(self, fn, reads=(), writes=()):
        return self.add("pe", fn, reads, writes)

    def act(self, fn, reads=(), writes=()):
        return self.add("act", fn, reads, writes)

    def dve(self, fn, reads=(), writes=()):
        return self.add("dve", fn, reads, writes)

    def pool(self, fn, reads=(), writes=()):
        return self.add("pool", fn, reads, writes)

    def dma(self, fn, reads=(), writes=(), q="sp"):
        return self.add(q, fn, reads, writes, dma=True)

    def emit(self, name):
        sy = self.sync
        nc = self.nc
        ops = self.ops
        last = {}
        for op in ops:
            if not op.dma:
                last[op.eng] = op
        for op in last.values():
            op.flag = True
        for op in ops:
            if op.dma:
                kind = "sw" if op.eng == "pool" else "hw"
                pool_ = sy.dma_pool[kind]
                j = pool_[sy.dma_rr[kind] % len(pool_)]
                sy.dma_rr[kind] += 1
                op.prewait = (sy.dma_sems[j], sy.dma_cnt[j])
                sy.dma_cnt[j] += 16
                op.token = (sy.dma_sems[j], sy.dma_cnt[j])
            elif op.flag:
                e = op.eng
                if e not in sy.eng_sem or sy.eng_cnt[e] >= sy.SEM_LIMIT:
                    sy.new_eng_sem(e)
                sy.eng_cnt[e] += 1
                op.token = (sy.eng_sem[e], sy.eng_cnt[e])
        end_tokens = [op.token for op in last.values()]
        end_tokens += [(s, c) for s, c in zip(sy.dma_sems, sy.dma_cnt) if c > 0]

        def stream(e):
            def body(eng):
                waited = sy.waited[e]

                def wait(tok):
                    s, v = tok
                    if v <= 0:
                        return
                    key = id(s)
                    if waited.get(key, 0) >= v:
                        return
                    eng.wait_ge(s, v)
                    waited[key] = v

                for op in ops:
                    if op.eng != e:
                        continue
                    if op.dma:
                        wait(op.prewait)
                    need = {}
                    for d_ in op.deps:
                        s, v = d_.token
                        k = id(s)
                        if k not in need or need[k][1] < v:
                            need[k] = (s, v)
                    for tok in need.values():
                        wait(tok)
                    inst = op.fn(eng)
                    if op.dma:
                        inst.then_inc(op.token[0], 16)
                    elif op.flag:
                        inst.then_inc(op.token[0], 1)
                for tok in end_tokens:
                    wait(tok)
            return body

        with nc.Block(name) as block:
            block.tensor(stream("pe"))
            block.scalar(stream("act"))
            block.vector(stream("dve"))
            block.gpsimd(stream("pool"))
            block.sync(stream("sp"))
        self.ops = []


class Buf:
    def __init__(self, t, name, nchunk=1, psum=False):
        self.t = t
        self.r = [Res(f"{name}.{i}", psum) for i in range(nchunk)]

    def all(self):
        return list(self.r)


class Ctx:
    N = [0]

    def __init__(self, nc, es):
        self.nc = nc
        self.es = es

    def sb(self, name, shape, dtype, nchunk=1):
        Ctx.N[0] += 1
        t = self.es.enter_context(self.nc.sbuf_tensor(f"{name}_{Ctx.N[0]}", list(shape), dtype))
        return Buf(t, name, nchunk)

    def ps(self, name, shape=(128, 512), dtype=F32, nchunk=1):
        Ctx.N[0] += 1
        t = self.es.enter_context(self.nc.psum_tensor(f"{name}_{Ctx.N[0]}", list(shape), dtype))
        return Buf(t, name, nchunk, psum=True)


def mm(s, out_ap, lhsT, rhs, start, stop, reads, writes):
    def fn(eng):
        return eng.matmul(out_ap, lhsT, rhs, start=start, stop=stop)
    return s.pe(fn, reads, writes)


def act(s, out_ap, in_ap, func, reads, writes, bias=None, scale=None):
    def fn(eng):
        kw = {}
        if bias is not None:
            kw["bias"] = bias
        if scale is not None:
            kw["scale"] = scale
        return eng.activation(out_ap, in_ap, func, **kw)
    return s.act(fn, reads, writes)


def tt(s, e, out_ap, in0, in1, op, reads, writes):
    def fn(eng):
        return eng.tensor_tensor(out_ap, in0, in1, op)
    return s.add(e, fn, reads, writes)


def ts(s, e, out_ap, in0, s1, s2, op0, op1, reads, writes):
    def fn(eng):
        if op1 is None:
            return eng.tensor_scalar(out_ap, in0, s1, None, op0)
        return eng.tensor_scalar(out_ap, in0, s1, s2, op0, op1)
    return s.add(e, fn, reads, writes)


def stt(s, out_ap, in0, scalar, in1, op0, op1, reads, writes):
    def fn(eng):
        return eng.scalar_tensor_tensor(out_ap, in0, scalar, in1, op0, op1)
    return s.dve(fn, reads, writes)


def cp(s, e, out_ap, in_ap, reads, writes):
    if e == "act":
        def fn(eng):
            return eng.copy(out_ap, in_ap)
    else:
        def fn(eng):
            return eng.tensor_copy(out_ap, in_ap)
    return s.add(e, fn, reads, writes)


def dma(s, out_ap, in_ap, reads, writes, q="sp"):
    def fn(eng):
        return eng.dma_start(out_ap, in_ap)
    return s.dma(fn, reads, writes, q=q)


def memset(s, e, ap, val, writes):
    def fn(eng):
        return eng.memset(ap, val)
    return s.add(e, fn, (), writes)


def load_w_bf16(s, dst, src_view, nk, ncols, res=None):
    res = dst.all() if res is None else res
    for k in range(nk):
        for c0 in range(0, ncols, 2048):
            c1 = min(ncols, c0 + 2048)
            dma(s, dst.t[:, k, c0:c1], src_view[:, k, c0:c1], (), res, q="pool")


class LNState:
    pass


class LN:
    def __init__(self, s, z, gbuf, bbuf, out32, outb, ps1, ps2, tmp, slot=0):
        self.s, self.z, self.g, self.b = s, z, gbuf, bbuf
        self.out32, self.outb, self.ps1, self.ps2, self.tmp, self.slot = out32, outb, ps1, ps2, tmp, slot

    def stats(self, k):
        s, z, tmp, ps1, ps2 = self.s, self.z, self.tmp, self.ps1, self.ps2
        KC = D // 128
        onesb = tmp["onesb"]
        zb = tmp["zb"][k % 2]
        sq = tmp["sq"][k % 2]
        cp(s, "act", zb.t[:, :], z.t[:, k, :], [z.r[k]], zb.all())
        act(s, sq.t[:, :], z.t[:, k, :], AF.Square, [z.r[k]], sq.all())
        mm(s, ps1.t[:, :], onesb.t[:, :], zb.t[:, :], k == 0, k == KC - 1, zb.all() + onesb.all(), ps1.all())
        mm(s, ps2.t[:, :], onesb.t[:, :], sq.t[:, :], k == 0, k == KC - 1, sq.all() + onesb.all(), ps2.all())

    def rstd(self):
        s, tmp, ps1, ps2 = self.s, self.tmp, self.ps1, self.ps2
        mean, rstd, nm = tmp["mean"][self.slot], tmp["rstd"][self.slot], tmp["nm"][self.slot]
        act(s, mean.t[:, :], ps1.t[:, :], AF.Copy, ps1.all(), mean.all(), scale=1.0 / D)
        tt(s, "dve", nm.t[:, :], mean.t[:, :], mean.t[:, :], ALU.mult, mean.all(), nm.all())
        stt(s, rstd.t[:, :], ps2.t[:, :], 1.0 / D, nm.t[:, :], ALU.mult, ALU.subtract,
            ps2.all() + nm.all(), rstd.all())
        act(s, rstd.t[:, :], rstd.t[:, :], AF.Ln, rstd.all() + tmp["eps"].all(), rstd.all(),
            bias=tmp["eps"].t[:, 0:1])
        act(s, rstd.t[:, :], rstd.t[:, :], AF.Exp, rstd.all(), rstd.all(), scale=-0.5)
        stt(s, nm.t[:, :], mean.t[:, :], -1.0, rstd.t[:, :], ALU.mult, ALU.mult,
            mean.all() + rstd.all(), nm.all())

    def apply(self, k):
        s, z, tmp = self.s, self.z, self.tmp
        rstd, nm = tmp["rstd"][self.slot], tmp["nm"][self.slot]
        g_ap, b_ap = self.g.t, self.b.t
        gb = self.g.all() + self.b.all()
        t_ = tmp["t"][k % 2]
        tt(s, "dve", t_.t[:, :], z.t[:, k, :], rstd.t[:, :], ALU.mult, [z.r[k]] + rstd.all(), t_.all())
        tt(s, "pool", t_.t[:, :], t_.t[:, :], nm.t[:, :], ALU.add, t_.all() + nm.all(), t_.all())
        act(s, self.out32.t[:, k, :], t_.t[:, :], AF.Identity, t_.all() + gb, [self.out32.r[k]],
            bias=b_ap[:, k:k + 1], scale=g_ap[:, k:k + 1])
        ts(s, "dve", self.outb.t[:, k, :], t_.t[:, :], g_ap[:, k:k + 1], b_ap[:, k:k + 1], ALU.mult, ALU.add,
           t_.all() + gb, [self.outb.r[k]])


def emit_layernorm(s, c, z, gbuf, bbuf, out32, outb, onesb, ps1, ps2, tmp, slot=0):
    ln = LN(s, z, gbuf, bbuf, out32, outb, ps1, ps2, tmp, slot)
    for k in range(D // 128):
        ln.stats(k)
    ln.rstd()
    for k in range(D // 128):
        ln.apply(k)


def ln_tmp(c, s, nslot=2):
    tmp = {
        "sq": [c.sb("lnsq", (128, T), BF16) for _ in range(2)],
        "zb": [c.sb("lnzb", (128, T), BF16) for _ in range(2)],
        "t": [c.sb("lnt", (128, T), F32) for _ in range(2)],
        "mean": [c.sb("lnmean", (128, T), F32) for _ in range(nslot)],
        "rstd": [c.sb("lnrstd", (128, T), F32) for _ in range(nslot)],
        "nm": [c.sb("lnnm", (128, T), F32) for _ in range(nslot)],
        "eps": c.sb("lneps", (128, 1), F32),
        "onesb": c.sb("lnones", (128, 128), BF16),
    }
    memset(s, "pool", tmp["onesb"].t[:, :], 1.0, tmp["onesb"].all())
    memset(s, "pool", tmp["eps"].t[:, :], LN_EPS, tmp["eps"].all())
    return tmp


def fm(ap):
    return ap.rearrange("(k p) s -> p k s", p=128)


def stage_ln_in(nc, sync, dr, S):
    with ExitStack() as es:
        c = Ctx(nc, es)
        s = Sched(sync)
        NTILE = S // T
        zb = [c.sb("z", (128, 8, T), F32, 8) for _ in range(3)]
        ob = [c.sb("ob", (128, 8, T), BF16, 8) for _ in range(2)]
        g = c.sb("g", (128, 8), F32)
        b = c.sb("b", (128, 8), F32)
        ps1, ps2 = [c.ps("ps1") for _ in range(2)], [c.ps("ps2") for _ in range(2)]
        tmp = ln_tmp(c, s)
        dma(s, g.t[:, :], dr["ln_in_g"][:, :], (), g.all())
        dma(s, b.t[:, :], dr["ln_in_b"][:, :], (), b.all())
        xv, hv, hbv = fm(dr["xT"]), fm(dr["H"]), fm(dr["Hb"])

        def load(j):
            dma(s, zb[j % 3].t[:, :, :], xv[:, :, j * T:(j + 1) * T], (), zb[j % 3].all())

        def store(j):
            sl_ = slice(j * T, (j + 1) * T)
            dma(s, hv[:, :, sl_], zb[j % 3].t[:, :, :], zb[j % 3].all(), ())
            dma(s, hbv[:, :, sl_], ob[j % 2].t[:, :, :], ob[j % 2].all(), ())
        load(0)
        prev = None
        for j in range(NTILE):
            z, o = zb[j % 3], ob[j % 2]
            if j + 1 < NTILE:
                load(j + 1)
            ln = LN(s, z, g, b, z, o, ps1[j % 2], ps2[j % 2], tmp, j % 2)
            for k in range(8):
                ln.stats(k)
                if prev is not None:
                    prev.apply(k)
            if prev is not None:
                store(j - 1)
            ln.rstd()
            prev = ln
        for k in range(8):
            prev.apply(k)
        store(NTILE - 1)
        s.emit("ln_in")


def wv(ap):
    return ap.rearrange("(k p) n -> p k n", p=128)


def stage_c1(nc, sync, dr, S, l, w1_pref=None):
    with ExitStack() as es:
        c = Ctx(nc, es)
        s = Sched(sync)
        NTILE = S // T
        w = c.sb("wout", (128, 8, D), BF16)
        load_w_bf16(s, w, wv(dr["w_out"][l]), 8, D)
        g = c.sb("g", (128, 8), F32)
        b = c.sb("b", (128, 8), F32)
        dma(s, g.t[:, :], dr["ln1_g"][l], (), g.all())
        dma(s, b.t[:, :], dr["ln1_b"][l], (), b.all())
        tmp = ln_tmp(c, s)
        hb_ = [c.sb("h", (128, 8, T), F32, 8) for _ in range(3)]
        mb = [c.sb("mix", (128, 8, T), BF16, 8) for _ in range(3)]
        ob = [c.sb("ob", (128, 8, T), BF16, 8) for _ in range(2)]
        pm = [c.ps("pm") for _ in range(4)]
        ps1, ps2 = [c.ps("ps1") for _ in range(2)], [c.ps("ps2") for _ in range(2)]
        hv, mv, h1v, h1bv = fm(dr["H"]), fm(dr["MIX"]), fm(dr["H1"]), fm(dr["H1b"])

        def load(j):
            sl_ = slice(j * T, (j + 1) * T)
            dma(s, mb[j % 3].t[:, :, :], mv[:, :, sl_], (), mb[j % 3].all())
            dma(s, hb_[j % 3].t[:, :, :], hv[:, :, sl_], (), hb_[j % 3].all())

        def store(j):
            sl_ = slice(j * T, (j + 1) * T)
            dma(s, h1v[:, :, sl_], hb_[j % 3].t[:, :, :], hb_[j % 3].all(), ())
            dma(s, h1bv[:, :, sl_], ob[j % 2].t[:, :, :], ob[j % 2].all(), ())
        load(0)
        prev = None
        for j in range(NTILE):
            h, mx, o = hb_[j % 3], mb[j % 3], ob[j % 2]
            if j + 1 < NTILE:
                load(j + 1)
            if j == 1 and w1_pref is not None:
                load_w_bf16(s, w1_pref, wv(dr["w_ff1"][l]), 8, DFF)
            ln = LN(s, h, g, b, h, o, ps1[j % 2], ps2[j % 2], tmp, j % 2)
            for m in range(8):
                p = pm[m % 4]
                for k in range(8):
                    mm(s, p.t[:, :], w.t[:, k, m * 128:(m + 1) * 128], mx.t[:, k, :],
                       k == 0, k == 7, [w.r[0], mx.r[k]], p.all())
                stt(s, h.t[:, m, :], h.t[:, m, :], ALPHA, p.t[:, :], ALU.mult, ALU.add,
                    [h.r[m]] + p.all(), [h.r[m]])
                if m >= 2:
                    ln.stats(m - 2)
                if prev is not None:
                    prev.apply(m)
            if prev is not None:
                store(j - 1)
            ln.stats(6)
            ln.stats(7)
            ln.rstd()
            prev = ln
        for m in range(8):
            prev.apply(m)
        store(NTILE - 1)
        s.emit(f"c1_{l}")


def stage_c2(nc, sync, dr, S, l, out32, outb, w1_pref=None):
    with ExitStack() as es:
        c = Ctx(nc, es)
        s = Sched(sync)
        NTILE = S // T
        if w1_pref is not None:
            w1 = w1_pref
        else:
            w1 = c.sb("w1", (128, 8, DFF), BF16)
            load_w_bf16(s, w1, wv(dr["w_ff1"][l]), 8, DFF)
        w2 = c.sb("w2", (128, 32, D), BF16)
        load_w_bf16(s, w2, wv(dr["w_ff2"][l]), 32, D)
        g = c.sb("g", (128, 8), F32)
        b = c.sb("b", (128, 8), F32)
        dma(s, g.t[:, :], dr["ln2_g"][l], (), g.all())
        dma(s, b.t[:, :], dr["ln2_b"][l], (), b.all())
        tmp = ln_tmp(c, s, nslot=1)
        h = c.sb("h", (128, 8, T), F32, 8)
        hbb = [c.sb("hb", (128, 8, T), BF16, 8) for _ in range(2)]
        a = c.sb("a", (128, 32, T), BF16, 32)
        sq = tmp["zb"]
        pf = [c.ps("pf") for _ in range(4)]
        pz = [c.ps("pz") for _ in range(2)]
        ps1, ps2 = c.ps("ps1"), c.ps("ps2")
        h1v, h1bv = fm(dr["H1"]), fm(dr["H1b"])
        o32v = fm(out32)
        obv = fm(outb) if outb is not None else None

        def load_hb(j):
            dma(s, hbb[j % 2].t[:, :, :], h1bv[:, :, j * T:(j + 1) * T], (), hbb[j % 2].all())

        def load_h(j):
            dma(s, h.t[:, :, :], h1v[:, :, j * T:(j + 1) * T], (), h.all())

        def store(j):
            sl_ = slice(j * T, (j + 1) * T)
            dma(s, o32v[:, :, sl_], h.t[:, :, :], h.all(), ())
            if obv is not None:
                dma(s, obv[:, :, sl_], hbb[j % 2].t[:, :, :], hbb[j % 2].all(), ())
        load_hb(0)
        load_h(0)
        if NTILE > 1:
            load_hb(1)
        prev = None
        for j in range(NTILE):
            hbt = hbb[j % 2]
            for m in range(32):
                p = pf[m % 4]
                for k in range(8):
                    mm(s, p.t[:, :], w1.t[:, k, m * 128:(m + 1) * 128], hbt.t[:, k, :],
                       k == 0, k == 7, [w1.r[0], hbt.r[k]], p.all())
                q = sq[m % 2]
                act(s, q.t[:, :], p.t[:, :], AF.Square, p.all(), q.all())
                stt(s, a.t[:, m, :], p.t[:, :], 0.0, q.t[:, :], ALU.is_gt, ALU.mult,
                    p.all() + q.all(), [a.r[m]])
                if prev is not None and m % 4 == 3:
                    prev.apply(m // 4)
            if prev is not None:
                store(j - 1)
                load_h(j)
                if j + 1 < NTILE:
                    load_hb(j + 1)
            ln = LN(s, h, g, b, h, hbt, ps1, ps2, tmp, 0)
            for m in range(8):
                p = pz[m % 2]
                for k in range(32):
                    mm(s, p.t[:, :], w2.t[:, k, m * 128:(m + 1) * 128], a.t[:, k, :],
                       k == 0, k == 31, [w2.r[0], a.r[k]], p.all())
                stt(s, h.t[:, m, :], h.t[:, m, :], ALPHA, p.t[:, :], ALU.mult, ALU.add,
                    [h.r[m]] + p.all(), [h.r[m]])
                if m >= 1:
                    ln.stats(m - 1)
            ln.stats(7)
            ln.rstd()
            prev = ln
        for m in range(8):
            prev.apply(m)
        store(NTILE - 1)
        s.emit(f"c2_{l}")


LAYER_W = {
    "w_in": (D, DIN), "mla_w_uq": (QR, NH * QK), "mla_w_ukv": (KVR, NH * (NOPE + DV)),
    "w_out": (D, D), "w_ff1": (D, DFF), "w_ff2": (DFF, D), "s5_w_glu": (256, 256),
}
LAYER_V128 = {"ln1_g": 8, "ln1_b": 8, "ln2_g": 8, "ln2_b": 8,
              "mla_q_norm_g": 3, "mla_kv_norm_g": 2}


class Rot:
    def __init__(self, bufs):
        self.bufs = bufs
        self.i = 0

    def get(self):
        b = self.bufs[self.i % len(self.bufs)]
        self.i += 1
        return b


def stage_a(nc, sync, dr, S, l, fuse_ln_in=False):
    with ExitStack() as es:
        c = Ctx(nc, es)
        s = Sched(sync)
        NTILE = S // T
        win = c.sb("win", (128, 8, DIN), BF16)
        load_w_bf16(s, win, wv(dr["w_in"][l]), 8, DIN)
        wrotk = c.sb("wrotk", (128, 8, 64), BF16)
        for k in range(8):
            ts(s, "dve", wrotk.t[:, k, 0:32], win.t[:, k, OFF_KR + 32:OFF_KR + 64], -1.0, None,
               ALU.mult, None, win.all(), wrotk.all())
            cp(s, "dve", wrotk.t[:, k, 32:64], win.t[:, k, OFF_KR:OFF_KR + 32], win.all(), wrotk.all())
        gq = c.sb("gq", (128, 3), F32)
        gkv = c.sb("gkv", (128, 2), F32)
        dma(s, gq.t[:, :], dr["mla_q_norm_g"][l], (), gq.all())
        dma(s, gkv.t[:, :], dr["mla_kv_norm_g"][l], (), gkv.all())
        stq = c.sb("stq", (128, 3, NH * QK), F32)
        dma(s, stq.t[:, :, :], wv(dr["mla_w_uq"][l]), (), stq.all())
        wuq = c.sb("wuq", (128, 3, NH * QK), BF16)
        for k in range(3):
            ts(s, "dve", wuq.t[:, k, :], stq.t[:, k, :], gq.t[:, k:k + 1], None, ALU.mult, None,
               stq.all() + gq.all(), wuq.all())
        wqr = c.sb("wqr", (128, 3, 256), BF16)
        wqx = c.sb("wqx", (128, 3, 256), BF16)
        for k in range(3):
            for h in range(NH):
                b0 = h * QK + NOPE
                cp(s, "pool", wqr.t[:, k, h * 64:(h + 1) * 64], wuq.t[:, k, b0:b0 + 64],
                   wuq.all(), wqr.all())
                ts(s, "dve", wqx.t[:, k, h * 64:h * 64 + 32], wuq.t[:, k, b0 + 32:b0 + 64], -1.0, None,
                   ALU.mult, None, wuq.all(), wqx.all())
                cp(s, "dve", wqx.t[:, k, h * 64 + 32:h * 64 + 64], wuq.t[:, k, b0:b0 + 32],
                   wuq.all(), wqx.all())
        stkv = c.sb("stkv", (128, 2, NH * 256), F32)
        dma(s, stkv.t[:, :, :], wv(dr["mla_w_ukv"][l]), (), stkv.all())
        wukv = c.sb("wukv", (128, 2, NH * 256), BF16)
        for k in range(2):
            ts(s, "dve", wukv.t[:, k, :], stkv.t[:, k, :], gkv.t[:, k:k + 1], None, ALU.mult, None,
               stkv.all() + gkv.all(), wukv.all())
        wvv = c.sb("wvv", (128, 2, NH * DV), BF16)
        for k in range(2):
            for h in range(NH):
                cp(s, "pool", wvv.t[:, k, h * DV:(h + 1) * DV],
                   wukv.t[:, k, h * 256 + NOPE:h * 256 + 256], wukv.all(), wvv.all())
        ones32 = c.sb("ones16", (128, 128), BF16)
        memset(s, "pool", ones32.t[:, :], 1.0, ones32.all())
        epsr = c.sb("epsr", (128, 1), F32)
        memset(s, "pool", epsr.t[:, :], RMS_EPS, epsr.all())
        lnsc = c.sb("lnsc", (128, 1), F32)
        memset(s, "pool", lnsc.t[:, :], math.log(QK ** -0.5), lnsc.all())
        hbb = [c.sb("hb", (128, 8, T), BF16, 8) for _ in range(2)]
        cq = c.sb("cq", (128, 3, T), BF16, 3)
        ckv = c.sb("ckv", (128, 2, T), BF16, 2)
        sqb = [c.sb("sq", (128, T), BF16) for _ in range(3)]
        rq = c.sb("rq", (128, T), F32)
        rkv = c.sb("rkv", (128, T), F32)
        rcol = c.sb("rcol", (128, 8), F32)
        cosb = [c.sb("cos", (128, T), F32) for _ in range(2)]
        sinb = [c.sb("sin", (128, T), F32) for _ in range(2)]
        cs = c.sb("cs", (128, T), F32)
        sn = c.sb("sn", (128, T), F32)
        t1 = Rot([c.sb("t1", (128, T), F32) for _ in range(2)])
        t2 = Rot([c.sb("t2", (128, T), F32) for _ in range(2)])
        o16 = Rot([c.sb("o16", (128, T), BF16) for _ in range(6)])
        o32 = Rot([c.sb("o32", (128, T), F32) for _ in range(6)])
        pp = Rot([c.ps("pp") for _ in range(4 if fuse_ln_in else 6)])
        pssq = c.ps("pssq")
        pcol = c.ps("pcol", (128, 8))
        hbv = fm(dr["Hb"])
        evac_i = [0]

        pending = []

        def tick():
            if pending:
                pending.pop(0)()

        def proj(p, pairs, M=128):
            n = len(pairs)
            for i, (lh, rh, rd) in enumerate(pairs):
                mm(s, p.t[0:M, :], lh, rh, i == 0, i == n - 1, rd, p.all())
            tick()

        if fuse_ln_in:
            zx = [c.sb("zx", (128, 8, T), F32, 8) for _ in range(2)]
            g_in = c.sb("g_in", (128, 8), F32)
            b_in = c.sb("b_in", (128, 8), F32)
            dma(s, g_in.t[:, :], dr["ln_in_g"][:, :], (), g_in.all())
            dma(s, b_in.t[:, :], dr["ln_in_b"][:, :], (), b_in.all())
            lntmp = ln_tmp(c, s, nslot=1)
            lps1, lps2 = c.ps("lps1"), c.ps("lps2")
            xv_, hv_ = fm(dr["xT"]), fm(dr["H"])

            def ln_pieces(j):
                z = zx[j % 2]
                ln = LN(s, z, g_in, b_in, z, hbb[j % 2], lps1, lps2, lntmp, 0)
                pcs = [(lambda k=k: ln.stats(k)) for k in range(8)] + [ln.rstd]
                pcs += [(lambda k=k: ln.apply(k)) for k in range(8)]
                pcs.append(lambda: dma(s, hv_[:, :, j * T:(j + 1) * T], z.t[:, :, :], z.all(), ()))
                return pcs

            def load_x(j):
                dma(s, zx[j % 2].t[:, :, :], xv_[:, :, j * T:(j + 1) * T], (), zx[j % 2].all())

        def load(j):
            sl_ = slice(j * T, (j + 1) * T)
            if not fuse_ln_in:
                dma(s, hbb[j % 2].t[:, :, :], hbv[:, :, sl_], (), hbb[j % 2].all())
            dma(s, cosb[j % 2].t[:, :], dr["rope_cos"][:, sl_], (), cosb[j % 2].all())
            dma(s, sinb[j % 2].t[:, :], dr["rope_sin"][:, sl_], (), sinb[j % 2].all())
        load(0)
        if fuse_ln_in:
            load_x(0)
            for f_ in ln_pieces(0):
                f_()
            if NTILE > 1:
                load_x(1)
        for j in range(NTILE):
            hb = hbb[j % 2]
            cos, sin = cosb[j % 2], sinb[j % 2]
            sl = slice(j * T, (j + 1) * T)
            if j + 1 < NTILE:
                load(j + 1)
                if fuse_ln_in:
                    pending.extend(ln_pieces(j + 1))

            def hin(c0, c1):
                return [(win.t[:, k, c0:c1], hb.t[:, k, :], [win.r[0], hb.r[k]]) for k in range(8)]

            for m in range(3):
                p = pp.get()
                proj(p, hin(OFF_CQ + m * 128, OFF_CQ + (m + 1) * 128))
                cp(s, "act", cq.t[:, m, :], p.t[:, :], p.all(), [cq.r[m]])
                q_ = sqb[m]
                act(s, q_.t[:, :], p.t[:, :], AF.Square, p.all(), q_.all())
                mm(s, pssq.t[:, :], ones32.t[:, :], q_.t[:, :], m == 0, m == 2,
                   q_.all() + ones32.all(), pssq.all())
            act(s, rq.t[:, :], pssq.t[:, :], AF.Ln, pssq.all() + epsr.all(), rq.all(), bias=epsr.t[:, 0:1],
                scale=1.0 / QR)
            act(s, rq.t[:, :], rq.t[:, :], AF.Exp, rq.all() + lnsc.all(), rq.all(), bias=lnsc.t[:, 0:1], scale=-0.5)
            tt(s, "dve", cs.t[:, :], cos.t[:, :], rq.t[:, :], ALU.mult, cos.all() + rq.all(), cs.all())
            tt(s, "pool", sn.t[:, :], sin.t[:, :], rq.t[:, :], ALU.mult, sin.all() + rq.all(), sn.all())

            def cqin(w_, c0, c1):
                return [(w_.t[:, k, c0:c1], cq.t[:, k, :], [w_.r[0], cq.r[k]]) for k in range(3)]

            for h in range(NH):
                p = pp.get()
                proj(p, cqin(wuq, h * QK, h * QK + NOPE))
                o = o16.get()
                tt(s, "dve", o.t[:, :], p.t[:, :], rq.t[:, :], ALU.mult, p.all() + rq.all(), o.all())
                dma(s, dr["QN"][h, :, sl], o.t[:, :], o.all(), ())
            for pr in range(2):
                pa, pb = pp.get(), pp.get()
                proj(pa, cqin(wqr, pr * 128, (pr + 1) * 128))
                proj(pb, cqin(wqx, pr * 128, (pr + 1) * 128))
                a_, b_ = t1.get(), t2.get()
                tt(s, "dve", a_.t[:, :], pa.t[:, :], cs.t[:, :], ALU.mult, pa.all() + cs.all(), a_.all())
                tt(s, "dve", b_.t[:, :], pb.t[:, :], sn.t[:, :], ALU.mult, pb.all() + sn.all(), b_.all())
                o = o16.get()
                tt(s, "pool", o.t[:, :], a_.t[:, :], b_.t[:, :], ALU.add, a_.all() + b_.all(), o.all())
                dma(s, dr["QR"][pr * 128:(pr + 1) * 128, sl], o.t[:, :], o.all(), ())
            for m in range(2):
                p = pp.get()
                proj(p, hin(OFF_CKV + m * 128, OFF_CKV + (m + 1) * 128))
                cp(s, "act", ckv.t[:, m, :], p.t[:, :], p.all(), [ckv.r[m]])
                q_ = sqb[m]
                act(s, q_.t[:, :], p.t[:, :], AF.Square, p.all(), q_.all())
                mm(s, pssq.t[:, :], ones32.t[:, :], q_.t[:, :], m == 0, m == 1,
                   q_.all() + ones32.all(), pssq.all())
            for sub in range(4):
                for m in range(2):
                    mm(s, pcol.t[:, 2 * sub:2 * sub + 2], sqb[m].t[:, sub * 128:(sub + 1) * 128],
                       ones32.t[:, 0:2], m == 0, m == 1, sqb[m].all() + ones32.all(), pcol.all())
            act(s, rkv.t[:, :], pssq.t[:, :], AF.Ln, pssq.all() + epsr.all(), rkv.all(), bias=epsr.t[:, 0:1],
                scale=1.0 / KVR)
            act(s, rkv.t[:, :], rkv.t[:, :], AF.Exp, rkv.all(), rkv.all(), scale=-0.5)
            act(s, rcol.t[:, :], pcol.t[:, :], AF.Ln, pcol.all() + epsr.all(), rcol.all(), bias=epsr.t[:, 0:1],
                scale=1.0 / KVR)
            act(s, rcol.t[:, :], rcol.t[:, :], AF.Exp, rcol.all(), rcol.all(), scale=-0.5)

            def kvin(w_, c0, c1):
                return [(w_.t[:, k, c0:c1], ckv.t[:, k, :], [w_.r[0], ckv.r[k]]) for k in range(2)]

            for h in range(NH):
                p = pp.get()
                proj(p, kvin(wukv, h * 256, h * 256 + NOPE))
                o = o16.get()
                tt(s, "dve", o.t[:, :], p.t[:, :], rkv.t[:, :], ALU.mult, p.all() + rkv.all(), o.all())
                dma(s, dr["KN"][h, :, sl], o.t[:, :], o.all(), ())
            for sub in range(4):
                p = pp.get()
                proj(p, [(ckv.t[:, k, sub * 128:(sub + 1) * 128], wvv.t[:, k, :], [wvv.r[0], ckv.r[k]])
                         for k in range(2)])
                o = o16.get()
                act(s, o.t[:, :], p.t[:, :], AF.Copy, p.all() + rcol.all(), o.all(),
                    scale=rcol.t[:, 2 * sub:2 * sub + 1])
                t0 = j * T + sub * 128
                dma(s, dr["V"][t0:t0 + 128, :], o.t[:, :], o.all(), ())
            pa, pb = pp.get(), pp.get()
            proj(pa, hin(OFF_KR, OFF_KR + 64), M=64)
            proj(pb, [(wrotk.t[:, k, :], hb.t[:, k, :], [wrotk.r[0], hb.r[k]]) for k in range(8)], M=64)
            a_, b_ = t1.get(), t2.get()
            tt(s, "dve", a_.t[0:64, :], pa.t[0:64, :], cos.t[0:64, :], ALU.mult, pa.all() + cos.all(), a_.all())
            tt(s, "dve", b_.t[0:64, :], pb.t[0:64, :], sin.t[0:64, :], ALU.mult, pb.all() + sin.all(), b_.all())
            o = o16.get()
            tt(s, "pool", o.t[0:64, :], a_.t[0:64, :], b_.t[0:64, :], ALU.add, a_.all() + b_.all(), o.all())
            dma(s, dr["KR"][:, sl], o.t[0:64, :], o.all(), ())
            for name, off in (("HGQ", OFF_HQ), ("HGF", OFF_HF), ("HGG", OFF_HG), ("SU", OFF_SU)):
                for m in range(2):
                    p = pp.get()
                    proj(p, hin(off + m * 128, off + (m + 1) * 128))
                    o = o32.get()
                    evac_i[0] += 1
                    cp(s, "act" if evac_i[0] % 2 else "dve", o.t[:, :], p.t[:, :], p.all(), o.all())
                    dma(s, dr[name][m * 128:(m + 1) * 128, sl], o.t[:, :], o.all(), ())
            for sub in range(4):
                p = pp.get()
                proj(p, [(hb.t[:, k, sub * 128:(sub + 1) * 128], win.t[:, k, OFF_HF:OFF_HF + 512],
                          [win.r[0], hb.r[k]]) for k in range(8)])
                o = o32.get()
                evac_i[0] += 1
                cp(s, "act" if evac_i[0] % 2 else "dve", o.t[:, :], p.t[:, :], p.all(), o.all())
                t0 = j * T + sub * 128
                dma(s, dr["HGT"][t0:t0 + 128, :], o.t[:, :], o.all(), ())
            while pending:
                tick()
            if fuse_ln_in and j + 2 < NTILE:
                load_x(j + 2)
        s.emit(f"a_{l}")


def stage_b1(nc, sync, dr, S, l, with_s5=True):
    with ExitStack() as es:
        c = Ctx(nc, es)
        s = Sched(sync)
        NTILE = S // T
        NCH = S // 128
        vb_ = [c.sb("vh", (128, NCH, DV), BF16) for _ in range(2)]
        knb = [c.sb("kn", (128, S), BF16) for _ in range(2)]
        kr = c.sb("kr", (128, S), BF16)
        vview = dr["V"].rearrange("(c p) (h d) -> h p c d", p=128, h=NH)

        def load_head(h):
            dma(s, knb[h % 2].t[:, :], dr["KN"][h], (), knb[h % 2].all())
            for c0 in range(0, NCH, 16):
                c1 = min(NCH, c0 + 16)
                dma(s, vb_[h % 2].t[:, c0:c1, :], vview[h][:, c0:c1, :], (), vb_[h % 2].all())
        memset(s, "pool", kr.t[64:128, :], 0.0, kr.all())
        dma(s, kr.t[0:64, :], dr["KR"][:, :], (), kr.all())
        load_head(0)
        onesb = c.sb("onesb", (128, 128), BF16)
        memset(s, "pool", onesb.t[:, :], 1.0, onesb.all())
        onesm = c.sb("onesm", (128, T), BF16)
        memset(s, "pool", onesm.t[:, :], 1.0, onesm.all())
        masks = []
        for d_ in range(4):
            m_ = c.sb("mask", (128, T), BF16)

            def fn(eng, m_=m_, d_=d_):
                return eng.affine_select(m_.t[:, :], onesm.t[:, :], pattern=[[1, T]],
                                         compare_op=ALU.is_ge, fill=0.0, base=-128 * d_,
                                         channel_multiplier=-1)
            s.pool(fn, onesm.all(), m_.all())
            ts(s, "dve", m_.t[:, :], m_.t[:, :], 30000.0, -30000.0, ALU.mult, ALU.add, m_.all(), m_.all())
            masks.append(m_)
        ident = c.sb("ident", (128, 128), BF16)

        def fid(eng):
            return eng.affine_select(ident.t[:, :], onesb.t[:, :], pattern=[[-1, 128]], compare_op=ALU.is_equal,
                                     fill=0.0, base=0, channel_multiplier=1)
        s.pool(fid, onesb.all(), ident.all())
        qnb = [c.sb("qn", (128, T), BF16) for _ in range(2)]
        qrb = [c.sb("qr", (128, T), BF16) for _ in range(2)]
        for q_ in qrb:
            memset(s, "pool", q_.t[64:128, :], 0.0, q_.all())
        pss = Rot([c.ps("pss") for _ in range(4)])
        pso = c.ps("pso")
        psl = c.ps("psl")
        ptb = Rot([c.sb("pt", (128, T), BF16) for _ in range(4)])
        rl = c.sb("rl", (128, T), F32)
        o32a = c.sb("o32a", (128, T), F32)
        ob = Rot([c.sb("ob", (128, T), BF16) for _ in range(2)])
        g3 = gen_b3(s, c, dr, S, l, c.ps("pyr"), None, c.ps("pyg")) if with_s5 else iter(())
        g3_alive = [True]

        def step_s5():
            if g3_alive[0]:
                try:
                    next(g3)
                except StopIteration:
                    g3_alive[0] = False
        blocks = []
        for h in range(NH):
            for i in range(NTILE):
                nb = 4 * i + 4
                for cc in range(nb):
                    blocks.append((h, i, cc, nb))
        state = {}
        tiles = [(h, i) for h in range(NH) for i in range(NTILE)]

        def load_q(tix):
            h, i = tiles[tix]
            sl = slice(i * T, (i + 1) * T)
            dma(s, qnb[tix % 2].t[:, :], dr["QN"][h, :, sl], (), qnb[tix % 2].all())
            dma(s, qrb[tix % 2].t[0:64, :], dr["QR"][h * 64:(h + 1) * 64, sl], (), qrb[tix % 2].all())
        load_q(0)

        def issue_s(n):
            h, i, cc, nb = blocks[n]
            kn = knb[h % 2]
            tix = h * NTILE + i
            qn, qr = qnb[tix % 2], qrb[tix % 2]
            if cc == 0:
                if tix + 1 < len(tiles):
                    load_q(tix + 1)
            p = pss.get()
            ks = slice(cc * 128, (cc + 1) * 128)
            diag = cc >= 4 * i
            qs = slice(128 * (cc - 4 * i), T) if diag else slice(0, T)
            mm(s, p.t[:, qs], kn.t[:, ks], qn.t[:, qs], True, False, kn.all() + qn.all(), p.all())
            mm(s, p.t[:, qs], kr.t[:, ks], qr.t[:, qs], False, not diag, kr.all() + qr.all(), p.all())
            if diag:
                mk = masks[cc - 4 * i]
                mm(s, p.t[:, qs], ident.t[:, :], mk.t[:, qs], False, True, ident.all() + mk.all(), p.all())
            state[n] = (p, qs)

        def issue_pv(n):
            h, i, cc, nb = blocks[n]
            if i == 0 and cc == 0 and h + 1 < NH:
                load_head(h + 1)
            p, qs = state.pop(n)
            pt = ptb.get()
            act(s, pt.t[:, qs], p.t[:, qs], AF.Exp, p.all(), pt.all())
            vh = vb_[h % 2]
            mm(s, pso.t[:, qs], vh.t[:, cc, :], pt.t[:, qs], cc == 0, cc == nb - 1,
               vh.all() + pt.all(), pso.all())
            mm(s, psl.t[:, qs], onesb.t[:, :], pt.t[:, qs], cc == 0, cc == nb - 1,
               onesb.all() + pt.all(), psl.all())
            if cc == nb - 1:
                cp(s, "act", o32a.t[:, :], pso.t[:, :], pso.all(), o32a.all())
                act(s, rl.t[:, :], psl.t[:, :], AF.Ln, psl.all(), rl.all())
                act(s, rl.t[:, :], rl.t[:, :], AF.Exp, rl.all(), rl.all(), scale=-1.0)
                o = ob.get()
                tt(s, "dve", o.t[:, :], o32a.t[:, :], rl.t[:, :], ALU.mult, o32a.all() + rl.all(), o.all())
                dma(s, dr["MIX"][h * DV:(h + 1) * DV, i * T:(i + 1) * T], o.t[:, :], o.all(), ())

        NB = len(blocks)
        LOOK = 3
        n_s5 = (S // TS5) * 13 + 24
        pace = max(1, -(-NB // n_s5)) if B13_PACE is None else B13_PACE
        for n in range(min(LOOK, NB)):
            issue_s(n)
        for n in range(NB):
            issue_pv(n)
            if n + LOOK < NB:
                issue_s(n + LOOK)
            if n % pace == 0:
                step_s5()
        while g3_alive[0]:
            step_s5()
        s.emit(f"b1_{l}")


def build_masks(s, c):
    ones = c.sb("mones", (128, 128), F32)
    memset(s, "pool", ones.t[:, :], 1.0, ones.all())
    U = c.sb("U", (128, 128), F32)
    Lm = c.sb("Lm", (128, 128), F32)
    BD = c.sb("BD", (128, 128), F32)

    def fu(eng):
        return eng.affine_select(U.t[:, :], ones.t[:, :], pattern=[[1, 128]], compare_op=ALU.is_ge,
                                 fill=0.0, base=0, channel_multiplier=-1)
    s.pool(fu, ones.all(), U.all())
    memset(s, "pool", U.t[0:64, 64:128], 0.0, U.all())

    def fl(eng):
        return eng.affine_select(Lm.t[:, :], ones.t[:, :], pattern=[[-1, 128]], compare_op=ALU.is_gt,
                                 fill=0.0, base=0, channel_multiplier=1)
    s.pool(fl, ones.all(), Lm.all())
    memset(s, "pool", Lm.t[64:128, 0:64], 0.0, Lm.all())
    memset(s, "pool", BD.t[:, :], 1.0, BD.all())
    memset(s, "pool", BD.t[0:64, 64:128], 0.0, BD.all())
    memset(s, "pool", BD.t[64:128, 0:64], 0.0, BD.all())
    return U, Lm, BD


def stage_b2(nc, sync, dr, S, l):
    with ExitStack() as es:
        c = Ctx(nc, es)
        s = Sched(sync)
        NTILE = S // T
        U, Lm, BD = build_masks(s, c)
        Ub = c.sb("Ub", (128, 128), BF16)
        cp(s, "pool", Ub.t[:, :], U.t[:, :], U.all(), Ub.all())
        ng = c.sb("ng", (128, 1), F32)
        dma(s, ng.t[:, :], dr["hg_ng"][l], (), ng.all())
        eps = c.sb("eps", (128, 1), F32)
        memset(s, "pool", eps.t[:, :], RMS_EPS, eps.all())
        use_lb = l > 0
        if use_lb:
            assert l == 1 and DEPTH == 2
            lbf = c.sb("lbf", (128, 2, 2), F32)
            dma(s, lbf.t[:, 0, :], dr["hg_lb_fm"][0], (), lbf.all())
            dma(s, lbf.t[:, 1, :], dr["hg_lb_fm"][1], (), lbf.all())
            lb = c.sb("lb", (128, 2), F32)
            oml = c.sb("oml", (128, 2), F32)
            tt(s, "dve", lb.t[:, :], lbf.t[:, 1, :], lbf.t[:, 0, :], ALU.subtract, lbf.all(), lb.all())
            act(s, lb.t[:, :], lb.t[:, :], AF.Sigmoid, lb.all(), lb.all())
            ts(s, "dve", oml.t[:, :], lb.t[:, :], -1.0, 1.0, ALU.mult, ALU.add, lb.all(), oml.all())
            lbr2 = c.sb("lbr2", (128, 2, 256), F32)
            dma(s, lbr2.t[:, 0, :], dr["hg_lower_bound"][0:1, :].partition_broadcast(128), (), lbr2.all())
            dma(s, lbr2.t[:, 1, :], dr["hg_lower_bound"][1:2, :].partition_broadcast(128), (), lbr2.all())
            lbrow = c.sb("lbrow", (128, 4, 256), F32)
            omlrow = c.sb("omlrow", (128, 4, 256), F32)
            tt(s, "dve", lbrow.t[:, 0, :], lbr2.t[:, 1, :], lbr2.t[:, 0, :], ALU.subtract, lbr2.all(), lbrow.all())
            act(s, lbrow.t[:, 0, :], lbrow.t[:, 0, :], AF.Sigmoid, lbrow.all(), lbrow.all())
            for u in range(1, 4):
                cp(s, "dve", lbrow.t[:, u, :], lbrow.t[:, 0, :], lbrow.all(), lbrow.all())
            ts(s, "dve", omlrow.t[:, :, :], lbrow.t[:, :, :], -1.0, 1.0, ALU.mult, ALU.add,
               lbrow.all(), omlrow.all())
        S32 = [c.sb("S32", (128, 128), F32) for _ in range(2)]
        Sb = [c.sb("Sb", (128, 128), BF16) for _ in range(2)]
        for g in range(2):
            memset(s, "pool", S32[g].t[:, :], 0.0, S32[g].all())
            memset(s, "pool", Sb[g].t[:, :], 0.0, Sb[g].all())
        vpad = [c.sb("vpad", (128, 4, 128), BF16) for _ in range(NH)]
        for h in range(NH):
            memset(s, "pool", vpad[h].t[:, :, :], 0.0, vpad[h].all())
        def two(f):
            return [f(), f()]
        vpad2 = two(lambda: [c.sb("vpad", (128, 4, 128), BF16) for _ in range(NH)])
        for sl_ in range(2):
            for h in range(NH):
                memset(s, "pool", vpad2[sl_][h].t[:, :, :], 0.0, vpad2[sl_][h].all())
        qf2 = two(lambda: c.sb("qf", (128, 2, T), F32, 2))
        zf2 = two(lambda: c.sb("zf", (128, 2, T), F32, 2))
        gf2 = two(lambda: c.sb("gf", (128, 2, T), F32, 2))
        tm2 = two(lambda: c.sb("tm", (128, 4, 512), F32))
        ft2 = two(lambda: c.sb("ft", (128, 4, 256), F32))
        lft2 = two(lambda: c.sb("lft", (128, 4, 256), F32))
        ecs2 = two(lambda: c.sb("ecs", (128, 4, 256), F32))
        khat2 = two(lambda: c.sb("khat", (128, 4, 256), BF16))
        vb2 = two(lambda: c.sb("vb", (128, 4, 256), BF16))
        eb2 = two(lambda: [c.sb("eb", (128, T), F32) for _ in range(2)])
        enb2 = two(lambda: [c.sb("enb", (128, T), F32) for _ in range(2)])
        qt2 = two(lambda: [c.sb("qt", (128, T), BF16) for _ in range(2)])
        kt2 = two(lambda: [c.sb("kt", (128, T), BF16) for _ in range(2)])
        attb = Rot([c.sb("attb", (128, 128), BF16) for _ in range(3)])
        o32 = [c.sb("o32", (128, T), F32) for _ in range(2)]
        sq = [c.sb("sq", (128, T), F32) for _ in range(2)]
        rs = [c.sb("rs", (128, T), F32) for _ in range(2)]
        ob = [c.sb("ob", (128, T), BF16) for _ in range(2)]
        pb = [c.ps("pb") for _ in range(2)]
        pc = c.ps("pc", (128, 1024))
        po = [c.ps("po") for _ in range(2)]
        patt = c.ps("patt", (128, 128))
        pu = c.ps("pu", (128, 128))
        tmv = dr["HGT"].rearrange("(j u p) n -> j p u n", p=128, u=4)

        def load(j):
            sl = slice(j * T, (j + 1) * T)
            k_ = j % 2
            dma(s, qf2[k_].t[:, :, :], fm(dr["HGQ"])[:, :, sl], (), qf2[k_].all())
            dma(s, zf2[k_].t[:, :, :], fm(dr["HGF"])[:, :, sl], (), zf2[k_].all())
            dma(s, gf2[k_].t[:, :, :], fm(dr["HGG"])[:, :, sl], (), gf2[k_].all())
            dma(s, tm2[k_].t[:, :, :], tmv[j], (), tm2[k_].all())

        def prep_steps(j):
            k_ = j % 2
            qf, zf, gf, tm, ft, lft, ecs, khat, vb = (qf2[k_], zf2[k_], gf2[k_], tm2[k_], ft2[k_], lft2[k_],
                                                      ecs2[k_], khat2[k_], vb2[k_])
            eb, enb, qt, kt, vpad = eb2[k_], enb2[k_], qt2[k_], kt2[k_], vpad2[k_]
            st = []

            def a1():
                act(s, zf.t[:, :, :], zf.t[:, :, :], AF.Sigmoid, zf.all(), zf.all())
                act(s, ft.t[:, :, :], tm.t[:, :, 0:256], AF.Sigmoid, tm.all(), ft.all())
                act(s, gf.t[:, :, :], gf.t[:, :, :], AF.Silu, gf.all(), gf.all())
            st.append(a1)

            def a2():
                for g in range(2):
                    if use_lb:
                        ts(s, "dve", zf.t[:, g, :], zf.t[:, g, :], oml.t[:, g:g + 1], lb.t[:, g:g + 1],
                           ALU.mult, ALU.add, [zf.r[g]] + oml.all() + lb.all(), [zf.r[g]])
                    ts(s, "dve", zf.t[:, g, :], zf.t[:, g, :], -1.0, 1.0, ALU.mult, ALU.add, [zf.r[g]], [zf.r[g]])
                if use_lb:
                    tt(s, "dve", ft.t[:, :, :], ft.t[:, :, :], omlrow.t[:, :, :], ALU.mult, ft.all() + omlrow.all(), ft.all())
                    tt(s, "dve", ft.t[:, :, :], ft.t[:, :, :], lbrow.t[:, :, :], ALU.add, ft.all() + lbrow.all(), ft.all())
            st.append(a2)

            def a3():
                act(s, lft.t[:, :, :], ft.t[:, :, :], AF.Ln, ft.all(), lft.all())
                cp(s, "pool", vb.t[:, :, :], tm.t[:, :, 256:512], tm.all(), vb.all())
            st.append(a3)

            def a4():
                ts(s, "dve", ft.t[:, :, :], ft.t[:, :, :], -1.0, 1.0, ALU.mult, ALU.add, ft.all(), ft.all())
                for h in range(NH):
                    cp(s, "pool", vpad[h].t[:, :, (h % 2) * 64:(h % 2) * 64 + 64], vb.t[:, :, h * 64:(h + 1) * 64],
                       vb.all(), vpad[h].all())
            st.append(a4)

            def a5():
                for g in range(2):
                    for u in range(4):
                        mm(s, pb[g].t[:, u * 128:(u + 1) * 128], lft.t[:, u, g * 128:(g + 1) * 128], U.t[:, :],
                           u == 0, u == 3, lft.all() + U.all(), pb[g].all())
            st.append(a5)

            def a6():
                for u in range(4):
                    mm(s, pc.t[:, u * 256:(u + 1) * 256], Lm.t[:, :], lft.t[:, u, :], u % 2 == 0, u % 2 == 1,
                       lft.all() + Lm.all(), pc.all())
            st.append(a6)

            def a7():
                for g in range(2):
                    act(s, eb[g].t[:, :], pb[g].t[:, :], AF.Exp, pb[g].all(), eb[g].all())
                    act(s, enb[g].t[:, :], pb[g].t[:, :], AF.Exp, pb[g].all(), enb[g].all(), scale=-1.0)
                act(s, ecs.t[:, :, :], pc.t[:, :], AF.Exp, pc.all(), ecs.all())
            st.append(a7)

            def a8():
                for g in range(2):
                    tt(s, "dve", qt[g].t[:, :], qf.t[:, g, :], eb[g].t[:, :], ALU.mult, [qf.r[g]] + eb[g].all(), qt[g].all())
                    tt(s, "dve", kt[g].t[:, :], zf.t[:, g, :], enb[g].t[:, :], ALU.mult, [zf.r[g]] + enb[g].all(), kt[g].all())
                tt(s, "dve", khat.t[:, :, :], ft.t[:, :, :], ecs.t[:, :, :], ALU.mult, ft.all() + ecs.all(), khat.all())
            st.append(a8)
            return st

        def rec_steps(j):
            k_ = j % 2
            gf, khat, vb = gf2[k_], khat2[k_], vb2[k_]
            eb, qt, kt, vpad = eb2[k_], qt2[k_], kt2[k_], vpad2[k_]
            sl = slice(j * T, (j + 1) * T)
            st = []
            first = [True, True]
            for u in range(4):
                us = slice(u * 128, (u + 1) * 128)

                def r_att(u=u, us=us):
                    for h in range(NH):
                        g, hp = h // 2, (h % 2) * 64
                        mm(s, patt.t[:, :], kt[g].t[hp:hp + 64, us], qt[g].t[hp:hp + 64, us], True, True,
                           kt[g].all() + qt[g].all(), patt.all())
                        ab = attb.get()
                        tt(s, "dve", ab.t[:, :], patt.t[:, :], Ub.t[:, :], ALU.mult, patt.all() + Ub.all(), ab.all())
                        mm(s, po[g].t[:, us], vpad[h].t[:, u, :], ab.t[:, :], first[g], False,
                           vpad[h].all() + ab.all(), po[g].all())
                        first[g] = False
                st.append(r_att)
                for ch in range(2):
                    def r_upd(u=u, ch=ch):
                        cs_ = slice(u * 128 + ch * 64, u * 128 + ch * 64 + 64)
                        rows = slice(ch * 64, ch * 64 + 64)
                        for g in range(2):
                            last = (u == 3 and ch == 1)
                            mm(s, po[g].t[:, cs_], Sb[g].t[:, :], qt[g].t[:, cs_], False, last,
                               Sb[g].all() + qt[g].all(), po[g].all())
                            mm(s, pu.t[:, :], khat.t[rows, u, g * 128:(g + 1) * 128], vb.t[rows, u, g * 128:(g + 1) * 128],
                               True, True, khat.all() + vb.all(), pu.all())
                            col = u * 128 + ch * 64 + 63
                            stt(s, S32[g].t[:, :], S32[g].t[:, :], eb[g].t[:, col:col + 1], pu.t[:, :], ALU.mult, ALU.add,
                                S32[g].all() + eb[g].all() + pu.all(), S32[g].all())
                            tt(s, "dve", Sb[g].t[:, :], S32[g].t[:, :], BD.t[:, :], ALU.mult, S32[g].all() + BD.all(), Sb[g].all())
                    st.append(r_upd)

            def fin():
                for g in range(2):
                    cp(s, "act", o32[g].t[:, :], po[g].t[:, :], po[g].all(), o32[g].all())
                    act(s, sq[g].t[:, :], po[g].t[:, :], AF.Square, po[g].all(), sq[g].all())
                    mm(s, pb[g].t[:, :], BD.t[:, :], sq[g].t[:, :], True, True, BD.all() + sq[g].all(), pb[g].all())
                    act(s, rs[g].t[:, :], pb[g].t[:, :], AF.Ln, pb[g].all() + eps.all(), rs[g].all(), bias=eps.t[:, 0:1], scale=1.0 / HDV)
                    act(s, rs[g].t[:, :], rs[g].t[:, :], AF.Exp, rs[g].all(), rs[g].all(), scale=-0.5)
                    tt(s, "dve", o32[g].t[:, :], o32[g].t[:, :], rs[g].t[:, :], ALU.mult, o32[g].all() + rs[g].all(), o32[g].all())
                    stt(s, ob[g].t[:, :], o32[g].t[:, :], ng.t[:, 0:1], gf.t[:, g, :], ALU.mult, ALU.mult,
                        o32[g].all() + ng.all() + [gf.r[g]], ob[g].all())
                    r0 = NH * DV + g * 128
                    dma(s, dr["MIX"][r0:r0 + 128, sl], ob[g].t[:, :], ob[g].all(), ())
            return st, fin

        load(0)
        for f_ in prep_steps(0):
            f_()
        for j in range(NTILE):
            if j + 1 < NTILE:
                load(j + 1)
                nxt = prep_steps(j + 1)
            else:
                nxt = []
            rsteps, fin = rec_steps(j)
            n_r = len(rsteps)
            early = nxt[:4]
            late = nxt[4:]
            for i_, r_ in enumerate(rsteps):
                r_()
                if i_ < len(early):
                    early[i_]()
            fin()
            for f_ in late:
                f_()
        s.emit(f"b2_{l}")


TWO_PI = 2.0 * math.pi


def emit_sin(s, c, out, ang, shape, tag):
    ki = c.sb(f"ki{tag}", shape, I32)
    kf = c.sb(f"kf{tag}", shape, F32)
    r = c.sb(f"r{tag}", shape, F32)
    m = c.sb(f"m{tag}", shape, F32)
    ts(s, "dve", kf.t[:, :], ang.t[:, :], 1.0 / TWO_PI, None, ALU.mult, None, ang.all(), kf.all())
    cp(s, "dve", ki.t[:, :], kf.t[:, :], kf.all(), ki.all())
    cp(s, "dve", kf.t[:, :], ki.t[:, :], ki.all(), kf.all())
    stt(s, r.t[:, :], kf.t[:, :], -TWO_PI, ang.t[:, :], ALU.mult, ALU.add, kf.all() + ang.all(), r.all())
    ts(s, "dve", m.t[:, :], r.t[:, :], math.pi, -TWO_PI, ALU.is_gt, ALU.mult, r.all(), m.all())
    tt(s, "dve", r.t[:, :], r.t[:, :], m.t[:, :], ALU.add, r.all() + m.all(), r.all())
    ts(s, "dve", m.t[:, :], r.t[:, :], -math.pi, TWO_PI, ALU.is_lt, ALU.mult, r.all(), m.all())
    tt(s, "dve", r.t[:, :], r.t[:, :], m.t[:, :], ALU.add, r.all() + m.all(), r.all())
    ts(s, "dve", r.t[:, :], r.t[:, :], math.pi, -math.pi, ALU.min, ALU.max, r.all(), r.all())
    act(s, out.t[:, :], r.t[:, :], AF.Sin, r.all(), out.all())


TS5 = 256
S5_MAXACT = 16
B13_PACE = None
FUSE_LN_IN = False


def gen_b3(s, c, dr, S, l, pyr, pyi, pyg):
    TT = TS5
    NTILE = S // TT
    NK = 8
    sh = (128, NK)
    ar = c.sb("ar", sh, F32)
    ai = c.sb("ai", sh, F32)
    dt = c.sb("dt", sh, F32)
    dma(s, ar.t[:, :], dr["s5_are_pp"][l], (), ar.all())
    dma(s, ai.t[:, :], dr["s5_aim_pp"][l], (), ai.all())
    dma(s, dt.t[:, :], dr["s5_ldt_pp"][l], (), dt.all())
    act(s, dt.t[:, :], dt.t[:, :], AF.Exp, dt.all(), dt.all())
    mag = c.sb("mag", sh, F32)
    th = c.sb("th", sh, F32)
    th2 = c.sb("th2", sh, F32)
    tt(s, "dve", mag.t[:, :], dt.t[:, :], ar.t[:, :], ALU.mult, dt.all() + ar.all(), mag.all())
    act(s, mag.t[:, :], mag.t[:, :], AF.Exp, mag.all(), mag.all())
    tt(s, "dve", th.t[:, :], dt.t[:, :], ai.t[:, :], ALU.mult, dt.all() + ai.all(), th.all())
    ts(s, "dve", th2.t[:, :], th.t[:, :], math.pi / 2, None, ALU.add, None, th.all(), th2.all())
    sn1 = c.sb("sn1", sh, F32)
    cs1 = c.sb("cs1", sh, F32)
    emit_sin(s, c, sn1, th, sh, "a")
    emit_sin(s, c, cs1, th2, sh, "b")
    yield
    nre = c.sb("nre", sh, F32)
    nim = c.sb("nim", sh, F32)
    tt(s, "dve", nre.t[:, :], mag.t[:, :], cs1.t[:, :], ALU.mult, mag.all() + cs1.all(), nre.all())
    ts(s, "dve", nre.t[:, :], nre.t[:, :], -1.0, None, ALU.add, None, nre.all(), nre.all())
    tt(s, "dve", nim.t[:, :], mag.t[:, :], sn1.t[:, :], ALU.mult, mag.all() + sn1.all(), nim.all())
    den = c.sb("den", sh, F32)
    t_a = c.sb("t_a", sh, F32)
    t_b = c.sb("t_b", sh, F32)
    tt(s, "dve", den.t[:, :], ar.t[:, :], ar.t[:, :], ALU.mult, ar.all(), den.all())
    tt(s, "dve", t_a.t[:, :], ai.t[:, :], ai.t[:, :], ALU.mult, ai.all(), t_a.all())
    tt(s, "dve", den.t[:, :], den.t[:, :], t_a.t[:, :], ALU.add, den.all() + t_a.all(), den.all())

    def frecip(eng):
        return eng.reciprocal(den.t[:, :], den.t[:, :])
    s.dve(frecip, den.all(), den.all())
    cre = c.sb("cre", sh, F32)
    cim = c.sb("cim", sh, F32)
    tt(s, "dve", t_a.t[:, :], nre.t[:, :], ar.t[:, :], ALU.mult, nre.all() + ar.all(), t_a.all())
    tt(s, "dve", t_b.t[:, :], nim.t[:, :], ai.t[:, :], ALU.mult, nim.all() + ai.all(), t_b.all())
    tt(s, "dve", cre.t[:, :], t_a.t[:, :], t_b.t[:, :], ALU.add, t_a.all() + t_b.all(), cre.all())
    tt(s, "dve", cre.t[:, :], cre.t[:, :], den.t[:, :], ALU.mult, cre.all() + den.all(), cre.all())
    tt(s, "dve", t_a.t[:, :], nim.t[:, :], ar.t[:, :], ALU.mult, nim.all() + ar.all(), t_a.all())
    tt(s, "dve", t_b.t[:, :], nre.t[:, :], ai.t[:, :], ALU.mult, nre.all() + ai.all(), t_b.all())
    tt(s, "dve", cim.t[:, :], t_a.t[:, :], t_b.t[:, :], ALU.subtract, t_a.all() + t_b.all(), cim.all())
    tt(s, "dve", cim.t[:, :], cim.t[:, :], den.t[:, :], ALU.mult, cim.all() + den.all(), cim.all())
    yield
    Ct = c.sb("Ct", (128, NK, TT), F32, NK)
    St = c.sb("St", (128, NK, TT), F32, NK)
    memset(s, "pool", Ct.t[:, :, 0:1], 1.0, Ct.all())
    memset(s, "pool", St.t[:, :, 0:1], 0.0, St.all())
    kre = [cs1]
    kim = [sn1]
    nstep = int(math.log2(TT))
    for k in range(1, nstep + 1):
        a_, b_ = c.sb(f"kre{k}", sh, F32), c.sb(f"kim{k}", sh, F32)
        p_, q_ = kre[-1], kim[-1]
        tt(s, "dve", t_a.t[:, :], p_.t[:, :], p_.t[:, :], ALU.mult, p_.all(), t_a.all())
        tt(s, "dve", t_b.t[:, :], q_.t[:, :], q_.t[:, :], ALU.mult, q_.all(), t_b.all())
        tt(s, "dve", a_.t[:, :], t_a.t[:, :], t_b.t[:, :], ALU.subtract, t_a.all() + t_b.all(), a_.all())
        stt(s, b_.t[:, :], p_.t[:, :], 2.0, q_.t[:, :], ALU.mult, ALU.mult, p_.all() + q_.all(), b_.all())
        kre.append(a_)
        kim.append(b_)
    yield
    tmpd = [c.sb("tmpd", (128, TT // 2), F32) for _ in range(2)]
    for k in range(nstep):
        n = 1 << k
        for kc in range(NK):
            cr, ci = kre[k].t[:, kc:kc + 1], kim[k].t[:, kc:kc + 1]
            rd = [Ct.r[kc], St.r[kc]] + kre[k].all() + kim[k].all()
            ta, tb = tmpd[0], tmpd[1]
            ts(s, "dve", ta.t[:, 0:n], St.t[:, kc, 0:n], ci, None, ALU.mult, None, rd, ta.all())
            stt(s, Ct.t[:, kc, n:2 * n], Ct.t[:, kc, 0:n], cr, ta.t[:, 0:n], ALU.mult, ALU.subtract,
                rd + ta.all(), [Ct.r[kc]])
            ts(s, "dve", tb.t[:, 0:n], Ct.t[:, kc, 0:n], ci, None, ALU.mult, None, rd, tb.all())
            stt(s, St.t[:, kc, n:2 * n], St.t[:, kc, 0:n], cr, tb.t[:, 0:n], ALU.mult, ALU.add,
                rd + tb.all(), [St.r[kc]])
        yield
    ETr, ETi = kre[nstep], kim[nstep]
    Pr = c.sb("Pr", (128, NK, TT), F32, NK)
    Pi = c.sb("Pi", (128, NK, TT), F32, NK)
    magt = c.sb("magt", (128, NK, TT), F32, NK)
    tfull = c.sb("tfull", (128, TT), F32)
    for kc in range(NK):
        cr, ci = cre.t[:, kc:kc + 1], cim.t[:, kc:kc + 1]
        rd = [Ct.r[kc], St.r[kc]] + cre.all() + cim.all()
        ts(s, "dve", tfull.t[:, :], St.t[:, kc, :], ci, None, ALU.mult, None, rd, tfull.all())
        stt(s, Pr.t[:, kc, :], Ct.t[:, kc, :], cr, tfull.t[:, :], ALU.mult, ALU.add, rd + tfull.all(), [Pr.r[kc]])
        ts(s, "dve", tfull.t[:, :], St.t[:, kc, :], cr, None, ALU.mult, None, rd, tfull.all())
        stt(s, Pi.t[:, kc, :], Ct.t[:, kc, :], ci, tfull.t[:, :], ALU.mult, ALU.subtract, rd + tfull.all(), [Pi.r[kc]])
        memset(s, "pool", magt.t[:, kc, :], 1.0, [magt.r[kc]])
        ts(s, "pool", magt.t[:, kc, :], magt.t[:, kc, :], mag.t[:, kc:kc + 1], None, ALU.mult, None,
           [magt.r[kc]] + mag.all(), [magt.r[kc]])
        if kc % 2 == 1:
            yield
    bre = c.sb("bre", (128, NK, 128), BF16)
    bim = c.sb("bim", (128, NK, 128), BF16)
    ctr = c.sb("ctr", (128, NK, 128), BF16)
    cti = c.sb("cti", (128, NK, 128), BF16)
    st = c.sb("stw", (128, NK, 128), F32)
    dma(s, bre.t[:, :, :], dr["s5_bT_re"][l], (), bre.all(), q="pool")
    dma(s, bim.t[:, :, :], dr["s5_bT_im"][l], (), bim.all(), q="pool")
    dma(s, ctr.t[:, :, :], dr["s5_cT_re"][l], (), ctr.all(), q="pool")
    dma(s, st.t[:, :, :], dr["s5_cT_im"][l], (), st.all())
    ts(s, "dve", cti.t[:, :, :], st.t[:, :, :], -1.0, None, ALU.mult, None, st.all(), cti.all())
    wg = c.sb("wg", (128, 2, 256), BF16)
    load_w_bf16(s, wg, wv(dr["s5_w_glu"][l]), 2, 256)
    dsk = c.sb("dsk", (128, 2), F32)
    bg = c.sb("bg", (128, 2), F32)
    dma(s, dsk.t[:, :], dr["s5_d_fm"][l], (), dsk.all())
    dma(s, bg.t[:, :], dr["s5_bglu_fm"][l], (), bg.all())
    gl_re = c.sb("gl_re", sh, F32)
    gl_im = c.sb("gl_im", sh, F32)
    memset(s, "pool", gl_re.t[:, :], 0.0, gl_re.all())
    memset(s, "pool", gl_im.t[:, :], 0.0, gl_im.all())
    ini_re = c.sb("ini_re", sh, F32)
    ini_im = c.sb("ini_im", sh, F32)
    u32 = [c.sb("u32", (128, 2, TT), F32, 2) for _ in range(3)]
    ubb = [c.sb("ub", (128, 2, TT), BF16, 2) for _ in range(3)]
    DEPTH_R = 2
    mA = Rot([c.sb("mA", (128, TT), F32) for _ in range(DEPTH_R)])
    mB = Rot([c.sb("mB", (128, TT), F32) for _ in range(DEPTH_R)])
    mC = Rot([c.sb("mC", (128, TT), F32) for _ in range(DEPTH_R)])
    mD = Rot([c.sb("mD", (128, TT), F32) for _ in range(DEPTH_R)])
    xre_r = Rot([c.sb("xre", (128, TT), F32) for _ in range(DEPTH_R)])
    xim_r = Rot([c.sb("xim", (128, TT), F32) for _ in range(DEPTH_R)])
    gre_r = Rot([c.sb("gre", (128, TT), F32) for _ in range(DEPTH_R)])
    gim_r = Rot([c.sb("gim", (128, TT), F32) for _ in range(DEPTH_R)])
    hreb = [c.sb("hre", (128, NK, TT), BF16, NK) for _ in range(2)]
    himb = [c.sb("him", (128, NK, TT), BF16, NK) for _ in range(2)]
    yv = [c.sb("yv", (128, TT), F32) for _ in range(2)]
    yt = [c.sb("yt", (128, TT), F32) for _ in range(2)]
    ygb = c.sb("ygb", (128, 2, TT), BF16, 2)
    sg = [c.sb("sg", (128, TT), F32) for _ in range(2)]
    ob = [c.sb("ob", (128, TT), BF16) for _ in range(2)]
    yield

    def nop():
        pass

    def pre_item(j):
        def f1():
            dma(s, u32[j % 3].t[:, :, :], fm(dr["SU"])[:, :, j * TT:(j + 1) * TT], (), u32[j % 3].all())

        def f2():
            cp(s, "act", ubb[j % 3].t[:, :, :], u32[j % 3].t[:, :, :], u32[j % 3].all(), ubb[j % 3].all())
        return [f1, nop, f2]

    item_no = [0]

    def kc_item(j, kc):
        n_ = item_no[0]
        item_no[0] += 1
        half = 0
        hs = slice(0, TT)
        ub = ubb[j % 3]
        hre, him = hreb[j % 2], himb[j % 2]
        ic = kc // 4
        st_ = {}

        def s1():
            mm(s, pyr.t[:, 0:TT], bre.t[:, kc, :], ub.t[:, ic, :], True, True, bre.all() + [ub.r[ic]], pyr.all())
            mm(s, pyr.t[:, TT:2 * TT], bim.t[:, kc, :], ub.t[:, ic, :], True, True, bim.all() + [ub.r[ic]], pyr.all())

        def s2():
            tA, tB, tC, tD = mA.get(), mB.get(), mC.get(), mD.get()
            st_["m"] = (tA, tB, tC, tD)
            y_re, y_im = pyr.t[:, 0:TT], pyr.t[:, TT:2 * TT]
            tt(s, "dve", tA.t[:, :], y_re, Pr.t[:, kc, :], ALU.mult, pyr.all() + [Pr.r[kc]], tA.all())
            tt(s, "dve", tB.t[:, :], y_im, Pi.t[:, kc, :], ALU.mult, pyr.all() + [Pi.r[kc]], tB.all())
            tt(s, "dve", tC.t[:, :], y_im, Pr.t[:, kc, :], ALU.mult, pyr.all() + [Pr.r[kc]], tC.all())
            tt(s, "dve", tD.t[:, :], y_re, Pi.t[:, kc, :], ALU.mult, pyr.all() + [Pi.r[kc]], tD.all())

        def s3():
            pass

        def s4():
            tA, tB, tC, tD = st_["m"]
            xre, xim = xre_r.get(), xim_r.get()
            st_["x"] = (xre, xim)
            tt(s, "pool", xre.t[:, :], tA.t[:, :], tB.t[:, :], ALU.subtract, tA.all() + tB.all(), xre.all())
            tt(s, "pool", xim.t[:, :], tC.t[:, :], tD.t[:, :], ALU.add, tC.all() + tD.all(), xim.all())

        def s5():
            xre, xim = st_["x"]
            gre, gim = gre_r.get(), gim_r.get()
            st_["g"] = (gre, gim)
            for (xx, gg, ini) in ((xre, gre, ini_re), (xim, gim, ini_im)):
                def fscan(eng, xx=xx, gg=gg, ini=ini):
                    return eng.tensor_tensor_scan(gg.t[:, :], magt.t[:, kc, :], xx.t[:, :],
                                                  ini.t[:, kc:kc + 1], ALU.mult, ALU.add)
                s.dve(fscan, [magt.r[kc]] + xx.all() + ini.all(), gg.all())

        def s6():
            gre, gim = st_["g"]
            cp(s, "pool", gl_re.t[:, kc:kc + 1], gre.t[:, TT - 1:TT], gre.all(), gl_re.all())
            cp(s, "pool", gl_im.t[:, kc:kc + 1], gim.t[:, TT - 1:TT], gim.all(), gl_im.all())
            tA, tB, tC, tD = mA.get(), mB.get(), mC.get(), mD.get()
            st_["m2"] = (tA, tB, tC, tD)
            tt(s, "dve", tA.t[:, :], gre.t[:, :], Ct.t[:, kc, :], ALU.mult, gre.all() + [Ct.r[kc]], tA.all())
            tt(s, "pool", tB.t[:, :], gim.t[:, :], St.t[:, kc, :], ALU.mult, gim.all() + [St.r[kc]], tB.all())
            tt(s, "dve", tC.t[:, :], gre.t[:, :], St.t[:, kc, :], ALU.mult, gre.all() + [St.r[kc]], tC.all())
            tt(s, "pool", tD.t[:, :], gim.t[:, :], Ct.t[:, kc, :], ALU.mult, gim.all() + [Ct.r[kc]], tD.all())

        def s7():
            tA, tB, tC, tD = st_["m2"]
            tt(s, "pool", hre.t[:, kc, :], tA.t[:, :], tB.t[:, :], ALU.subtract, tA.all() + tB.all(), [hre.r[kc]])
            tt(s, "pool", him.t[:, kc, :], tC.t[:, :], tD.t[:, :], ALU.add, tC.all() + tD.all(), [him.r[kc]])
        return [s1, s2, s4, s5, s6, s7]

    def fin_item(j):
        sl = slice(j * TT, (j + 1) * TT)
        u = u32[j % 3]
        hre, him = hreb[j % 2], himb[j % 2]
        hv_ = [slice(0, TT), slice(TT, 2 * TT)]

        def f1():
            for oc in range(2):
                for i_, kc in enumerate(range(4 * oc, 4 * oc + 4)):
                    mm(s, pyg.t[:, hv_[oc]], ctr.t[:, kc, :], hre.t[:, kc, :], i_ == 0, False,
                       ctr.all() + [hre.r[kc]], pyg.all())
                    mm(s, pyg.t[:, hv_[oc]], cti.t[:, kc, :], him.t[:, kc, :], False, i_ == 3,
                       cti.all() + [him.r[kc]], pyg.all())

        def f2():
            for oc in range(2):
                stt(s, yv[oc].t[:, :], u.t[:, oc, :], dsk.t[:, oc:oc + 1], pyg.t[:, hv_[oc]], ALU.mult, ALU.add,
                    [u.r[oc]] + dsk.all() + pyg.all(), yv[oc].all())

        def f3():
            for oc in range(2):
                tt(s, "pool", yt[oc].t[:, :], yv[oc].t[:, :], yv[oc].t[:, :], ALU.mult, yv[oc].all(), yt[oc].all())

        def f4():
            for oc in range(2):
                t_, y_ = yt[oc], yv[oc]
                ts(s, "dve", t_.t[:, :], t_.t[:, :], 0.044715, 1.0, ALU.mult, ALU.add, t_.all(), t_.all())
                tt(s, "dve", t_.t[:, :], t_.t[:, :], y_.t[:, :], ALU.mult, t_.all() + y_.all(), t_.all())

        def f5():
            for oc in range(2):
                t_ = yt[oc]
                act(s, t_.t[:, :], t_.t[:, :], AF.Sigmoid, t_.all(), t_.all(), scale=2.0 * math.sqrt(2.0 / math.pi))

        def f6():
            for oc in range(2):
                t_, y_ = yt[oc], yv[oc]
                tt(s, "dve", y_.t[:, :], y_.t[:, :], t_.t[:, :], ALU.mult, y_.all() + t_.all(), y_.all())
                cp(s, "pool", ygb.t[:, oc, :], y_.t[:, :], y_.all(), [ygb.r[oc]])

        def g1():
            for oc in range(2):
                for ic in range(2):
                    mm(s, pyg.t[:, hv_[oc]], wg.t[:, ic, oc * 128:(oc + 1) * 128], ygb.t[:, ic, :], ic == 0, ic == 1,
                       wg.all() + [ygb.r[ic]], pyg.all())

        def g2():
            for oc in range(2):
                act(s, sg[oc].t[:, :], pyg.t[:, hv_[oc]], AF.Sigmoid, pyg.all() + bg.all(), sg[oc].all(),
                    bias=bg.t[:, oc:oc + 1])

        def g3_():
            for oc in range(2):
                tt(s, "dve", ob[oc].t[:, :], yv[oc].t[:, :], sg[oc].t[:, :], ALU.mult,
                   yv[oc].all() + sg[oc].all(), ob[oc].all())
                r0 = NH * DV + 256 + oc * 128
                dma(s, dr["MIX"][r0:r0 + 128, sl], ob[oc].t[:, :], ob[oc].all(), ())
        return [nop] * 6 + [f1, f2, f3, f4, f5, f6, g1, g2, g3_]

    def ini_ops():
        tt(s, "dve", t_a.t[:, :], gl_re.t[:, :], ETr.t[:, :], ALU.mult, gl_re.all() + ETr.all(), t_a.all())
        tt(s, "dve", t_b.t[:, :], gl_im.t[:, :], ETi.t[:, :], ALU.mult, gl_im.all() + ETi.all(), t_b.all())
        tt(s, "dve", ini_re.t[:, :], t_a.t[:, :], t_b.t[:, :], ALU.subtract, t_a.all() + t_b.all(), ini_re.all())
        tt(s, "dve", t_a.t[:, :], gl_re.t[:, :], ETi.t[:, :], ALU.mult, gl_re.all() + ETi.all(), t_a.all())
        tt(s, "dve", t_b.t[:, :], gl_im.t[:, :], ETr.t[:, :], ALU.mult, gl_im.all() + ETr.all(), t_b.all())
        tt(s, "dve", ini_im.t[:, :], t_a.t[:, :], t_b.t[:, :], ALU.add, t_a.all() + t_b.all(), ini_im.all())

    memset(s, "pool", ini_re.t[:, :], 0.0, ini_re.all())
    memset(s, "pool", ini_im.t[:, :], 0.0, ini_im.all())
    items = [pre_item(0)]
    if NTILE > 1:
        items.append(pre_item(1))
    for j in range(NTILE):
        for kc in range(NK):
            items.append(kc_item(j, kc))
            if kc == 5 and j + 2 < NTILE:
                items.append(pre_item(j + 2))
        items.append(fin_item(j))
        if j + 1 < NTILE:
            items.append([nop] * 5 + [ini_ops])
            items.extend([[nop]] * 3)
    active = []
    it = iter(items)
    while True:
        nxt = next(it, None) if len(active) < S5_MAXACT else None
        if nxt is not None:
            active.append([nxt, 0])
        if not active:
            break
        for a_ in list(active):
            a_[0][a_[1]]()
            a_[1] += 1
            if a_[1] == len(a_[0]):
                active.remove(a_)
        yield


def stage_b3(nc, sync, dr, S, l):
    with ExitStack() as es:
        c = Ctx(nc, es)
        s = Sched(sync)
        for _ in gen_b3(s, c, dr, S, l, c.ps("pyr"), None, c.ps("pyg")):
            pass
        s.emit(f"b3_{l}")


def build(S=SEQ, nlayers=DEPTH, stages=None, debug_out=(), ext_in=()):
    nc = bass.Bass("TRN2", target_bir_lowering=False)
    dr = {}

    def din(name, shape, dtype=F32):
        dr[name] = nc.dram_tensor(name, list(shape), dtype, kind="ExternalInput").ap()

    def dscr(name, shape, dtype):
        kind = "Internal"
        if name in debug_out:
            kind = "ExternalOutput"
        if name in ext_in:
            kind = "ExternalInput"
        dr[name] = nc.dram_tensor(name, list(shape), dtype, kind=kind).ap()

    if stages is None:
        stages = ("ln_in", "a", "b13", "b2", "c1", "c2")

    def want(st):
        return st in stages

    din("xT", (D, S))
    din("ln_in_g", (128, 8))
    din("ln_in_b", (128, 8))
    for k_, shp in LAYER_W.items():
        din(k_, (DEPTH,) + shp)
    for k_, n in LAYER_V128.items():
        din(k_, (DEPTH, 128, n))
    dscr("H", (D, S), F32)
    dscr("Hb", (D, S), BF16)
    dscr("H1", (D, S), F32)
    dscr("H1b", (D, S), BF16)
    dscr("MIX", (D, S), BF16)
    din("rope_cos", (128, S))
    din("rope_sin", (128, S))
    dscr("QN", (NH, NOPE, S), BF16)
    dscr("QR", (NH * ROPE, S), BF16)
    dscr("KN", (NH, NOPE, S), BF16)
    dscr("KR", (ROPE, S), BF16)
    dscr("V", (S, NH * DV), BF16)
    dscr("HGQ", (256, S), F32)
    dscr("HGF", (256, S), F32)
    dscr("HGG", (256, S), F32)
    dscr("HGT", (S, 512), F32)
    dscr("SU", (256, S), F32)
    din("hg_ng", (DEPTH, 128, 1))
    for nm_ in ("s5_are_pp", "s5_aim_pp", "s5_ldt_pp"):
        din(nm_, (DEPTH, 128, 8))
    for nm_ in ("s5_bT_re", "s5_bT_im", "s5_cT_re", "s5_cT_im"):
        din(nm_, (DEPTH, 128, 8, 128))
    din("s5_d_fm", (DEPTH, 128, 2))
    din("s5_bglu_fm", (DEPTH, 128, 2))
    din("hg_lb_fm", (DEPTH, 128, 2))
    din("hg_lower_bound", (DEPTH, 256))
    dr["outT"] = nc.dram_tensor("outT", [D, S], F32, kind="ExternalOutput").ap()
    with ExitStack() as es:
        sync = Sync(nc, es)
        fuse0 = FUSE_LN_IN and want("ln_in") and want("a")
        if want("ln_in") and not fuse0:
            stage_ln_in(nc, sync, dr, S)
        for l in range(nlayers):
            if want("a"):
                stage_a(nc, sync, dr, S, l, fuse_ln_in=(fuse0 and l == 0))
            if want("b1"):
                stage_b1(nc, sync, dr, S, l, with_s5=False)
            if want("b2"):
                stage_b2(nc, sync, dr, S, l)
            if want("b3"):
                stage_b3(nc, sync, dr, S, l)
            if want("b13"):
                stage_b1(nc, sync, dr, S, l, with_s5=True)
            last = l == nlayers - 1
            if want("c1") and want("c2") and S // T >= 2:
                with ExitStack() as es2:
                    w1b = Ctx(nc, es2).sb("w1p", (128, 8, DFF), BF16)
                    stage_c1(nc, sync, dr, S, l, w1_pref=w1b)
                    stage_c2(nc, sync, dr, S, l, dr["outT"] if last else dr["H"],
                             None if last else dr["Hb"], w1_pref=w1b)
            else:
                if want("c1"):
                    stage_c1(nc, sync, dr, S, l)
                if want("c2"):
                    stage_c2(nc, sync, dr, S, l, dr["outT"] if last else dr["H"],
                             None if last else dr["Hb"])
    return nc


def _fmv(v):
    v = np.asarray(v, np.float32)
    return np.ascontiguousarray(v.reshape(v.shape[0], -1, 128).transpose(0, 2, 1))


def _rope_tables(S):
    freqs = (ROPE_THETA ** (-np.arange(0, ROPE, 2, dtype=np.float32) / ROPE)).astype(np.float32)
    ang = np.arange(S, dtype=np.float32)[:, None] * freqs[None, :]
    cos = np.cos(ang).astype(np.float32).T
    sin = np.sin(ang).astype(np.float32).T
    return (np.ascontiguousarray(np.concatenate([cos] * 4, 0)),
            np.ascontiguousarray(np.concatenate([sin] * 4, 0)))


def _s5_layouts(inp):
    L = inp["s5_a_re"].shape[0]
    out = {}

    def pp(a):
        a = np.asarray(a, np.float32)
        return np.ascontiguousarray(a.reshape(L, 8, 2, 64).transpose(0, 2, 3, 1).reshape(L, 128, 8))
    out["s5_are_pp"] = pp(inp["s5_a_re"])
    out["s5_aim_pp"] = pp(inp["s5_a_im"])
    out["s5_ldt_pp"] = pp(np.repeat(np.asarray(inp["s5_log_dt"], np.float32)[:, :, None], 64, axis=2))

    def bT(b):
        b = np.asarray(b, np.float32)
        o = np.zeros((L, 128, 8, 128), np.float32)
        for g in range(16):
            o[:, (g % 8) * 16:(g % 8) * 16 + 16, g // 2, (g % 2) * 64:(g % 2) * 64 + 64] = b[:, g].transpose(0, 2, 1)
        return o

    def cT(cc):
        cc = np.asarray(cc, np.float32)
        o = np.zeros((L, 128, 8, 128), np.float32)
        for g in range(16):
            o[:, (g % 2) * 64:(g % 2) * 64 + 64, g // 2, (g % 8) * 16:(g % 8) * 16 + 16] = cc[:, g].transpose(0, 2, 1)
        return o
    out["s5_bT_re"] = bT(inp["s5_b_re"])
    out["s5_bT_im"] = bT(inp["s5_b_im"])
    out["s5_cT_re"] = cT(inp["s5_c_re"])
    out["s5_cT_im"] = cT(inp["s5_c_im"])
    out["s5_d_fm"] = _fmv(inp["s5_d"])
    out["s5_bglu_fm"] = _fmv(inp["s5_b_glu"])
    return out


def make_shared_inputs(inp, S):
    cos, sin = _rope_tables(S)
    im = {"ln_in_g": _fmv(np.asarray(inp["ln_in_g"])[None])[0],
          "ln_in_b": _fmv(np.asarray(inp["ln_in_b"])[None])[0],
          "rope_cos": cos, "rope_sin": sin}
    for k in LAYER_W:
        im[k] = np.ascontiguousarray(np.asarray(inp[k], np.float32))
    for k in LAYER_V128:
        im[k] = _fmv(inp[k])
    im["hg_ng"] = np.ascontiguousarray(np.tile(np.asarray(inp["hg_norm_g"], np.float32), (1, 2))[:, :, None])
    im["hg_lb_fm"] = _fmv(inp["hg_lower_bound"])
    im["hg_lower_bound"] = np.ascontiguousarray(np.asarray(inp["hg_lower_bound"], np.float32))
    im.update(_s5_layouts(inp))
    return im


N_CORES = 8


def kernel(**inputs):
    x = np.asarray(inputs["x"], np.float32)
    B, S, _ = x.shape
    shared = make_shared_inputs(inputs, S)
    nc = build(S=S, nlayers=DEPTH)
    in_maps = []
    for core in range(N_CORES):
        m = dict(shared)
        m["xT"] = np.ascontiguousarray(x[core % B].T)
        in_maps.append(m)
    res = run_bass_kernel_spmd(nc, in_maps, core_ids=list(range(N_CORES)))
    out = np.stack([np.asarray(res.results[b]["outT"]).T for b in range(B)], 0)
    return np.ascontiguousarray(out.astype(np.float32))
```
